# Optimizing a Trainium2 kernel written in Bass

```python
import jax, jax.numpy as jnp
from jax import lax
import numpy as np

D_MODEL = 2048
BATCH = 4
SEQ = 4096
DEPTH = 2

CHUNK = 64
Q_BLOCK = 128
N_MEM = 256
D_MIX = D_MODEL
FOX_HEADS = 8
FOX_HEAD_DIM = 128
FOX_WIDTH = FOX_HEADS * FOX_HEAD_DIM
HGRN_HEADS = 8
HGRN_KEY_DIM = 128
HGRN_VAL_DIM = (D_MIX - FOX_WIDTH) // HGRN_HEADS
HGRN_KEY_WIDTH = HGRN_HEADS * HGRN_KEY_DIM
HGRN_VAL_WIDTH = HGRN_HEADS * HGRN_VAL_DIM
IN_SPLIT_SIZES = (FOX_WIDTH, FOX_WIDTH, FOX_WIDTH, FOX_HEADS,
                  HGRN_KEY_WIDTH, HGRN_KEY_WIDTH, HGRN_VAL_WIDTH, HGRN_VAL_WIDTH)
N_IN = sum(IN_SPLIT_SIZES)
IN_SPLIT_POINTS = tuple(int(v) for v in np.cumsum(IN_SPLIT_SIZES)[:-1])
XATTN_HEADS = 4
XATTN_HEAD_DIM = D_MODEL // XATTN_HEADS
D_FF = ((8 * D_MODEL // 3 + 255) // 256) * 256
CONV_WIDTH = 3
LN_EPS = 1e-5
RMS_EPS = 1e-6
MASK_VALUE = -1e30
MIN_FORGET = 1e-6
DEEPNORM_ALPHA = (2 * DEPTH) ** 0.25
DEEPNORM_BETA = (8 * DEPTH) ** -0.25

kernel_name = 'hymba_fox_hgrn2_deepnorm_convffn_encoder'


def layer_norm(x, g, b):
    xf = x.astype(jnp.float32)
    mu = jnp.mean(xf, axis=-1, keepdims=True)
    var = jnp.mean(jnp.square(xf - mu), axis=-1, keepdims=True)
    return ((xf - mu) * lax.rsqrt(var + LN_EPS) * g.astype(jnp.float32)
            + b.astype(jnp.float32)).astype(x.dtype)


def rms_norm(x, g):
    xf = x.astype(jnp.float32)
    ms = jnp.mean(jnp.square(xf), axis=-1, keepdims=True)
    return (xf * lax.rsqrt(ms + RMS_EPS) * g.astype(jnp.float32)).astype(x.dtype)


def fox_attention(q, k, v, log_f):
    B, S, H, Dh = q.shape
    nb = S // Q_BLOCK
    c = jnp.cumsum(log_f, axis=1).transpose(0, 2, 1)
    qt = q.transpose(0, 2, 1, 3) * (Dh ** -0.5)
    kt = k.transpose(0, 2, 1, 3)
    vt = v.transpose(0, 2, 1, 3)
    q_blocks = qt.reshape(B, H, nb, Q_BLOCK, Dh).transpose(2, 0, 1, 3, 4)
    c_blocks = c.reshape(B, H, nb, Q_BLOCK).transpose(2, 0, 1, 3)
    starts = jnp.arange(nb, dtype=jnp.int32) * Q_BLOCK
    key_pos = jnp.arange(S, dtype=jnp.int32)

    def block(args):
        qb, cb, start = args
        s = jnp.einsum('bhqd,bhkd->bhqk', qb, kt).astype(jnp.float32)
        q_pos = start + jnp.arange(Q_BLOCK, dtype=jnp.int32)
        mask = key_pos[None, :] <= q_pos[:, None]
        bias = jnp.where(mask, cb[..., None] - c[:, :, None, :], 0.0)
        p = jax.nn.softmax(jnp.where(mask, s + bias, MASK_VALUE), axis=-1)
        return jnp.einsum('bhqk,bhkd->bhqd', p.astype(vt.dtype), vt)

    out = lax.map(block, (q_blocks, c_blocks, starts))
    return out.transpose(1, 0, 3, 2, 4).reshape(B, S, H * Dh)


def hgrn2_chunkwise(q, k, v, log_f):
    B, S, H, dk = q.shape
    dv = v.shape[-1]
    nc = S // CHUNK

    def to_chunks(t):
        return t.reshape(B, nc, CHUNK, H, t.shape[-1]).transpose(1, 0, 3, 2, 4)

    qc = to_chunks(q.astype(jnp.float32))
    kc = to_chunks(k.astype(jnp.float32))
    vc = to_chunks(v.astype(jnp.float32))
    G = jnp.cumsum(to_chunks(log_f.astype(jnp.float32)), axis=3)
    causal = jnp.tril(jnp.ones((CHUNK, CHUNK), dtype=bool))[:, :, None]
    causal_f = causal.astype(jnp.float32)

    def step(state, inp):
        qb, kb, vb, Gb = inp
        o_inter = jnp.einsum('bhtd,bhde->bhte', qb * jnp.exp(Gb), state)
        diff = Gb[:, :, :, None, :] - Gb[:, :, None, :, :]
        decay = jnp.exp(jnp.where(causal, diff, 0.0)) * causal_f
        a = jnp.einsum('bhtsd,bhsd->bhts', qb[:, :, :, None, :] * decay, kb)
        o_intra = jnp.einsum('bhts,bhse->bhte', a, vb)
        G_last = Gb[:, :, -1, :]
        k_dec = kb * jnp.exp(G_last[:, :, None, :] - Gb)
        state = jnp.exp(G_last)[..., None] * state + jnp.einsum('bhsd,bhse->bhde', k_dec, vb)
        return state, o_inter + o_intra

    state0 = jnp.zeros((B, H, dk, dv), jnp.float32)
    _, o = lax.scan(step, state0, (qc, kc, vc, G))
    return o.transpose(1, 0, 3, 2, 4).reshape(B, S, H, dv).astype(v.dtype)


def parallel_mixer(x, w_in, fox_f_bias, lb, hgrn_norm_w, w_out):
    B, S, _ = x.shape
    proj = x @ w_in
    fq, fk, fv, ff, hq, hf, hi, hg = jnp.split(proj, IN_SPLIT_POINTS, axis=-1)
    fox_log_f = jax.nn.log_sigmoid((ff + fox_f_bias).astype(jnp.float32))
    fox_out = fox_attention(fq.reshape(B, S, FOX_HEADS, FOX_HEAD_DIM),
                            fk.reshape(B, S, FOX_HEADS, FOX_HEAD_DIM),
                            fv.reshape(B, S, FOX_HEADS, FOX_HEAD_DIM),
                            fox_log_f)
    hz = hf.astype(jnp.float32)
    f_gate = lb + (1.0 - lb) * jax.nn.sigmoid(hz)
    log_f = jnp.log(jnp.maximum(f_gate, MIN_FORGET))
    k_in = (1.0 - lb) * jax.nn.sigmoid(-hz)
    hq_h = hq.reshape(B, S, HGRN_HEADS, HGRN_KEY_DIM) * (HGRN_KEY_DIM ** -0.5)
    h_out = hgrn2_chunkwise(hq_h,
                            k_in.reshape(B, S, HGRN_HEADS, HGRN_KEY_DIM),
                            hi.reshape(B, S, HGRN_HEADS, HGRN_VAL_DIM),
                            log_f.reshape(B, S, HGRN_HEADS, HGRN_KEY_DIM))
    h_out = rms_norm(h_out, hgrn_norm_w.reshape(HGRN_HEADS, HGRN_VAL_DIM))
    h_out = h_out.reshape(B, S, HGRN_VAL_WIDTH) * jax.nn.silu(hg)
    return jnp.concatenate([fox_out, h_out.astype(fox_out.dtype)], axis=-1) @ w_out


def cross_attention(x, mem, wq, wk, wv, wo):
    B, S, _ = x.shape
    M = mem.shape[1]
    q = (x @ wq).reshape(B, S, XATTN_HEADS, XATTN_HEAD_DIM)
    k = (mem @ wk).reshape(B, M, XATTN_HEADS, XATTN_HEAD_DIM)
    v = (mem @ wv).reshape(B, M, XATTN_HEADS, XATTN_HEAD_DIM)
    s = jnp.einsum('bqhd,bkhd->bhqk', q, k).astype(jnp.float32) * (XATTN_HEAD_DIM ** -0.5)
    p = jax.nn.softmax(s, axis=-1)
    o = jnp.einsum('bhqk,bkhd->bqhd', p.astype(v.dtype), v).reshape(B, S, D_MODEL)
    return o @ wo


def conv_ffn(x, w_up, conv_w, conv_b, w_down):
    h = x @ w_up
    h = lax.conv_general_dilated(h, conv_w[:, None, :], window_strides=(1,),
                                 padding=[(CONV_WIDTH - 1, 0)],
                                 dimension_numbers=('NWC', 'WIO', 'NWC'),
                                 feature_group_count=h.shape[-1]) + conv_b
    a, b = jnp.split(h, 2, axis=-1)
    return (jax.nn.silu(a) * b) @ w_down


def setup_inputs(seed: int = 0) -> dict:
    key = jax.random.key(seed)
    ks = jax.random.split(key, 24)
    L = DEPTH

    def nrm(k, shape, scale):
        return jax.random.normal(k, shape, jnp.float32) * scale

    return {
        'x': nrm(ks[0], (BATCH, SEQ, D_MODEL), 1.0),
        'mem': nrm(ks[1], (BATCH, N_MEM, D_MODEL), 1.0),
        'w_in': nrm(ks[2], (L, D_MODEL, N_IN), D_MODEL ** -0.5),
        'fox_f_bias': nrm(ks[3], (L, FOX_HEADS), 0.1),
        'hgrn_lb_logits': nrm(ks[4], (L, HGRN_KEY_WIDTH), 0.1),
        'hgrn_norm_w': 1.0 + nrm(ks[5], (L, HGRN_VAL_WIDTH), 0.01),
        'w_out': nrm(ks[6], (L, D_MIX, D_MODEL), D_MIX ** -0.5 * DEEPNORM_BETA),
        'ln1_g': 1.0 + nrm(ks[7], (L, D_MODEL), 0.01),
        'ln1_b': nrm(ks[8], (L, D_MODEL), 0.01),
        'xq_w': nrm(ks[9], (L, D_MODEL, D_MODEL), D_MODEL ** -0.5),
        'xk_w': nrm(ks[10], (L, D_MODEL, D_MODEL), D_MODEL ** -0.5),
        'xv_w': nrm(ks[11], (L, D_MODEL, D_MODEL), D_MODEL ** -0.5 * DEEPNORM_BETA),
        'xo_w': nrm(ks[12], (L, D_MODEL, D_MODEL), D_MODEL ** -0.5 * DEEPNORM_BETA),
        'ln2_g': 1.0 + nrm(ks[13], (L, D_MODEL), 0.01),
        'ln2_b': nrm(ks[14], (L, D_MODEL), 0.01),
        'ffn_up': nrm(ks[15], (L, D_MODEL, 2 * D_FF), D_MODEL ** -0.5),
        'conv_w': nrm(ks[16], (L, CONV_WIDTH, 2 * D_FF), CONV_WIDTH ** -0.5),
        'conv_b': nrm(ks[17], (L, 2 * D_FF), 0.01),
        'ffn_down': nrm(ks[18], (L, D_FF, D_MODEL), D_FF ** -0.5 * DEEPNORM_BETA),
        'ln3_g': 1.0 + nrm(ks[19], (L, D_MODEL), 0.01),
        'ln3_b': nrm(ks[20], (L, D_MODEL), 0.01),
    }


def reference(x, mem, w_in, fox_f_bias, hgrn_lb_logits, hgrn_norm_w, w_out,
              ln1_g, ln1_b, xq_w, xk_w, xv_w, xo_w, ln2_g, ln2_b,
              ffn_up, conv_w, conv_b, ffn_down, ln3_g, ln3_b):
    lb_soft = jax.nn.softmax(hgrn_lb_logits.astype(jnp.float32), axis=0)
    lower_bounds = jnp.cumsum(lb_soft, axis=0) - lb_soft[0]
    for l in range(DEPTH):
        mix = parallel_mixer(x, w_in[l], fox_f_bias[l], lower_bounds[l], hgrn_norm_w[l], w_out[l])
        x = layer_norm(DEEPNORM_ALPHA * x + mix, ln1_g[l], ln1_b[l])
        xa = cross_attention(x, mem, xq_w[l], xk_w[l], xv_w[l], xo_w[l])
        x = layer_norm(DEEPNORM_ALPHA * x + xa, ln2_g[l], ln2_b[l])
        ff = conv_ffn(x, ffn_up[l], conv_w[l], conv_b[l], ffn_down[l])
        x = layer_norm(DEEPNORM_ALPHA * x + ff, ln3_g[l], ln3_b[l])
    return x
```

```python
import numpy as np
import concourse.bass as bass
import concourse.mybir as mybir
from contextlib import ExitStack

F32 = mybir.dt.float32
BF16 = mybir.dt.bfloat16
I32 = mybir.dt.int32
AF = mybir.ActivationFunctionType
ALU = mybir.AluOpType

ENGS = ("pe", "act", "dve", "pool", "sp")
DMA_RING = 8


class _Op:
    __slots__ = ("fn", "deps", "kind", "signal", "count", "dslot", "dcount")

    def __init__(self, fn, deps, kind):
        self.fn = fn
        self.deps = deps
        self.kind = kind
        self.signal = False
        self.count = 0
        self.dslot = None
        self.dcount = 0


class Prog:
    def __init__(self, nc, st):
        self.nc = nc
        self.csem = {e: st.enter_context(nc.semaphore("c_" + e)) for e in ENGS}
        self.dsem = {e: [st.enter_context(nc.semaphore("d_%s_%d" % (e, i))) for i in range(DMA_RING)]
                     for e in ("sp", "pool")}
        self.ksem = st.enter_context(nc.semaphore("k_coll"))
        self.ncoll = 0
        self.cbase = {e: 0 for e in ENGS}
        self.ndma = {e: 0 for e in ENGS}
        self.barrier = []
        self.total_ops = {e: 0 for e in ENGS}
        self.total_waits = {e: 0 for e in ENGS}
        self._reset()

    def _reset(self):
        self.ops = {e: [] for e in ENGS}
        self.last_writer = {}
        self.readers = {}

    def _add(self, eng, fn, reads, writes, kind):
        deps = set()
        for k in reads:
            lw = self.last_writer.get(k)
            if lw is not None:
                deps.add(lw)
        for k in writes:
            lw = self.last_writer.get(k)
            if lw is not None:
                deps.add(lw)
            for r in self.readers.get(k, ()):
                deps.add(r)
        idx = len(self.ops[eng])
        if eng == "pe":
            deps = {d for d in deps if d[0] != "pe"}
        op = _Op(fn, deps, kind)
        if kind == "k":
            self.ncoll += 1
            op.dcount = self.ncoll
        if kind == "d":
            n = self.ndma[eng]
            self.ndma[eng] = n + 1
            op.dslot = n % DMA_RING
            op.dcount = (n // DMA_RING + 1) * 16
        self.ops[eng].append(op)
        me = (eng, idx)
        for k in writes:
            self.last_writer[k] = me
            self.readers[k] = []
        for k in reads:
            self.readers.setdefault(k, []).append(me)
        return me

    def op(self, eng, fn, reads=(), writes=()):
        return self._add(eng, fn, reads, writes, "c")

    def dma(self, eng, out, in_, reads=(), writes=(), **kw):
        return self._add(eng, lambda e: e.dma_start(out=out, in_=in_, **kw), reads, writes, "d")

    def flush(self, final=False):
        nc = self.nc
        for e in ENGS:
            for op in self.ops[e]:
                best = {}
                keep = set()
                for (de, di) in op.deps:
                    if self.ops[de][di].kind == "c":
                        if best.get(de, -1) < di:
                            best[de] = di
                    else:
                        keep.add((de, di))
                for de, di in best.items():
                    keep.add((de, di))
                    self.ops[de][di].signal = True
                op.deps = keep
            for op in reversed(self.ops[e]):
                if op.kind == "c":
                    op.signal = True
                    break
        for e in ENGS:
            c = self.cbase[e]
            for op in self.ops[e]:
                if op.kind == "c" and op.signal:
                    c += 1
                    op.count = c
            self.cbase[e] = c
        nxt = []
        for e in ENGS:
            if self.cbase[e] > 0:
                nxt.append((self.csem[e], self.cbase[e]))
        if self.ncoll > 0:
            nxt.append((self.ksem, self.ncoll))
        for e in self.dsem:
            n = self.ndma[e]
            for slot in range(DMA_RING):
                if n > slot:
                    cnt = ((n - 1 - slot) // DMA_RING + 1) * 16
                    nxt.append((self.dsem[e][slot], cnt))
        prev_barrier = self.barrier

        def resolve(dep):
            de, di = dep
            dop = self.ops[de][di]
            if dop.kind == "c":
                return self.csem[de], dop.count
            if dop.kind == "k":
                return self.ksem, dop.dcount
            return self.dsem[de][dop.dslot], dop.dcount

        def body(e, eng):
            waited = {}
            nw = 0

            def wait(sem, val):
                nonlocal nw
                key = id(sem)
                if waited.get(key, 0) < val:
                    eng.wait_ge(sem, val)
                    waited[key] = val
                    nw += 1

            for (s, v) in prev_barrier:
                wait(s, v)
            for op in self.ops[e]:
                for dep in sorted(op.deps):
                    s, v = resolve(dep)
                    wait(s, v)
                if op.kind == "k":
                    ins = op.fn(eng)
                    ins.then_inc(self.ksem, 1)
                elif op.kind == "d":
                    if op.dcount > 16:
                        wait(self.dsem[e][op.dslot], op.dcount - 16)
                    ins = op.fn(eng)
                    ins.then_inc(self.dsem[e][op.dslot], 16)
                else:
                    ins = op.fn(eng)
                    if op.signal:
                        ins.then_inc(self.csem[e], 1)
            if final and e == "sp":
                for (s, v) in nxt:
                    wait(s, v)
            self.total_ops[e] += len(self.ops[e])
            self.total_waits[e] += nw

        with nc.Block() as block:
            @block.tensor
            def _(eng):
                body("pe", eng)

            @block.scalar
            def _(eng):
                body("act", eng)

            @block.vector
            def _(eng):
                body("dve", eng)

            @block.gpsimd
            def _(eng):
                body("pool", eng)

            @block.sync
            def _(eng):
                body("sp", eng)
        self.barrier = nxt
        self._reset()


def _coll(self, kind, ins, outs, groups, reads=(), writes=()):
    return self._add("pool", lambda e: e.collective_compute(kind, ALU.bypass, replica_groups=groups,
                                                            ins=[a.opt() for a in ins], outs=[a.opt() for a in outs]),
                     reads, writes, "k")


Prog.coll = _coll


_UIDC = [0]


def uid():
    _UIDC[0] += 1
    return "_u%d" % _UIDC[0]


D = 2048
NH = 4
DH = 128
CH = 64
RMS_EPS = 1e-6
MIN_FORGET = 1e-6


def _xload(P, X, xs, dr, j, x_f32):
    if "xG" in dr:
        nb_half = dr["xG"].shape[0]
        r, blk = j // nb_half, j % nb_half
        view = dr["xG"][blk][r * 2048:(r + 1) * 2048, :].rearrange("(kc p) t -> p kc t", p=128)
        P.dma("sp", X[xs][:], view, writes=["mx%d" % xs])
        return
    view = dr["xT"][:, j * 512:(j + 1) * 512].rearrange("(kc p) t -> p kc t", p=128)
    P.dma("pool" if x_f32 else "sp", X[xs][:], view, writes=["mx%d" % xs])


def emit_fox(nc, P, S, dr, final_keys, x_f32=True, final=False, bgq=None, nbg=0):
    NB = S // 512
    with ExitStack() as st:
        sfx = uid()
        sb = lambda name, shape, dt: st.enter_context(nc.sbuf_tensor(name + sfx, shape, dt))
        W = [sb("mw%d" % i, [128, 16, 512], BF16) for i in range(3)]
        wff = sb("mwff", [128, 16, 4], BF16)
        X = [sb("mx%d" % i, [128, 16, 512], BF16) for i in range(2)]
        kT = sb("mkT", [128, NH, S], BF16)
        vt = sb("mvt", [128, S // 128, 512], BF16)
        qT = sb("mqT", [128, NH, 512], BF16)
        cbc = sb("mcbc", [128, NH, 512], F32)
        cbm = sb("mcbm", [128, NH, 4, 128], F32)
        nct = sb("mnct", [128, S // 128, NH], F32)
        NTMP, NPT = 4, 4
        tmp = [sb("mtmp%d" % i, [128, 512], F32) for i in range(NTMP)]
        PT = [sb("mPT%d" % i, [128, 512], BF16) for i in range(NPT)]
        ost = [sb("most%d" % i, [128, 512], BF16) for i in range(2)]
        rl = sb("mrl", [128, 512], F32)
        e1 = sb("me1", [4, 512], F32)
        lf = sb("mlf", [4, 512], F32)
        cT = sb("mcT", [4, 512], F32)
        cprev = sb("mcprev", [4, 1], F32)
        ones4 = sb("mones4", [4, 512], F32)
        nfb = sb("mnfb", [4, 1], F32)
        sel = sb("msel", [4, NH, 128], F32)
        id4 = sb("mid4", [4, 4], F32)
        nmask = sb("mnmask", [128, 128], F32)
        ones = sb("mones", [128, 128], BF16)
        ps = [st.enter_context(nc.psum_tensor("mps%d" % i + sfx, [128, 512], F32)) for i in range(8)]
        pn = [0]

        def bank():
            i = pn[0] % 4
            pn[0] += 1
            return i

        P.op("dve", lambda e: e.memset(ones[:], 1.0), writes=["mones"])
        P.op("dve", lambda e: e.memset(ones4[:], 1.0), writes=["mones4"])
        P.op("dve", lambda e: e.memset(cprev[:], 0.0), writes=["mcprev"])
        P.dma("sp", sel[:], dr["sel"], writes=["msel"])
        P.dma("sp", id4[:], dr["id4"], writes=["mid4"])
        P.dma("sp", nmask[:], dr["nmask"], writes=["mnmask"])
        P.dma("sp", nfb[:], dr["fbias"], writes=["mnfb"])
        P.op("dve", lambda e: e.tensor_scalar(out=nfb[:], in0=nfb[:], scalar1=-1.0, scalar2=None, op0=ALU.mult),
             reads=["mnfb"], writes=["mnfb"])
        _xload(P, X, 0, dr, 0, x_f32)
        for i in range(3):
            P.dma("pool", W[i][:], dr["wqkv"][:, i * 512:(i + 1) * 512].rearrange("(kc p) c -> p kc c", p=128),
                  writes=["mw%d" % i])
        P.dma("pool", wff[:], dr["wff"].rearrange("(kc p) c -> p kc c", p=128), writes=["mwff"])

        tn = [0]
        pn3 = [0]
        for j in range(NB):
            xs = j % 2
            xk = "mx%d" % xs
            if j + 1 < NB:
                _xload(P, X, (j + 1) % 2, dr, j + 1, x_f32)
            for _ in range(nbg):
                if bgq:
                    o_, i_ = bgq.pop(0)
                    P.dma("pool", o_, i_)
            b = bank()
            for kc in range(16):
                P.op("pe", lambda e, kc=kc, b=b, xs=xs: e.matmul(
                    ps[b][0:4, :], lhsT=wff[:, kc, :], rhs=X[xs][:, kc, :],
                    start=(kc == 0), stop=(kc == 15)), reads=["mwff", xk], writes=["mps%d" % b])
            P.op("act", lambda e, b=b: e.activation(out=e1[:], in_=ps[b][0:4, :], func=AF.Exp, scale=-1.0,
                                                     bias=nfb[:, 0:1]),
                 reads=["mps%d" % b, "mnfb"], writes=["me1"])
            P.op("act", lambda e: e.activation(out=lf[:], in_=e1[:], func=AF.Ln, bias=1.0),
                 reads=["me1"], writes=["mlf"])
            P.op("dve", lambda e: e.tensor_scalar(out=lf[:], in0=lf[:], scalar1=-1.0, scalar2=None, op0=ALU.mult),
                 reads=["mlf"], writes=["mlf"])
            P.op("dve", lambda e: e.tensor_tensor_scan(out=cT[:], data0=ones4[:], data1=lf[:], initial=cprev[:, 0:1],
                                                        op0=ALU.mult, op1=ALU.add),
                 reads=["mones4", "mlf", "mcprev"], writes=["mcT"])
            P.op("dve", lambda e: e.tensor_copy(out=cprev[:], in_=cT[:, 511:512]),
                 reads=["mcT"], writes=["mcprev"])
            for h in range(NH):
                b = bank()
                for kc in range(16):
                    P.op("pe", lambda e, kc=kc, h=h, b=b, xs=xs: e.matmul(
                        ps[b][:, :], lhsT=W[0][:, kc, h * 128:(h + 1) * 128], rhs=X[xs][:, kc, :],
                        start=(kc == 0), stop=(kc == 15)), reads=["mw0", xk], writes=["mps%d" % b])
                P.op("act", lambda e, h=h, b=b: e.activation(out=qT[:, h, :], in_=ps[b][:, :], func=AF.Copy,
                                                              scale=float(DH) ** -0.5),
                     reads=["mps%d" % b], writes=["mq%d" % h])
            for h in range(NH):
                b = bank()
                for kc in range(16):
                    P.op("pe", lambda e, kc=kc, h=h, b=b, xs=xs: e.matmul(
                        ps[b][:, :], lhsT=W[1][:, kc, h * 128:(h + 1) * 128], rhs=X[xs][:, kc, :],
                        start=(kc == 0), stop=(kc == 15)), reads=["mw1", xk], writes=["mps%d" % b])
                P.op("act", lambda e, h=h, b=b, j=j: e.activation(out=kT[:, h, j * 512:(j + 1) * 512],
                                                                  in_=ps[b][:, :], func=AF.Copy),
                     reads=["mps%d" % b], writes=["mk%d_%d" % (h, j)])
            for s4 in range(4):
                b = bank()
                for kc in range(16):
                    P.op("pe", lambda e, kc=kc, s4=s4, b=b, xs=xs: e.matmul(
                        ps[b][:, :], lhsT=X[xs][:, kc, s4 * 128:(s4 + 1) * 128], rhs=W[2][:, kc, :],
                        start=(kc == 0), stop=(kc == 15)), reads=["mw2", xk], writes=["mps%d" % b])
                P.op("dve", lambda e, s4=s4, b=b, j=j: e.tensor_copy(out=vt[:, 4 * j + s4, :], in_=ps[b][:, :]),
                     reads=["mps%d" % b], writes=["mv%d" % (4 * j + s4)])
            for s4 in range(4):
                b = bank()
                P.op("pe", lambda e, s4=s4, b=b: e.matmul(ps[b][:, 0:4], lhsT=cT[:, s4 * 128:(s4 + 1) * 128],
                                                          rhs=id4[:], start=True, stop=True),
                     reads=["mcT", "mid4"], writes=["mps%d" % b])
                P.op("dve", lambda e, s4=s4, b=b, j=j: e.tensor_scalar(out=nct[:, 4 * j + s4, :], in0=ps[b][:, 0:4],
                                                                        scalar1=-1.0, scalar2=None, op0=ALU.mult),
                     reads=["mps%d" % b], writes=["mnc%d" % (4 * j + s4)])
            for h in range(NH):
                b = bank()
                P.op("pe", lambda e, h=h, b=b: e.matmul(ps[b][:, :], lhsT=sel[:, h, :], rhs=cT[:],
                                                        start=True, stop=True),
                     reads=["mcT", "msel"], writes=["mps%d" % b])
                P.op("act", lambda e, h=h, b=b: e.activation(out=cbc[:, h, :], in_=ps[b][:, :], func=AF.Copy),
                     reads=["mps%d" % b], writes=["mcbc%d" % h])
                for r in range(4):
                    P.op("pool", lambda e, h=h, r=r: e.tensor_tensor(out=cbm[:, h, r, :],
                                                                     in0=cbc[:, h, r * 128:(r + 1) * 128],
                                                                     in1=nmask[:], op=ALU.add),
                         reads=["mcbc%d" % h, "mnmask"], writes=["mcbm%d_%d" % (h, r)])
            LOOK = 2
            for h in range(NH):
                bo, bl = 4 + h % 2, 6 + h % 2
                last = 4 * j + 3
                pend = {}

                def stage_a(i, h=h, j=j):
                    r = i - 4 * j
                    c0 = 128 * r if r >= 0 else 0
                    b = bank()
                    P.op("pe", lambda e, h=h, i=i, c0=c0, b=b: e.matmul(
                        ps[b][:, c0:512], lhsT=kT[:, h, i * 128:(i + 1) * 128], rhs=qT[:, h, c0:512],
                        start=True, stop=True),
                        reads=["mk%d_%d" % (h, i // 4), "mq%d" % h], writes=["mps%d" % b])
                    ti = tn[0] % NTMP
                    tn[0] += 1
                    tk = "mtmp%d" % ti
                    if r >= 0:
                        P.op("dve", lambda e, h=h, r=r, c0=c0, b=b, ti=ti: e.tensor_tensor(
                            out=tmp[ti][:, c0:c0 + 128], in0=ps[b][:, c0:c0 + 128], in1=cbm[:, h, r, :], op=ALU.add),
                            reads=["mps%d" % b, "mcbm%d_%d" % (h, r)], writes=[tk])
                        if c0 + 128 < 512:
                            P.op("dve", lambda e, h=h, c0=c0, b=b, ti=ti: e.tensor_tensor(
                                out=tmp[ti][:, c0 + 128:512], in0=ps[b][:, c0 + 128:512],
                                in1=cbc[:, h, c0 + 128:512], op=ALU.add),
                                reads=["mps%d" % b, "mcbc%d" % h, tk], writes=[tk])
                    else:
                        P.op("dve", lambda e, h=h, b=b, ti=ti: e.tensor_tensor(
                            out=tmp[ti][:, :], in0=ps[b][:, :], in1=cbc[:, h, :], op=ALU.add),
                            reads=["mps%d" % b, "mcbc%d" % h], writes=[tk])
                    pi = pn3[0] % NPT
                    pn3[0] += 1
                    pk = "mPT%d" % pi
                    P.op("act", lambda e, h=h, i=i, c0=c0, ti=ti, pi=pi: e.activation(
                        out=PT[pi][:, c0:512], in_=tmp[ti][:, c0:512], func=AF.Exp, bias=nct[:, i, h:h + 1]),
                        reads=[tk, "mnc%d" % i], writes=[pk])
                    pend[i] = (c0, pi, pk)

                def stage_b(i, h=h, bo=bo, bl=bl, last=last):
                    c0, pi, pk = pend.pop(i)
                    P.op("pe", lambda e, h=h, i=i, c0=c0, pi=pi, bo=bo, last=last: e.matmul(
                        ps[bo][:, c0:512], lhsT=vt[:, i, h * 128:(h + 1) * 128], rhs=PT[pi][:, c0:512],
                        start=(i == 0), stop=(i == last)),
                        reads=["mv%d" % i, pk], writes=["mps%d" % bo])
                    P.op("pe", lambda e, i=i, c0=c0, pi=pi, bl=bl, last=last: e.matmul(
                        ps[bl][:, c0:512], lhsT=ones[:], rhs=PT[pi][:, c0:512],
                        start=(i == 0), stop=(i == last)),
                        reads=["mones", pk], writes=["mps%d" % bl])

                for i in range(min(LOOK, last + 1)):
                    stage_a(i)
                for i in range(last + 1):
                    if i + LOOK <= last:
                        stage_a(i + LOOK)
                    stage_b(i)
                P.op("dve", lambda e, bl=bl: e.reciprocal(out=rl[:], in_=ps[bl][:, :]),
                     reads=["mps%d" % bl], writes=["mrl"])
                oi = (j * NH + h) % 2
                P.op("dve", lambda e, bo=bo, oi=oi: e.tensor_tensor(out=ost[oi][:], in0=ps[bo][:, :], in1=rl[:],
                                                                    op=ALU.mult),
                     reads=["mps%d" % bo, "mrl"], writes=["most%d" % oi])
                ok = "mixf%d_%d" % (h, j)
                P.dma("sp", dr["mix"][h * 128:(h + 1) * 128, j * 512:(j + 1) * 512], ost[oi][:],
                      reads=["most%d" % oi], writes=[ok])
                final_keys.append(ok)
        P.flush(final=final)


def emit_hgrn(nc, P, S, layer, dr, final_keys, x_f32=True, final=False, bgq=None, nbg=0):
    NB = S // 512
    with ExitStack() as st:
        sfx = uid()
        sb = lambda name, shape, dt: st.enter_context(nc.sbuf_tensor(name + sfx, shape, dt))
        W = [sb("hw%d" % i, [128, 16, 512], BF16) for i in range(4)]
        X = [sb("hx%d" % i, [128, 16, 512], BF16) for i in range(2)]
        tsig = sb("htsig", [128, 512], F32)
        tsgm = sb("htsgm", [128, 512], F32)
        tf = sb("htf", [128, 512], F32)
        tG = sb("htG", [128, 512], F32)
        tD = sb("htD", [128, 512], F32)
        tD3 = sb("htD3", [128, 512], F32)
        tE1 = sb("htE1", [128, 512], F32)
        tE2 = sb("htE2", [128, 512], F32)
        tEG = sb("htEG", [128, 512], F32)
        tE3 = sb("htE3", [128, 512], F32)
        qtil = sb("hqtil", [128, NH, 512], BF16)
        ktil = sb("hktil", [128, NH, 512], BF16)
        qg = sb("hqg", [128, NH, 512], BF16)
        kdT = [sb("hkdT%d" % i, [128, 512], BF16) for i in range(2)]
        kdec = sb("hkdec", [64, NH, 8, 128], BF16)
        vh = sb("hvh", [64, 8, 512], BF16)
        EGl = sb("hEGl", [128, NH, 8], F32)
        sg = sb("hsg", [128, NH, 512], F32)
        oT = sb("hoT", [128, NH, 512], F32)
        osq = sb("hosq", [128, 512], BF16)
        rs = sb("hrs", [128, 512], F32)
        hn = sb("hhn", [128, 512], F32)
        ost = [sb("host%d" % i, [128, 512], BF16) for i in range(2)]
        stf = sb("hstf", [128, NH, 128], F32)
        stb = sb("hstb", [128, NH, 128], BF16)
        ATs = [sb("hATs%d" % i, [64, 64], BF16) for i in range(4)]
        rst = sb("hrst", [128, 512], F32)
        tri = sb("htri", [64, 64], I32)
        identb = sb("hidentb", [128, 128], BF16)
        ones = sb("hones", [128, 128], BF16)
        lbl = sb("hlbl", [128, 2, NH], F32)
        lbe = sb("hlbe", [128, 2, NH], F32)
        lbz = sb("hlbz", [128, NH], F32)
        lbs = sb("hlbs", [128, 2, NH], F32)
        lb = sb("hlb", [128, NH], F32)
        oml = sb("homl", [128, NH], F32)
        nw = sb("hnw", [128, NH], F32)
        ps = [st.enter_context(nc.psum_tensor("hps%d" % i + sfx, [128, 512], F32)) for i in range(7)]
        psb = st.enter_context(nc.psum_tensor("hpsb" + sfx, [128, 1024], BF16))
        pn = [0]

        def bank():
            i = pn[0] % 7
            pn[0] += 1
            return i

        P.op("dve", lambda e: e.memset(ones[:], 1.0), writes=["hones"])
        P.op("dve", lambda e: e.memset(stf[:], 0.0), writes=["hstf%d" % h for h in range(NH)])
        P.op("dve", lambda e: e.memset(stb[:], 0.0), writes=["hstb%d" % h for h in range(NH)])
        for i in range(4):
            P.op("pool", lambda e, i=i: e.memset(ATs[i][:], 0.0), writes=["hATs%d" % i])
        P.dma("sp", rst[:], dr["rst"], writes=["hrst"])
        P.dma("sp", tri[:], dr["tri"], writes=["htri"])
        P.dma("sp", identb[:], dr["identb"], writes=["hidentb"])
        P.dma("sp", lbl[:], dr["lbl"], writes=["hlbl"])
        P.dma("sp", nw[:], dr["normw"], writes=["hnw"])
        _xload(P, X, 0, dr, 0, x_f32)
        for i in range(4):
            P.dma("pool", W[i][:], dr["wh"][:, i * 512:(i + 1) * 512].rearrange("(kc p) c -> p kc c", p=128),
                  writes=["hw%d" % i])
        P.op("act", lambda e: e.activation(out=lbe[:], in_=lbl[:], func=AF.Exp), reads=["hlbl"], writes=["hlbe"])
        P.op("dve", lambda e: e.tensor_tensor(out=lbz[:], in0=lbe[:, 0, :], in1=lbe[:, 1, :], op=ALU.add),
             reads=["hlbe"], writes=["hlbz"])
        P.op("dve", lambda e: e.reciprocal(out=lbz[:], in_=lbz[:]), reads=["hlbz"], writes=["hlbz"])
        for l in range(2):
            P.op("dve", lambda e, l=l: e.tensor_tensor(out=lbs[:, l, :], in0=lbe[:, l, :], in1=lbz[:], op=ALU.mult),
                 reads=["hlbe", "hlbz"], writes=["hlbs%d" % l])
        if layer == 0:
            P.op("dve", lambda e: e.tensor_tensor(out=lb[:], in0=lbs[:, 0, :], in1=lbs[:, 0, :], op=ALU.subtract),
                 reads=["hlbs0"], writes=["hlb"])
        else:
            P.op("dve", lambda e: e.tensor_tensor(out=lb[:], in0=lbs[:, 0, :], in1=lbs[:, 1, :], op=ALU.add),
                 reads=["hlbs0", "hlbs1"], writes=["hlb"])
            P.op("dve", lambda e: e.tensor_tensor(out=lb[:], in0=lb[:], in1=lbs[:, 0, :], op=ALU.subtract),
                 reads=["hlb", "hlbs0"], writes=["hlb"])
        P.op("dve", lambda e: e.tensor_scalar(out=oml[:], in0=lb[:], scalar1=-1.0, scalar2=1.0, op0=ALU.mult,
                                               op1=ALU.add), reads=["hlb"], writes=["homl"])

        an = [0]
        for j in range(NB):
            xs = j % 2
            xk = "mx%d" % xs
            if j + 1 < NB:
                _xload(P, X, (j + 1) % 2, dr, j + 1, x_f32)
            for _ in range(nbg):
                if bgq:
                    o_, i_ = bgq.pop(0)
                    P.dma("pool", o_, i_)
            for c in range(8):
                b = bank()
                for kc in range(16):
                    P.op("pe", lambda e, kc=kc, c=c, b=b, xs=xs: e.matmul(
                        ps[b][0:64, :], lhsT=X[xs][:, kc, c * 64:(c + 1) * 64], rhs=W[2][:, kc, :],
                        start=(kc == 0), stop=(kc == 15)), reads=["hw2", xk], writes=["hps%d" % b])
                P.op("act", lambda e, c=c, b=b: e.activation(out=vh[:, c, :], in_=ps[b][0:64, :], func=AF.Copy),
                     reads=["hps%d" % b], writes=["hvh%d" % c])
            pend_x = []
            for h in range(NH):
                def proj(wi, h=h):
                    b = bank()
                    for kc in range(16):
                        P.op("pe", lambda e, kc=kc, b=b, xs=xs: e.matmul(
                            ps[b][:, :], lhsT=W[wi][:, kc, h * 128:(h + 1) * 128], rhs=X[xs][:, kc, :],
                            start=(kc == 0), stop=(kc == 15)), reads=["hw%d" % wi, xk], writes=["hps%d" % b])
                    return b
                bf = proj(1)
                kf = "hps%d" % bf
                P.op("act", lambda e, bf=bf: e.activation(out=tsig[:], in_=ps[bf][:, :], func=AF.Sigmoid),
                     reads=[kf], writes=["htsig"])
                P.op("act", lambda e, bf=bf: e.activation(out=tsgm[:], in_=ps[bf][:, :], func=AF.Sigmoid, scale=-1.0),
                     reads=[kf], writes=["htsgm"])
                P.op("dve", lambda e, h=h: e.tensor_scalar(out=tf[:], in0=tsig[:], scalar1=oml[:, h:h + 1],
                                                            scalar2=lb[:, h:h + 1], op0=ALU.mult, op1=ALU.add),
                     reads=["htsig", "homl", "hlb"], writes=["htf"])
                P.op("dve", lambda e: e.tensor_scalar(out=tf[:], in0=tf[:], scalar1=MIN_FORGET, scalar2=None,
                                                       op0=ALU.max), reads=["htf"], writes=["htf"])
                P.op("act", lambda e: e.activation(out=tf[:], in_=tf[:], func=AF.Ln), reads=["htf"], writes=["htf"])
                P.op("dve", lambda e: e.tensor_tensor_scan(out=tG[:], data0=rst[:], data1=tf[:], initial=0.0,
                                                            op0=ALU.mult, op1=ALU.add),
                     reads=["hrst", "htf"], writes=["htG"])
                Gv = tG[:].rearrange("p (c s) -> p c s", s=CH)
                P.op("dve", lambda e: e.tensor_tensor(out=tD[:].rearrange("p (c s) -> p c s", s=CH), in0=Gv,
                                                       in1=Gv[:, :, 31:32].to_broadcast([128, 8, CH]),
                                                       op=ALU.subtract), reads=["htG"], writes=["htD"])
                P.op("dve", lambda e: e.tensor_tensor(out=tD3[:].rearrange("p (c s) -> p c s", s=CH),
                                                       in0=Gv[:, :, 63:64].to_broadcast([128, 8, CH]), in1=Gv,
                                                       op=ALU.subtract), reads=["htG"], writes=["htD3"])
                P.op("act", lambda e: e.activation(out=tE1[:], in_=tD[:], func=AF.Exp), reads=["htD"], writes=["htE1"])
                P.op("act", lambda e: e.activation(out=tE2[:], in_=tD[:], func=AF.Exp, scale=-1.0),
                     reads=["htD"], writes=["htE2"])
                P.op("act", lambda e: e.activation(out=tEG[:], in_=tG[:], func=AF.Exp), reads=["htG"], writes=["htEG"])
                P.op("act", lambda e: e.activation(out=tE3[:], in_=tD3[:], func=AF.Exp),
                     reads=["htD3"], writes=["htE3"])
                P.op("pool", lambda e, h=h: e.tensor_copy(
                    out=EGl[:, h, :], in_=tEG[:].rearrange("p (c s) -> p c s", s=CH)[:, :, 63]),
                    reads=["htEG"], writes=["hEGl%d" % h])
                bq = proj(0)
                kq = "hps%d" % bq
                P.op("dve", lambda e, h=h, bq=bq: e.scalar_tensor_tensor(
                    out=qtil[:, h, :], in0=ps[bq][:, :], scalar=float(DH) ** -0.5, in1=tE1[:],
                    op0=ALU.mult, op1=ALU.mult), reads=[kq, "htE1"], writes=["hqtil%d" % h])
                P.op("dve", lambda e, h=h, bq=bq: e.scalar_tensor_tensor(
                    out=qg[:, h, :], in0=ps[bq][:, :], scalar=float(DH) ** -0.5, in1=tEG[:],
                    op0=ALU.mult, op1=ALU.mult), reads=[kq, "htEG"], writes=["hqg%d" % h])
                P.op("dve", lambda e, h=h: e.scalar_tensor_tensor(
                    out=ktil[:, h, :], in0=tsgm[:], scalar=oml[:, h:h + 1], in1=tE2[:],
                    op0=ALU.mult, op1=ALU.mult), reads=["htsgm", "homl", "htE2"], writes=["hktil%d" % h])
                P.op("dve", lambda e, h=h: e.scalar_tensor_tensor(
                    out=kdT[h % 2][:], in0=tsgm[:], scalar=oml[:, h:h + 1], in1=tE3[:],
                    op0=ALU.mult, op1=ALU.mult), reads=["htsgm", "homl", "htE3"], writes=["hkdT%d" % (h % 2)])
                bg = proj(3)
                P.op("act", lambda e, h=h, bg=bg: e.activation(out=sg[:, h, :], in_=ps[bg][:, :], func=AF.Silu),
                     reads=["hps%d" % bg], writes=["hsg%d" % h])

                def xpose(h=h):
                    kd = kdT[h % 2]
                    kk = "hkdT%d" % (h % 2)
                    for c in range(8):
                        P.op("pe", lambda e, c=c, kd=kd: e.transpose(out=psb[0:64, c * 128:(c + 1) * 128],
                                                                     in_=kd[:, c * 64:(c + 1) * 64],
                                                                     identity=identb[:]),
                             reads=[kk, "hidentb"], writes=["hpsb"])
                    P.op("act", lambda e, h=h: e.activation(out=kdec[:, h, :, :].rearrange("p c d -> p (c d)"),
                                                             in_=psb[0:64, :], func=AF.Copy),
                         reads=["hpsb"], writes=["hkdec%d" % h])
                if pend_x:
                    pend_x.pop(0)()
                pend_x.append(xpose)
            while pend_x:
                pend_x.pop(0)()
            for c in range(8):
                aks = []
                for h in range(NH):
                    b = bank()
                    P.op("pe", lambda e, c=c, h=h, b=b: e.matmul(
                        ps[b][0:64, 0:64], lhsT=ktil[:, h, c * 64:(c + 1) * 64], rhs=qtil[:, h, c * 64:(c + 1) * 64],
                        start=True, stop=True), reads=["hktil%d" % h, "hqtil%d" % h], writes=["hps%d" % b])
                    ai = an[0] % 4
                    an[0] += 1
                    ak = "hATs%d" % ai
                    P.op("dve", lambda e, b=b, ai=ai: e.copy_predicated(out=ATs[ai][:], mask=tri[:],
                                                                        data=ps[b][0:64, 0:64]),
                         reads=["hps%d" % b, "htri", ak], writes=[ak])
                    aks.append((ai, ak))
                for h in range(NH):
                    ai, ak = aks[h]
                    b2 = bank()
                    P.op("pe", lambda e, c=c, h=h, b2=b2, ai=ai: e.matmul(
                        ps[b2][:, 0:64], lhsT=vh[:, c, h * 128:(h + 1) * 128], rhs=ATs[ai][:],
                        start=True, stop=False), reads=["hvh%d" % c, ak], writes=["hps%d" % b2])
                    P.op("pe", lambda e, c=c, h=h, b2=b2: e.matmul(
                        ps[b2][:, 0:64], lhsT=stb[:, h, :], rhs=qg[:, h, c * 64:(c + 1) * 64],
                        start=False, stop=True), reads=["hstb%d" % h, "hqg%d" % h], writes=["hps%d" % b2])
                    P.op("act", lambda e, c=c, h=h, b2=b2: e.activation(out=oT[:, h, c * 64:(c + 1) * 64],
                                                                        in_=ps[b2][:, 0:64], func=AF.Copy),
                         reads=["hps%d" % b2], writes=["hoT%d_%d" % (h, c)])
                    b3 = bank()
                    P.op("pe", lambda e, c=c, h=h, b3=b3: e.matmul(
                        ps[b3][:, 0:128], lhsT=kdec[:, h, c, :], rhs=vh[:, c, h * 128:(h + 1) * 128],
                        start=True, stop=True), reads=["hkdec%d" % h, "hvh%d" % c], writes=["hps%d" % b3])
                    P.op("dve", lambda e, c=c, h=h, b3=b3: e.scalar_tensor_tensor(
                        out=stf[:, h, :], in0=stf[:, h, :], scalar=EGl[:, h, c:c + 1], in1=ps[b3][:, 0:128],
                        op0=ALU.mult, op1=ALU.add),
                        reads=["hstf%d" % h, "hEGl%d" % h, "hps%d" % b3], writes=["hstf%d" % h])
                    P.op("pool", lambda e, h=h: e.tensor_copy(out=stb[:, h, :], in_=stf[:, h, :]),
                         reads=["hstf%d" % h], writes=["hstb%d" % h])
            for h in range(NH):
                P.op("act", lambda e, h=h: e.activation(out=osq[:], in_=oT[:, h, :], func=AF.Square),
                     reads=["hoT%d_%d" % (h, c) for c in range(8)], writes=["hosq"])
                b = bank()
                P.op("pe", lambda e, b=b: e.matmul(ps[b][:, :], lhsT=ones[:], rhs=osq[:], start=True, stop=True),
                     reads=["hones", "hosq"], writes=["hps%d" % b])
                P.op("dve", lambda e, b=b: e.tensor_scalar(out=rs[:], in0=ps[b][:, :], scalar1=1.0 / DH,
                                                            scalar2=RMS_EPS, op0=ALU.mult, op1=ALU.add),
                     reads=["hps%d" % b], writes=["hrs"])
                P.op("act", lambda e: e.activation(out=rs[:], in_=rs[:], func=AF.Sqrt), reads=["hrs"], writes=["hrs"])
                P.op("dve", lambda e: e.reciprocal(out=rs[:], in_=rs[:]), reads=["hrs"], writes=["hrs"])
                P.op("dve", lambda e, h=h: e.scalar_tensor_tensor(
                    out=hn[:], in0=oT[:, h, :], scalar=nw[:, h:h + 1], in1=rs[:], op0=ALU.mult, op1=ALU.mult),
                    reads=["hoT%d_%d" % (h, c) for c in range(8)] + ["hnw", "hrs"], writes=["hhn"])
                oi = (j * NH + h) % 2
                P.op("dve", lambda e, h=h, oi=oi: e.tensor_tensor(out=ost[oi][:], in0=hn[:], in1=sg[:, h, :],
                                                                  op=ALU.mult),
                     reads=["hhn", "hsg%d" % h], writes=["host%d" % oi])
                ok = "mixh%d_%d" % (h, j)
                P.dma("sp", dr["mix"][512 + h * 128:512 + (h + 1) * 128, j * 512:(j + 1) * 512], ost[oi][:],
                      reads=["host%d" % oi], writes=[ok])
                final_keys.append(ok)
        P.flush(final=final)


D = 2048
DFF = 5632
NMEM = 256
ALPHA = 4.0 ** 0.25
LN_EPS = 1e-5
XH = 4
XDH = 512


class CG:
    def __init__(self, name, n, A, B, G):
        self.name, self.n, self.A, self.B, self.G = name, n, A, B, G

    def a(self, kc):
        return self.A[:, kc, :self.n]

    def b(self, kc):
        return self.B[:, kc, :self.n]

    def g(self, kc):
        return self.G[:, kc, :self.n]

    def ka(self, kc):
        return "%s.A%d" % (self.name, kc)

    def kb(self, kc):
        return "%s.B%d" % (self.name, kc)

    def kg(self, kc):
        return "%s.G%d" % (self.name, kc)


class TCtx:
    def __init__(self, nc, P, st, NT):
        self.nc, self.P, self.NT = nc, P, NT
        sfx = uid()
        sb = lambda name, shape, dt: st.enter_context(nc.sbuf_tensor(name + sfx, shape, dt))
        self.W = [sb("tw%d" % i, [128, 16, 512], BF16) for i in range(3)]
        self.wn = 0
        self.A = sb("tA", [128, 16, 512], F32)
        self.B = sb("tB", [128, 16, 512], BF16)
        self.G = sb("tG", [128, 44, 512], BF16)
        self.Ah = sb("tAh", [128, 16, 2], F32)
        self.Bh = sb("tBh", [128, 16, 2], BF16)
        self.Gh = sb("tGh", [128, 32, 2], BF16)
        self.memT = sb("tmemT", [128, 16, NMEM], BF16)
        self.KT = sb("tKT", [128, 16, NMEM], BF16)
        self.V = sb("tV", [128, 2, D], BF16)
        self.PT = [sb("tPT%d" % i, [128, 2, 512], BF16) for i in range(2)]
        self.rl = [sb("trl%d" % i, [128, 512], F32) for i in range(2)]
        self.mean = sb("tmean", [128, 512], F32)
        self.msq = sb("tmsq", [128, 512], F32)
        self.rstd = sb("trstd", [128, 512], F32)
        self.lt = [sb("tlt%d" % i, [128, 512], F32) for i in range(4)]
        self.hb = [sb("thb%d" % i, [128, 514], F32) for i in range(4)]
        self.ct = [sb("tct%d" % i, [128, 512], F32) for i in range(4)]
        self.hal = sb("thal", [128, 88, 2], F32)
        self.ones = sb("tones", [128, 128], BF16)
        self.flag = sb("tflag", [128, 1], F32)
        self.epsb = sb("tepsb", [128, 1], F32)
        self.nflag = sb("tnflag", [128, 1], F32)
        self.lnp = sb("tlnp", [128, 6, 16], F32)
        self.cw = sb("tcw", [128, 88, 3], F32)
        self.cb = sb("tcb", [128, 88], F32)
        self.ps = [st.enter_context(nc.psum_tensor("tps%d" % i + sfx, [128, 512], F32)) for i in range(8)]
        self.pn = 0
        self.ltn = 0
        self.hbn = 0

    def bank(self):
        i = self.pn % 4
        self.pn += 1
        return i

    def wload(self, view, nk):
        s = self.wn % 3
        self.wn += 1
        q = "sp" if view.dtype == BF16 else "pool"
        self.P.dma(q, self.W[s][:, :nk, :], view, writes=["tw%d" % s])
        return s


def wview(dr, name, r0, nk, c0):
    if name + "_b" in dr:
        t = (c0 // 512) * 4 + r0 // (11 * 128) if name == "down" else c0 // 512
        return dr[name + "_b"][t].rearrange("p (kc c) -> p kc c", c=512)
    return dr[name][r0:r0 + nk * 128, c0:c0 + 512].rearrange("(kc p) c -> p kc c", p=128)


def conv_tasks(dr):
    tasks = []
    for name, ntile in (("xk", 4), ("xv", 4), ("w_out", 4), ("xq", 4), ("xo", 4), ("up", 22)):
        for t in range(ntile):
            tasks.append((dr[name + "_b"][t].rearrange("p (kc c) -> p kc c", c=512),
                          dr[name][:, t * 512:(t + 1) * 512].rearrange("(kc p) c -> p kc c", p=128)))
    for ct in range(4):
        for kg in range(4):
            tasks.append((dr["down_b"][ct * 4 + kg].rearrange("p (kc c) -> p kc c", c=512),
                          dr["down"][kg * 1408:(kg + 1) * 1408, ct * 512:(ct + 1) * 512].rearrange(
                              "(kc p) c -> p kc c", p=128)))
    return tasks


def dense(T, dr, name, nk, ncols, cgs, src, ksrc, evac, col0=0):
    P = T.P
    for ct in range(ncols // 512):
        c0 = col0 + ct * 512
        s = T.wload(wview(dr, name, 0, nk, c0), nk)
        for cg in cgs:
            for jc in range(4):
                b = T.bank()
                for kc in range(nk):
                    P.op("pe", lambda e, s=s, kc=kc, jc=jc, b=b, cg=cg: e.matmul(
                        T.ps[b][:, :cg.n], lhsT=T.W[s][:, kc, jc * 128:(jc + 1) * 128], rhs=src(cg, kc),
                        start=(kc == 0), stop=(kc == nk - 1)),
                        reads=["tw%d" % s, ksrc(cg, kc)], writes=["tps%d" % b])
                evac(cg, ct * 4 + jc, b)


def ln_pre(T, cg, kc, early):
    P = T.P
    dst, dk = (cg.g(kc), cg.kg(kc)) if early else (cg.b(kc), cg.kb(kc))
    P.op("pool" if kc % 2 else "dve", lambda e: e.tensor_copy(out=dst, in_=cg.a(kc)),
         reads=[cg.ka(kc)], writes=[dk])
    P.op("act", lambda e: e.activation(out=cg.g(16 + kc), in_=cg.a(kc), func=AF.Square),
         reads=[cg.ka(kc)], writes=[cg.kg(16 + kc)])


def layernorm(T, cg, li, early=False):
    P = T.P
    n = cg.n
    gcol, bcol = 2 * li, 2 * li + 1
    if not early:
        for kc in range(16):
            ln_pre(T, cg, kc, False)
    b1, b2 = 4, 5
    for kc in range(16):
        P.op("pe", lambda e, kc=kc: e.matmul(T.ps[b1][:, :n], lhsT=T.ones[:],
                                              rhs=(cg.g(kc) if early else cg.b(kc)),
                                              start=(kc == 0), stop=(kc == 15)),
             reads=["tones", cg.kg(kc) if early else cg.kb(kc)], writes=["tps%d" % b1])
    for kc in range(16):
        P.op("pe", lambda e, kc=kc: e.matmul(T.ps[b2][:, :n], lhsT=T.ones[:], rhs=cg.g(16 + kc),
                                              start=(kc == 0), stop=(kc == 15)),
             reads=["tones", cg.kg(16 + kc)], writes=["tps%d" % b2])
    P.op("act", lambda e: e.activation(out=T.mean[:, :n], in_=T.ps[b1][:, :n], func=AF.Copy, scale=1.0 / D),
         reads=["tps%d" % b1], writes=["tmean"])
    P.op("act", lambda e: e.activation(out=T.msq[:, :n], in_=T.ps[b1][:, :n], func=AF.Square, scale=1.0 / D),
         reads=["tps%d" % b1], writes=["tmsq"])
    P.op("dve", lambda e: e.scalar_tensor_tensor(out=T.rstd[:, :n], in0=T.ps[b2][:, :n], scalar=1.0 / D,
                                                  in1=T.msq[:, :n], op0=ALU.mult, op1=ALU.subtract),
         reads=["tps%d" % b2, "tmsq"], writes=["trstd"])
    P.op("act", lambda e: e.activation(out=T.rstd[:, :n], in_=T.rstd[:, :n], func=AF.Sqrt, bias=T.epsb[:, 0:1]),
         reads=["trstd", "tepsb"], writes=["trstd"])
    P.op("dve", lambda e: e.reciprocal(out=T.rstd[:, :n], in_=T.rstd[:, :n]),
         reads=["trstd"], writes=["trstd"])
    for kc in range(16):
        li_ = T.ltn % 4
        T.ltn += 1
        lt = T.lt[li_]
        lk = "tlt%d" % li_
        P.op("dve", lambda e, kc=kc, lt=lt: e.tensor_tensor(out=lt[:, :n], in0=cg.a(kc), in1=T.mean[:, :n],
                                                            op=ALU.subtract),
             reads=[cg.ka(kc), "tmean"], writes=[lk])
        P.op("pool" if kc % 2 else "dve",
             lambda e, lt=lt: e.tensor_tensor(out=lt[:, :n], in0=lt[:, :n], in1=T.rstd[:, :n], op=ALU.mult),
             reads=[lk, "trstd"], writes=[lk])
        P.op("act", lambda e, kc=kc, lt=lt: e.activation(out=cg.a(kc), in_=lt[:, :n], func=AF.Identity,
                                                          scale=T.lnp[:, gcol, kc:kc + 1],
                                                          bias=T.lnp[:, bcol, kc:kc + 1]),
             reads=[lk, "tlnp"], writes=[cg.ka(kc)])
        P.op("act", lambda e, kc=kc, lt=lt: e.activation(out=cg.b(kc), in_=lt[:, :n], func=AF.Identity,
                                                          scale=T.lnp[:, gcol, kc:kc + 1],
                                                          bias=T.lnp[:, bcol, kc:kc + 1]),
             reads=[lk, "tlnp"], writes=[cg.kb(kc)])


def emit_T(nc, P, NT, dr, final_keys, final=False):
    with ExitStack() as st:
        _emit_T(nc, P, st, NT, dr, final_keys)
        P.flush(final=final)


def _emit_T(nc, P, st, NT, dr, final_keys):
    T = TCtx(nc, P, st, NT)
    NB = NT // 512
    P.op("dve", lambda e: e.memset(T.ones[:], 1.0), writes=["tones"])
    P.op("dve", lambda e: e.memset(T.epsb[:], LN_EPS), writes=["tepsb"])
    P.dma("sp", T.flag[:], dr["flag"], writes=["tflag"])
    P.op("dve", lambda e: e.tensor_scalar(out=T.nflag[:], in0=T.flag[:], scalar1=-1.0, scalar2=1.0,
                                           op0=ALU.mult, op1=ALU.add), reads=["tflag"], writes=["tflag"])
    P.dma("sp", T.lnp[:], dr["lnp"], writes=["tlnp"])
    P.dma("sp", T.cw[:], dr["cw"], writes=["tcw"])
    P.dma("sp", T.cb[:], dr["cb"], writes=["tcb"])
    P.dma("pool", T.memT[:], dr["memT"].rearrange("(kc p) m -> p kc m", p=128), writes=["tmemT"])

    memcg = CG("mem", NMEM, None, None, None)

    def ev_k(cg, j, b):
        P.op("act", lambda e: e.activation(out=T.KT[:, j, :], in_=T.ps[b][:, :NMEM], func=AF.Copy),
             reads=["tps%d" % b], writes=["tKT%d" % j])

    dense(T, dr, "xk", 16, D, [memcg], lambda cg, kc: T.memT[:, kc, :], lambda cg, kc: "tmemT", ev_k)
    for ct in range(4):
        s = T.wload(wview(dr, "xv", 0, 16, ct * 512), 16)
        for mc in range(2):
            b = T.bank()
            for kc in range(16):
                P.op("pe", lambda e, s=s, kc=kc, mc=mc, b=b: e.matmul(
                    T.ps[b][:, :], lhsT=T.memT[:, kc, mc * 128:(mc + 1) * 128], rhs=T.W[s][:, kc, :],
                    start=(kc == 0), stop=(kc == 15)),
                    reads=["tw%d" % s, "tmemT"], writes=["tps%d" % b])
            P.op("act", lambda e, mc=mc, ct=ct, b=b: e.activation(out=T.V[:, mc, ct * 512:(ct + 1) * 512],
                                                                in_=T.ps[b][:, :], func=AF.Copy),
                 reads=["tps%d" % b], writes=["tV%d_%d" % (mc, ct)])

    main = CG("m", 512, T.A, T.B, T.G)
    halo = CG("h", 2, T.Ah, T.Bh, T.Gh)

    for blk in range(NB):
        t0 = blk * 512
        cgs = [main, halo] if blk == 0 else [main]
        P.dma("sp", T.A[:], dr["xres"][:, t0:t0 + 512].rearrange("(kc p) t -> p kc t", p=128),
              writes=[main.ka(kc) for kc in range(16)])
        if "mixG" in dr:
            for hf_ in range(2):
                P.dma("sp", T.G[:, 16 * hf_:16 * hf_ + 16, :],
                      dr["mixG"][:, hf_ * NT + t0:hf_ * NT + t0 + 512].rearrange("(kc p) t -> p kc t", p=128),
                      reads=["mixG"], writes=[main.kg(16 * hf_ + kc) for kc in range(16)])
            for kc in range(16):
                P.op("dve", lambda e, kc=kc: e.tensor_scalar(out=main.b(kc), in0=main.g(kc), scalar1=T.nflag[:, 0:1],
                                                              scalar2=None, op0=ALU.mult),
                     reads=[main.kg(kc), "tflag"], writes=[main.kb(kc)])
                P.op("dve", lambda e, kc=kc: e.scalar_tensor_tensor(out=main.b(kc), in0=main.g(16 + kc),
                                                                     scalar=T.flag[:, 0:1], in1=main.b(kc),
                                                                     op0=ALU.mult, op1=ALU.add),
                     reads=[main.kg(16 + kc), "tflag", main.kb(kc)], writes=[main.kb(kc)])
        else:
            P.dma("sp", T.B[:], dr["mix"][:, t0:t0 + 512].rearrange("(kc p) t -> p kc t", p=128),
                  writes=[main.kb(kc) for kc in range(16)])
        if blk == 0:
            P.dma("sp", T.Ah[:], dr["xres_h"].rearrange("(kc p) t -> p kc t", p=128),
                  writes=[halo.ka(kc) for kc in range(16)])
            P.dma("sp", T.Bh[:], dr["mix_h"].rearrange("(kc p) t -> p kc t", p=128),
                  reads=["mixG"], writes=[halo.kb(kc) for kc in range(16)])

        def ev_res(cg, j, b):
            P.op("dve", lambda e: e.scalar_tensor_tensor(out=cg.a(j), in0=cg.a(j), scalar=ALPHA,
                                                          in1=T.ps[b][:, :cg.n], op0=ALU.mult, op1=ALU.add),
                 reads=[cg.ka(j), "tps%d" % b], writes=[cg.ka(j)])

        srcB = lambda cg, kc: cg.b(kc)
        ksrcB = lambda cg, kc: cg.kb(kc)
        def ev_res_ln(cg, j, b):
            ev_res(cg, j, b)
            ln_pre(T, cg, j, True)

        dense(T, dr, "w_out", 16, D, cgs, srcB, ksrcB, ev_res_ln)
        for cg in cgs:
            layernorm(T, cg, 0, early=True)

        def ev_q(cg, j, b):
            P.op("act", lambda e: e.activation(out=cg.g(j), in_=T.ps[b][:, :cg.n], func=AF.Copy,
                                                scale=float(XDH) ** -0.5),
                 reads=["tps%d" % b], writes=[cg.kg(j)])

        dense(T, dr, "xq", 16, D, cgs, srcB, ksrcB, ev_q)
        def xattn(cg):
            n = cg.n

            def scores(h):
                hp = h % 2
                for mc in range(2):
                    b = T.bank()
                    for dc in range(4):
                        j = h * 4 + dc
                        P.op("pe", lambda e, j=j, mc=mc, b=b, dc=dc: e.matmul(
                            T.ps[b][:, :n], lhsT=T.KT[:, j, mc * 128:(mc + 1) * 128], rhs=cg.g(j),
                            start=(dc == 0), stop=(dc == 3)),
                            reads=["tKT%d" % j, cg.kg(j)], writes=["tps%d" % b])
                    P.op("act", lambda e, mc=mc, b=b, hp=hp: e.activation(out=T.PT[hp][:, mc, :n],
                                                                          in_=T.ps[b][:, :n], func=AF.Exp),
                         reads=["tps%d" % b], writes=["tPT%d_%d" % (hp, mc)])

            def pv(h):
                hp = h % 2
                bl = 6 + hp
                for mc in range(2):
                    P.op("pe", lambda e, mc=mc, hp=hp, bl=bl: e.matmul(T.ps[bl][:, :n], lhsT=T.ones[:],
                                                                      rhs=T.PT[hp][:, mc, :n],
                                                                      start=(mc == 0), stop=(mc == 1)),
                         reads=["tones", "tPT%d_%d" % (hp, mc)], writes=["tps%d" % bl])
                P.op("dve", lambda e, hp=hp, bl=bl: e.reciprocal(out=T.rl[hp][:, :n], in_=T.ps[bl][:, :n]),
                     reads=["tps%d" % bl], writes=["trl%d" % hp])
                for ec in range(4):
                    j = h * 4 + ec
                    b = T.bank()
                    for mc in range(2):
                        P.op("pe", lambda e, mc=mc, j=j, b=b, hp=hp: e.matmul(
                            T.ps[b][:, :n], lhsT=T.V[:, mc, j * 128:(j + 1) * 128], rhs=T.PT[hp][:, mc, :n],
                            start=(mc == 0), stop=(mc == 1)),
                            reads=["tV%d_%d" % (mc, j // 4), "tPT%d_%d" % (hp, mc)], writes=["tps%d" % b])
                    P.op("dve", lambda e, j=j, b=b, hp=hp: e.tensor_tensor(out=cg.b(j), in0=T.ps[b][:, :n],
                                                                           in1=T.rl[hp][:, :n], op=ALU.mult),
                         reads=["tps%d" % b, "trl%d" % hp], writes=[cg.kb(j)])

            scores(0)
            for h in range(XH):
                if h + 1 < XH:
                    scores(h + 1)
                pv(h)

        for cg_ in cgs:
            xattn(cg_)
        dense(T, dr, "xo", 16, D, cgs, srcB, ksrcB, ev_res_ln)
        for cg in cgs:
            layernorm(T, cg, 1, early=True)

        def conv(cg, fc, b):
            n = cg.n
            if cg is halo:
                P.op("dve", lambda e: e.tensor_scalar(out=T.hal[:, fc, :], in0=T.ps[b][:, :2],
                                                       scalar1=T.flag[:, 0:1], scalar2=None, op0=ALU.mult),
                     reads=["tps%d" % b, "tflag"], writes=["thal%d" % fc])
                return None
            hi = T.hbn % 4
            ci = T.hbn % 4
            T.hbn += 1
            hb, hk = T.hb[hi], "thb%d" % hi
            ctb, ck = T.ct[ci], "tct%d" % ci
            P.op("pool", lambda e: e.tensor_copy(out=hb[:, 0:2], in_=T.hal[:, fc, :]),
                 reads=["thal%d" % fc], writes=[hk])
            P.op("act", lambda e: e.activation(out=hb[:, 2:2 + n], in_=T.ps[b][:, :n], func=AF.Copy),
                 reads=["tps%d" % b, hk], writes=[hk])
            P.op("dve", lambda e: e.tensor_scalar(out=ctb[:, :n], in0=hb[:, 2:2 + n], scalar1=T.cw[:, fc, 2:3],
                                                   scalar2=T.cb[:, fc:fc + 1], op0=ALU.mult, op1=ALU.add),
                 reads=[hk, "tcw", "tcb"], writes=[ck])
            P.op("dve", lambda e: e.scalar_tensor_tensor(out=ctb[:, :n], in0=hb[:, 1:1 + n],
                                                          scalar=T.cw[:, fc, 1:2], in1=ctb[:, :n],
                                                          op0=ALU.mult, op1=ALU.add),
                 reads=[hk, "tcw", ck], writes=[ck])
            P.op("dve", lambda e: e.scalar_tensor_tensor(out=ctb[:, :n], in0=hb[:, 0:n],
                                                          scalar=T.cw[:, fc, 0:1], in1=ctb[:, :n],
                                                          op0=ALU.mult, op1=ALU.add),
                 reads=[hk, "tcw", ck], writes=[ck])
            P.op("dve", lambda e: e.tensor_copy(out=T.hal[:, fc, :], in_=hb[:, n:n + 2]),
                 reads=[hk], writes=["thal%d" % fc])
            return ctb, ck

        silu_q = []

        def silu_flush(keep):
            while len(silu_q) > keep:
                cg, j, ctb, ck = silu_q.pop(0)
                P.op("act", lambda e, cg=cg, j=j, ctb=ctb: e.activation(out=cg.g(j), in_=ctb[:, :cg.n], func=AF.Silu),
                     reads=[ck], writes=[cg.kg(j)])

        def ev_a(cg, j, b):
            r = conv(cg, j, b)
            if r is None:
                return
            ctb, ck = r
            silu_q.append((cg, j, ctb, ck))
            silu_flush(2)

        def ev_b(cg, j, b):
            r = conv(cg, 44 + j, b)
            if r is None:
                return
            ctb, ck = r
            P.op("dve", lambda e: e.tensor_tensor(out=cg.g(j), in0=cg.g(j), in1=ctb[:, :cg.n], op=ALU.mult),
                 reads=[ck, cg.kg(j)], writes=[cg.kg(j)])

        if blk == 0:
            dense(T, dr, "up", 16, DFF, [halo], srcB, ksrcB, ev_a, col0=0)
            dense(T, dr, "up", 16, DFF, [halo], srcB, ksrcB, ev_b, col0=DFF)
        dense(T, dr, "up", 16, DFF, [main], srcB, ksrcB, ev_a, col0=0)
        silu_flush(0)
        dense(T, dr, "up", 16, DFF, [main], srcB, ksrcB, ev_b, col0=DFF)

        cg = main
        for ct in range(4):
            for kg in range(4):
                s = T.wload(wview(dr, "down", kg * 1408, 11, ct * 512), 11)
                for jc in range(4):
                    b = 4 + jc
                    for kc in range(11):
                        kk = kg * 11 + kc
                        P.op("pe", lambda e, s=s, kc=kc, jc=jc, b=b, kk=kk: e.matmul(
                            T.ps[b][:, :], lhsT=T.W[s][:, kc, jc * 128:(jc + 1) * 128], rhs=cg.g(kk),
                            start=(kk == 0), stop=(kk == 43)),
                            reads=["tw%d" % s, cg.kg(kk)], writes=["tps%d" % b])
            for jc in range(4):
                ev_res(cg, ct * 4 + jc, 4 + jc)
        layernorm(T, cg, 2)
        P.dma("sp", dr["xout"][:, t0:t0 + 512].rearrange("(kc p) t -> p kc t", p=128), T.A[:],
              reads=[main.ka(kc) for kc in range(16)], writes=["xout%d" % blk])
        final_keys.append("xout%d" % blk)
        if "xoutb" in dr:
            P.dma("sp", dr["xoutb"][blk].rearrange("(kc p) t -> p kc t", p=128), T.B[:],
                  reads=[main.kb(kc) for kc in range(16)], writes=["xoutb%d" % blk])
            final_keys.append("xoutb%d" % blk)
            P.coll("AllGather", [dr["xoutb"][blk]], [dr["xG_out"][blk]], dr["groups"],
                   reads=["xoutb%d" % blk], writes=["xG%d" % blk])
        if "xouth" in dr and blk == NB - 1:
            P.dma("sp", dr["xouth"].rearrange("(kc p) t -> p kc t", p=128), T.A[:, :, 510:512],
                  reads=[main.ka(kc) for kc in range(16)], writes=["xouth"])
    return T


import ml_dtypes
from concourse.bass_utils import run_bass_kernel_spmd

SEQ = 4096
NB_ = 4
HALF = SEQ // 2
GROUPS = [[0, 1], [2, 3], [4, 5], [6, 7]]


def m_consts():
    sel = np.zeros((4, 4, 128), np.float32)
    for h in range(4):
        sel[h, h, :] = 1.0
    id4 = np.eye(4, dtype=np.float32)
    s = np.arange(128)[:, None]
    t = np.arange(128)[None, :]
    nmask = np.where(s <= t, 0.0, -1e30).astype(np.float32)
    rst = np.ones((128, 512), np.float32)
    rst[:, ::64] = 0.0
    s = np.arange(64)[:, None]
    t = np.arange(64)[None, :]
    tri = (s <= t).astype(np.int32)
    identb = np.eye(128, dtype=np.float32).astype(ml_dtypes.bfloat16)
    return dict(sel=sel, id4=id4, nmask=nmask, rst=rst, tri=tri, identb=identb)


def m_weights(inputs, l, g):
    w_in_l = np.asarray(inputs["w_in"][l], np.float32)
    hs = slice(4 * g * 128, (4 * g + 4) * 128)
    fq = w_in_l[:, 0:1024][:, hs]
    fk = w_in_l[:, 1024:2048][:, hs]
    fv = w_in_l[:, 2048:3072][:, hs]
    ff = w_in_l[:, 3072:3080][:, 4 * g:4 * g + 4]
    o = 3080
    hq = w_in_l[:, o:o + 1024][:, hs]
    hf = w_in_l[:, o + 1024:o + 2048][:, hs]
    hi = w_in_l[:, o + 2048:o + 3072][:, hs]
    hg = w_in_l[:, o + 3072:o + 4096][:, hs]
    c = np.ascontiguousarray
    return {
        "wqkv%d" % l: c(np.concatenate([fq, fk, fv], axis=1)),
        "wff%d" % l: c(ff),
        "fbias%d" % l: c(np.asarray(inputs["fox_f_bias"][l], np.float32)[4 * g:4 * g + 4].reshape(4, 1)),
        "wh%d" % l: c(np.concatenate([hq, hf, hi, hg], axis=1)),
        "normw%d" % l: c(np.asarray(inputs["hgrn_norm_w"][l], np.float32)[hs].reshape(4, 128).T),
        "lbl": c(np.asarray(inputs["hgrn_lb_logits"], np.float32)[:, hs].reshape(2, 4, 128).transpose(2, 0, 1)),
    }


def t_weights(inputs, l):
    f = lambda a: np.ascontiguousarray(np.asarray(a, dtype=np.float32))
    pk = lambda v: f(np.asarray(v).reshape(-1, 128).T)
    lnp = np.stack([pk(inputs[k][l]) for k in ("ln1_g", "ln1_b", "ln2_g", "ln2_b", "ln3_g", "ln3_b")], axis=1)
    cw = np.asarray(inputs["conv_w"][l]).reshape(3, 88, 128).transpose(2, 1, 0)
    wo = np.asarray(inputs["w_out"][l], np.float32).reshape(16, 128, 2048)
    perm = []
    for c_ in range(4):
        for r in range(2):
            for u in range(2):
                lc = c_ * 2 + u
                perm.append((lc // 4) * 8 + 4 * r + (lc % 4))
    wo = wo[perm].reshape(2048, 2048)
    d = dict(lnp=f(lnp), cw=f(cw), cb=pk(inputs["conv_b"][l]),
             w_out=f(wo), xq=f(inputs["xq_w"][l]), xk=f(inputs["xk_w"][l]),
             xv=f(inputs["xv_w"][l]), xo=f(inputs["xo_w"][l]), up=f(inputs["ffn_up"][l]),
             down=f(inputs["ffn_down"][l]))
    return {"%s%d" % (k, l): v for k, v in d.items()}


def build_fused(S=SEQ, GROUPS=GROUPS):
    nc = bass.Bass("TRN2", target_bir_lowering=False)
    st0 = ExitStack()
    with st0:
        def din(name, shape, dt=F32):
            return nc.dram_tensor(name, shape, dt, kind="ExternalInput").ap()

        def dint(name, shape, dt):
            return nc.dram_tensor(name, shape, dt).ap()
        NT = S // 2
        cst = dict(sel=din("sel", [4, 4, 128]), id4=din("id4", [4, 4]), nmask=din("nmask", [128, 128]),
                   rst=din("rst", [128, 512]), tri=din("tri", [64, 64], I32),
                   identb=din("identb", [128, 128], BF16), lbl=din("lbl", [128, 2, 4]))
        xT0 = din("xT0", [2048, S])
        xres0 = din("xres0", [2048, NT])
        xres_h0 = din("xres_h0", [2048, 2])
        flag = din("flag", [128, 1])
        memT = din("memT", [2048, 256])
        mw, tw = [], []
        for l in range(2):
            mw.append(dict(wqkv=din("wqkv%d" % l, [2048, 1536]), wff=din("wff%d" % l, [2048, 4]),
                           fbias=din("fbias%d" % l, [4, 1]), wh=din("wh%d" % l, [2048, 2048]),
                           normw=din("normw%d" % l, [128, 4])))
            tw.append(dict(lnp=din("lnp%d" % l, [128, 6, 16]), cw=din("cw%d" % l, [128, 88, 3]),
                           cb=din("cb%d" % l, [128, 88]), w_out=din("w_out%d" % l, [2048, 2048]),
                           xq=din("xq%d" % l, [2048, 2048]), xk=din("xk%d" % l, [2048, 2048]),
                           xv=din("xv%d" % l, [2048, 2048]), xo=din("xo%d" % l, [2048, 2048]),
                           up=din("up%d" % l, [2048, 11264]), down=din("down%d" % l, [5632, 2048])))
        for l in range(2):
            for nm, nt_, w_ in (("xk", 4, 8192), ("xv", 4, 8192), ("w_out", 4, 8192), ("xq", 4, 8192),
                                ("xo", 4, 8192), ("up", 22, 8192), ("down", 16, 5632)):
                tw[l][nm + "_b"] = dint("%s_b%d" % (nm, l), [nt_, 128, w_], BF16)
        xout = nc.dram_tensor("xout", [2048, NT], F32, kind="ExternalOutput").ap()
        mixA = [dint("mixA%d" % l, [1024, S], BF16) for l in range(2)]
        mixG = [dint("mixG%d" % l, [2048, S], BF16) for l in range(2)]
        xI = dint("xI", [2048, NT], F32)
        xB = dint("xB", [NT // 512, 2048, 512], BF16)
        xG = dint("xG", [NT // 512, 4096, 512], BF16)
        xH = dint("xH", [2048, 2], F32)
        xHG = dint("xHG", [4096, 2], F32)
        P = Prog(nc, st0)
        fk = []
        for l in range(2):
            dm = dict(cst)
            dm.update(mw[l])
            dm["mix"] = mixA[l]
            if l == 0:
                dm["xT"] = xT0
            else:
                P.coll("AllGather", [xH], [xHG], GROUPS, writes=["xHG"])
                dm["xG"] = xG
            bg = conv_tasks(tw[l])
            nblk = S // 512
            nbg = -(-len(bg) // (2 * nblk))
            emit_fox(nc, P, S, dm, fk, x_f32=(l == 0), bgq=bg, nbg=nbg)
            for c_ in range(2):
                P.coll("AllGather", [mixA[l][c_ * 256:(c_ + 1) * 256, :]], [mixG[l][c_ * 512:(c_ + 1) * 512, :]],
                       GROUPS, writes=["mixG"])
            emit_hgrn(nc, P, S, l, dm, fk, x_f32=(l == 0), bgq=bg, nbg=nbg)
            while bg:
                o_, i_ = bg.pop(0)
                P.dma("pool", o_, i_)
            for c_ in range(2, 4):
                P.coll("AllGather", [mixA[l][c_ * 256:(c_ + 1) * 256, :]], [mixG[l][c_ * 512:(c_ + 1) * 512, :]],
                       GROUPS, writes=["mixG"])
            dt = dict(tw[l])
            dt.update(flag=flag, memT=memT, mixG=mixG[l], mix_h=mixG[l][:, NT - 2:NT])
            if l == 0:
                dt.update(xres=xres0, xres_h=xres_h0, xout=xI, xoutb=xB, xouth=xH, xG_out=xG, groups=GROUPS)
            else:
                dt.update(xres=xI, xres_h=xHG[0:2048, :], xout=xout)
            emit_T(nc, P, NT, dt, fk, final=(l == 1))
    return nc


def kernel(**inputs):
    c = np.ascontiguousarray
    x = np.asarray(inputs["x"], np.float32)
    mem = np.asarray(inputs["mem"], np.float32)
    consts = m_consts()
    tw = {}
    for l in range(2):
        tw.update(t_weights(inputs, l))
    mwg = []
    for g in range(2):
        d = {}
        for l in range(2):
            d.update(m_weights(inputs, l, g))
        mwg.append(d)
    in_maps = []
    for b in range(NB_):
        xT = c(x[b].T)
        mT = c(mem[b].T)
        for g in range(2):
            d = dict(tw)
            d.update(mwg[g])
            d.update(consts)
            lo = g * HALF
            d["xT0"] = xT
            d["xres0"] = c(xT[:, lo:lo + HALF])
            d["xres_h0"] = c(xT[:, lo - 2:lo]) if g == 1 else np.zeros((2048, 2), np.float32)
            d["flag"] = np.full((128, 1), float(g), np.float32)
            d["memT"] = mT
            in_maps.append(d)
    nc = build_fused()
    res = run_bass_kernel_spmd(nc, in_maps, core_ids=list(range(8))).results
    out = np.empty((NB_, SEQ, 2048), np.float32)
    for b in range(NB_):
        for g in range(2):
            out[b, g * HALF:(g + 1) * HALF, :] = np.asarray(res[2 * b + g]["xout"]).T
    return out
```

```python
import numpy as np
import concourse.bass as bass
import concourse.mybir as mybir
from contextlib import ExitStack

F32 = mybir.dt.float32
BF16 = mybir.dt.bfloat16
I32 = mybir.dt.int32
AF = mybir.ActivationFunctionType
ALU = mybir.AluOpType

ENGS = ("pe", "act", "dve", "pool", "sp")
DMA_RING = 8


class _Op:
    __slots__ = ("fn", "deps", "kind", "signal", "count", "dslot", "dcount")

    def __init__(self, fn, deps, kind):
        self.fn = fn
        self.deps = deps
        self.kind = kind
        self.signal = False
        self.count = 0
        self.dslot = None
        self.dcount = 0


class Prog:
    def __init__(self, nc, st):
        self.nc = nc
        self.csem = {e: st.enter_context(nc.semaphore("c_" + e)) for e in ENGS}
        self.dsem = {e: [st.enter_context(nc.semaphore("d_%s_%d" % (e, i))) for i in range(DMA_RING)]
                     for e in ("sp", "pool")}
        self.ksem = st.enter_context(nc.semaphore("k_coll"))
        self.ncoll = 0
        self.cbase = {e: 0 for e in ENGS}
        self.ndma = {e: 0 for e in ENGS}
        self.barrier = []
        self.total_ops = {e: 0 for e in ENGS}
        self.total_waits = {e: 0 for e in ENGS}
        self._reset()

    def _reset(self):
        self.ops = {e: [] for e in ENGS}
        self.last_writer = {}
        self.readers = {}

    def _add(self, eng, fn, reads, writes, kind):
        deps = set()
        for k in reads:
            lw = self.last_writer.get(k)
            if lw is not None:
                deps.add(lw)
        for k in writes:
            lw = self.last_writer.get(k)
            if lw is not None:
                deps.add(lw)
            for r in self.readers.get(k, ()):
                deps.add(r)
        idx = len(self.ops[eng])
        if eng == "pe":
            deps = {d for d in deps if d[0] != "pe"}
        op = _Op(fn, deps, kind)
        if kind == "k":
            self.ncoll += 1
            op.dcount = self.ncoll
        if kind == "d":
            n = self.ndma[eng]
            self.ndma[eng] = n + 1
            op.dslot = n % DMA_RING
            op.dcount = (n // DMA_RING + 1) * 16
        self.ops[eng].append(op)
        me = (eng, idx)
        for k in writes:
            self.last_writer[k] = me
            self.readers[k] = []
        for k in reads:
            self.readers.setdefault(k, []).append(me)
        return me

    def op(self, eng, fn, reads=(), writes=()):
        return self._add(eng, fn, reads, writes, "c")

    def dma(self, eng, out, in_, reads=(), writes=(), **kw):
        return self._add(eng, lambda e: e.dma_start(out=out, in_=in_, **kw), reads, writes, "d")

    def flush(self, final=False):
        nc = self.nc
        for e in ENGS:
            for op in self.ops[e]:
                best = {}
                keep = set()
                for (de, di) in op.deps:
                    if self.ops[de][di].kind == "c":
                        if best.get(de, -1) < di:
                            best[de] = di
                    else:
                        keep.add((de, di))
                for de, di in best.items():
                    keep.add((de, di))
                    self.ops[de][di].signal = True
                op.deps = keep
            for op in reversed(self.ops[e]):
                if op.kind == "c":
                    op.signal = True
                    break
        for e in ENGS:
            c = self.cbase[e]
            for op in self.ops[e]:
                if op.kind == "c" and op.signal:
                    c += 1
                    op.count = c
            self.cbase[e] = c
        nxt = []
        for e in ENGS:
            if self.cbase[e] > 0:
                nxt.append((self.csem[e], self.cbase[e]))
        if self.ncoll > 0:
            nxt.append((self.ksem, self.ncoll))
        for e in self.dsem:
            n = self.ndma[e]
            for slot in range(DMA_RING):
                if n > slot:
                    cnt = ((n - 1 - slot) // DMA_RING + 1) * 16
                    nxt.append((self.dsem[e][slot], cnt))
        prev_barrier = self.barrier

        def resolve(dep):
            de, di = dep
            dop = self.ops[de][di]
            if dop.kind == "c":
                return self.csem[de], dop.count
            if dop.kind == "k":
                return self.ksem, dop.dcount
            return self.dsem[de][dop.dslot], dop.dcount

        def body(e, eng):
            waited = {}
            nw = 0

            def wait(sem, val):
                nonlocal nw
                key = id(sem)
                if waited.get(key, 0) < val:
                    eng.wait_ge(sem, val)
                    waited[key] = val
                    nw += 1

            for (s, v) in prev_barrier:
                wait(s, v)
            for op in self.ops[e]:
                for dep in sorted(op.deps):
                    s, v = resolve(dep)
                    wait(s, v)
                if op.kind == "k":
                    ins = op.fn(eng)
                    ins.then_inc(self.ksem, 1)
                elif op.kind == "d":
                    if op.dcount > 16:
                        wait(self.dsem[e][op.dslot], op.dcount - 16)
                    ins = op.fn(eng)
                    ins.then_inc(self.dsem[e][op.dslot], 16)
                else:
                    ins = op.fn(eng)
                    if op.signal:
                        ins.then_inc(self.csem[e], 1)
            if final and e == "sp":
                for (s, v) in nxt:
                    wait(s, v)
            self.total_ops[e] += len(self.ops[e])
            self.total_waits[e] += nw

        with nc.Block() as block:
            @block.tensor
            def _(eng):
                body("pe", eng)

            @block.scalar
            def _(eng):
                body("act", eng)

            @block.vector
            def _(eng):
                body("dve", eng)

            @block.gpsimd
            def _(eng):
                body("pool", eng)

            @block.sync
            def _(eng):
                body("sp", eng)
        self.barrier = nxt
        self._reset()


def _coll(self, kind, ins, outs, groups, reads=(), writes=()):
    return self._add("pool", lambda e: e.collective_compute(kind, ALU.bypass, replica_groups=groups,
                                                            ins=[a.opt() for a in ins], outs=[a.opt() for a in outs]),
                     reads, writes, "k")


Prog.coll = _coll


_UIDC = [0]


def uid():
    _UIDC[0] += 1
    return "_u%d" % _UIDC[0]


D = 2048
NH = 4
DH = 128
CH = 64
RMS_EPS = 1e-6
MIN_FORGET = 1e-6


def _xload(P, X, xs, dr, j, x_f32):
    if "xG" in dr:
        nb_half = dr["xG"].shape[0]
        r, blk = j // nb_half, j % nb_half
        view = dr["xG"][blk][r * 2048:(r + 1) * 2048, :].rearrange("(kc p) t -> p kc t", p=128)
        P.dma("sp", X[xs][:], view, writes=["mx%d" % xs])
        return
    view = dr["xT"][:, j * 512:(j + 1) * 512].rearrange("(kc p) t -> p kc t", p=128)
    P.dma("pool" if x_f32 else "sp", X[xs][:], view, writes=["mx%d" % xs])


def emit_fox(nc, P, S, dr, final_keys, x_f32=True, final=False, bgq=None, nbg=0):
    NB = S // 512
    with ExitStack() as st:
        sfx = uid()
        sb = lambda name, shape, dt: st.enter_context(nc.sbuf_tensor(name + sfx, shape, dt))
        W = [sb("mw%d" % i, [128, 16, 512], BF16) for i in range(3)]
        wff = sb("mwff", [128, 16, 4], BF16)
        X = [sb("mx%d" % i, [128, 16, 512], BF16) for i in range(2)]
        kT = sb("mkT", [128, NH, S], BF16)
        vt = sb("mvt", [128, S // 128, 512], BF16)
        qT = sb("mqT", [128, NH, 512], BF16)
        cbc = sb("mcbc", [128, NH, 512], F32)
        cbm = sb("mcbm", [128, NH, 4, 128], F32)
        nct = sb("mnct", [128, S // 128, NH], F32)
        NTMP, NPT = 4, 4
        tmp = [sb("mtmp%d" % i, [128, 512], F32) for i in range(NTMP)]
        PT = [sb("mPT%d" % i, [128, 512], BF16) for i in range(NPT)]
        ost = [sb("most%d" % i, [128, 512], BF16) for i in range(2)]
        rl = sb("mrl", [128, 512], F32)
        e1 = sb("me1", [4, 512], F32)
        lf = sb("mlf", [4, 512], F32)
        cT = sb("mcT", [4, 512], F32)
        cprev = sb("mcprev", [4, 1], F32)
        ones4 = sb("mones4", [4, 512], F32)
        nfb = sb("mnfb", [4, 1], F32)
        sel = sb("msel", [4, NH, 128], F32)
        id4 = sb("mid4", [4, 4], F32)
        nmask = sb("mnmask", [128, 128], F32)
        ones = sb("mones", [128, 128], BF16)
        ps = [st.enter_context(nc.psum_tensor("mps%d" % i + sfx, [128, 512], F32)) for i in range(8)]
        pn = [0]

        def bank():
            i = pn[0] % 4
            pn[0] += 1
            return i

        P.op("dve", lambda e: e.memset(ones[:], 1.0), writes=["mones"])
        P.op("dve", lambda e: e.memset(ones4[:], 1.0), writes=["mones4"])
        P.op("dve", lambda e: e.memset(cprev[:], 0.0), writes=["mcprev"])
        P.dma("sp", sel[:], dr["sel"], writes=["msel"])
        P.dma("sp", id4[:], dr["id4"], writes=["mid4"])
        P.dma("sp", nmask[:], dr["nmask"], writes=["mnmask"])
        P.dma("sp", nfb[:], dr["fbias"], writes=["mnfb"])
        P.op("dve", lambda e: e.tensor_scalar(out=nfb[:], in0=nfb[:], scalar1=-1.0, scalar2=None, op0=ALU.mult),
             reads=["mnfb"], writes=["mnfb"])
        _xload(P, X, 0, dr, 0, x_f32)
        for i in range(3):
            P.dma("pool", W[i][:], dr["wqkv"][:, i * 512:(i + 1) * 512].rearrange("(kc p) c -> p kc c", p=128),
                  writes=["mw%d" % i])
        P.dma("pool", wff[:], dr["wff"].rearrange("(kc p) c -> p kc c", p=128), writes=["mwff"])

        tn = [0]
        pn3 = [0]
        for j in range(NB):
            xs = j % 2
            xk = "mx%d" % xs
            if j + 1 < NB:
                _xload(P, X, (j + 1) % 2, dr, j + 1, x_f32)
            for _ in range(nbg):
                if bgq:
                    o_, i_ = bgq.pop(0)
                    P.dma("pool", o_, i_)
            b = bank()
            for kc in range(16):
                P.op("pe", lambda e, kc=kc, b=b, xs=xs: e.matmul(
                    ps[b][0:4, :], lhsT=wff[:, kc, :], rhs=X[xs][:, kc, :],
                    start=(kc == 0), stop=(kc == 15)), reads=["mwff", xk], writes=["mps%d" % b])
            P.op("act", lambda e, b=b: e.activation(out=e1[:], in_=ps[b][0:4, :], func=AF.Exp, scale=-1.0,
                                                     bias=nfb[:, 0:1]),
                 reads=["mps%d" % b, "mnfb"], writes=["me1"])
            P.op("act", lambda e: e.activation(out=lf[:], in_=e1[:], func=AF.Ln, bias=1.0),
                 reads=["me1"], writes=["mlf"])
            P.op("dve", lambda e: e.tensor_scalar(out=lf[:], in0=lf[:], scalar1=-1.0, scalar2=None, op0=ALU.mult),
                 reads=["mlf"], writes=["mlf"])
            P.op("dve", lambda e: e.tensor_tensor_scan(out=cT[:], data0=ones4[:], data1=lf[:], initial=cprev[:, 0:1],
                                                        op0=ALU.mult, op1=ALU.add),
                 reads=["mones4", "mlf", "mcprev"], writes=["mcT"])
            P.op("dve", lambda e: e.tensor_copy(out=cprev[:], in_=cT[:, 511:512]),
                 reads=["mcT"], writes=["mcprev"])
            for h in range(NH):
                b = bank()
                for kc in range(16):
                    P.op("pe", lambda e, kc=kc, h=h, b=b, xs=xs: e.matmul(
                        ps[b][:, :], lhsT=W[0][:, kc, h * 128:(h + 1) * 128], rhs=X[xs][:, kc, :],
                        start=(kc == 0), stop=(kc == 15)), reads=["mw0", xk], writes=["mps%d" % b])
                P.op("act", lambda e, h=h, b=b: e.activation(out=qT[:, h, :], in_=ps[b][:, :], func=AF.Copy,
                                                              scale=float(DH) ** -0.5),
                     reads=["mps%d" % b], writes=["mq%d" % h])
            for h in range(NH):
                b = bank()
                for kc in range(16):
                    P.op("pe", lambda e, kc=kc, h=h, b=b, xs=xs: e.matmul(
                        ps[b][:, :], lhsT=W[1][:, kc, h * 128:(h + 1) * 128], rhs=X[xs][:, kc, :],
                        start=(kc == 0), stop=(kc == 15)), reads=["mw1", xk], writes=["mps%d" % b])
                P.op("act", lambda e, h=h, b=b, j=j: e.activation(out=kT[:, h, j * 512:(j + 1) * 512],
                                                                  in_=ps[b][:, :], func=AF.Copy),
                     reads=["mps%d" % b], writes=["mk%d_%d" % (h, j)])
            for s4 in range(4):
                b = bank()
                for kc in range(16):
                    P.op("pe", lambda e, kc=kc, s4=s4, b=b, xs=xs: e.matmul(
                        ps[b][:, :], lhsT=X[xs][:, kc, s4 * 128:(s4 + 1) * 128], rhs=W[2][:, kc, :],
                        start=(kc == 0), stop=(kc == 15)), reads=["mw2", xk], writes=["mps%d" % b])
                P.op("dve", lambda e, s4=s4, b=b, j=j: e.tensor_copy(out=vt[:, 4 * j + s4, :], in_=ps[b][:, :]),
                     reads=["mps%d" % b], writes=["mv%d" % (4 * j + s4)])
            for s4 in range(4):
                b = bank()
                P.op("pe", lambda e, s4=s4, b=b: e.matmul(ps[b][:, 0:4], lhsT=cT[:, s4 * 128:(s4 + 1) * 128],
                                                          rhs=id4[:], start=True, stop=True),
                     reads=["mcT", "mid4"], writes=["mps%d" % b])
                P.op("dve", lambda e, s4=s4, b=b, j=j: e.tensor_scalar(out=nct[:, 4 * j + s4, :], in0=ps[b][:, 0:4],
                                                                        scalar1=-1.0, scalar2=None, op0=ALU.mult),
                     reads=["mps%d" % b], writes=["mnc%d" % (4 * j + s4)])
            for h in range(NH):
                b = bank()
                P.op("pe", lambda e, h=h, b=b: e.matmul(ps[b][:, :], lhsT=sel[:, h, :], rhs=cT[:],
                                                        start=True, stop=True),
                     reads=["mcT", "msel"], writes=["mps%d" % b])
                P.op("act", lambda e, h=h, b=b: e.activation(out=cbc[:, h, :], in_=ps[b][:, :], func=AF.Copy),
                     reads=["mps%d" % b], writes=["mcbc%d" % h])
                for r in range(4):
                    P.op("pool", lambda e, h=h, r=r: e.tensor_tensor(out=cbm[:, h, r, :],
                                                                     in0=cbc[:, h, r * 128:(r + 1) * 128],
                                                                     in1=nmask[:], op=ALU.add),
                         reads=["mcbc%d" % h, "mnmask"], writes=["mcbm%d_%d" % (h, r)])
            LOOK = 2
            for h in range(NH):
                bo, bl = 4 + h % 2, 6 + h % 2
                last = 4 * j + 3
                pend = {}

                def stage_a(i, h=h, j=j):
                    r = i - 4 * j
                    c0 = 128 * r if r >= 0 else 0
                    b = bank()
                    P.op("pe", lambda e, h=h, i=i, c0=c0, b=b: e.matmul(
                        ps[b][:, c0:512], lhsT=kT[:, h, i * 128:(i + 1) * 128], rhs=qT[:, h, c0:512],
                        start=True, stop=True),
                        reads=["mk%d_%d" % (h, i // 4), "mq%d" % h], writes=["mps%d" % b])
                    ti = tn[0] % NTMP
                    tn[0] += 1
                    tk = "mtmp%d" % ti
                    if r >= 0:
                        P.op("dve", lambda e, h=h, r=r, c0=c0, b=b, ti=ti: e.tensor_tensor(
                            out=tmp[ti][:, c0:c0 + 128], in0=ps[b][:, c0:c0 + 128], in1=cbm[:, h, r, :], op=ALU.add),
                            reads=["mps%d" % b, "mcbm%d_%d" % (h, r)], writes=[tk])
                        if c0 + 128 < 512:
                            P.op("dve", lambda e, h=h, c0=c0, b=b, ti=ti: e.tensor_tensor(
                                out=tmp[ti][:, c0 + 128:512], in0=ps[b][:, c0 + 128:512],
                                in1=cbc[:, h, c0 + 128:512], op=ALU.add),
                                reads=["mps%d" % b, "mcbc%d" % h, tk], writes=[tk])
                    else:
                        P.op("dve", lambda e, h=h, b=b, ti=ti: e.tensor_tensor(
                            out=tmp[ti][:, :], in0=ps[b][:, :], in1=cbc[:, h, :], op=ALU.add),
                            reads=["mps%d" % b, "mcbc%d" % h], writes=[tk])
                    pi = pn3[0] % NPT
                    pn3[0] += 1
                    pk = "mPT%d" % pi
                    P.op("act", lambda e, h=h, i=i, c0=c0, ti=ti, pi=pi: e.activation(
                        out=PT[pi][:, c0:512], in_=tmp[ti][:, c0:512], func=AF.Exp, bias=nct[:, i, h:h + 1]),
                        reads=[tk, "mnc%d" % i], writes=[pk])
                    pend[i] = (c0, pi, pk)

                def stage_b(i, h=h, bo=bo, bl=bl, last=last):
                    c0, pi, pk = pend.pop(i)
                    P.op("pe", lambda e, h=h, i=i, c0=c0, pi=pi, bo=bo, last=last: e.matmul(
                        ps[bo][:, c0:512], lhsT=vt[:, i, h * 128:(h + 1) * 128], rhs=PT[pi][:, c0:512],
                        start=(i == 0), stop=(i == last)),
                        reads=["mv%d" % i, pk], writes=["mps%d" % bo])
                    P.op("pe", lambda e, i=i, c0=c0, pi=pi, bl=bl, last=last: e.matmul(
                        ps[bl][:, c0:512], lhsT=ones[:], rhs=PT[pi][:, c0:512],
                        start=(i == 0), stop=(i == last)),
                        reads=["mones", pk], writes=["mps%d" % bl])

                for i in range(min(LOOK, last + 1)):
                    stage_a(i)
                for i in range(last + 1):
                    if i + LOOK <= last:
                        stage_a(i + LOOK)
                    stage_b(i)
                P.op("dve", lambda e, bl=bl: e.reciprocal(out=rl[:], in_=ps[bl][:, :]),
                     reads=["mps%d" % bl], writes=["mrl"])
                oi = (j * NH + h) % 2
                P.op("dve", lambda e, bo=bo, oi=oi: e.tensor_tensor(out=ost[oi][:], in0=ps[bo][:, :], in1=rl[:],
                                                                    op=ALU.mult),
                     reads=["mps%d" % bo, "mrl"], writes=["most%d" % oi])
                ok = "mixf%d_%d" % (h, j)
                P.dma("sp", dr["mix"][h * 128:(h + 1) * 128, j * 512:(j + 1) * 512], ost[oi][:],
                      reads=["most%d" % oi], writes=[ok])
                final_keys.append(ok)
        P.flush(final=final)


def emit_hgrn(nc, P, S, layer, dr, final_keys, x_f32=True, final=False, bgq=None, nbg=0):
    NB = S // 512
    with ExitStack() as st:
        sfx = uid()
        sb = lambda name, shape, dt: st.enter_context(nc.sbuf_tensor(name + sfx, shape, dt))
        W = [sb("hw%d" % i, [128, 16, 512], BF16) for i in range(4)]
        X = [sb("hx%d" % i, [128, 16, 512], BF16) for i in range(2)]
        tsig = sb("htsig", [128, 512], F32)
        tsgm = sb("htsgm", [128, 512], F32)
        tf = sb("htf", [128, 512], F32)
        tG = sb("htG", [128, 512], F32)
        tD = sb("htD", [128, 512], F32)
        tD3 = sb("htD3", [128, 512], F32)
        tE1 = sb("htE1", [128, 512], F32)
        tE2 = sb("htE2", [128, 512], F32)
        tEG = sb("htEG", [128, 512], F32)
        tE3 = sb("htE3", [128, 512], F32)
        qtil = sb("hqtil", [128, NH, 512], BF16)
        ktil = sb("hktil", [128, NH, 512], BF16)
        qg = sb("hqg", [128, NH, 512], BF16)
        kdT = [sb("hkdT%d" % i, [128, 512], BF16) for i in range(2)]
        kdec = sb("hkdec", [64, NH, 8, 128], BF16)
        vh = sb("hvh", [64, 8, 512], BF16)
        EGl = sb("hEGl", [128, NH, 8], F32)
        sg = sb("hsg", [128, NH, 512], F32)
        oT = sb("hoT", [128, NH, 512], F32)
        osq = sb("hosq", [128, 512], BF16)
        rs = sb("hrs", [128, 512], F32)
        hn = sb("hhn", [128, 512], F32)
        ost = [sb("host%d" % i, [128, 512], BF16) for i in range(2)]
        stf = sb("hstf", [128, NH, 128], F32)
        stb = sb("hstb", [128, NH, 128], BF16)
        ATs = [sb("hATs%d" % i, [64, 64], BF16) for i in range(4)]
        rst = sb("hrst", [128, 512], F32)
        tri = sb("htri", [64, 64], I32)
        identb = sb("hidentb", [128, 128], BF16)
        ones = sb("hones", [128, 128], BF16)
        lbl = sb("hlbl", [128, 2, NH], F32)
        lbe = sb("hlbe", [128, 2, NH], F32)
        lbz = sb("hlbz", [128, NH], F32)
        lbs = sb("hlbs", [128, 2, NH], F32)
        lb = sb("hlb", [128, NH], F32)
        oml = sb("homl", [128, NH], F32)
        nw = sb("hnw", [128, NH], F32)
        ps = [st.enter_context(nc.psum_tensor("hps%d" % i + sfx, [128, 512], F32)) for i in range(7)]
        psb = st.enter_context(nc.psum_tensor("hpsb" + sfx, [128, 1024], BF16))
        pn = [0]

        def bank():
            i = pn[0] % 7
            pn[0] += 1
            return i

        P.op("dve", lambda e: e.memset(ones[:], 1.0), writes=["hones"])
        P.op("dve", lambda e: e.memset(stf[:], 0.0), writes=["hstf%d" % h for h in range(NH)])
        P.op("dve", lambda e: e.memset(stb[:], 0.0), writes=["hstb%d" % h for h in range(NH)])
        for i in range(4):
            P.op("pool", lambda e, i=i: e.memset(ATs[i][:], 0.0), writes=["hATs%d" % i])
        P.dma("sp", rst[:], dr["rst"], writes=["hrst"])
        P.dma("sp", tri[:], dr["tri"], writes=["htri"])
        P.dma("sp", identb[:], dr["identb"], writes=["hidentb"])
        P.dma("sp", lbl[:], dr["lbl"], writes=["hlbl"])
        P.dma("sp", nw[:], dr["normw"], writes=["hnw"])
        _xload(P, X, 0, dr, 0, x_f32)
        for i in range(4):
            P.dma("pool", W[i][:], dr["wh"][:, i * 512:(i + 1) * 512].rearrange("(kc p) c -> p kc c", p=128),
                  writes=["hw%d" % i])
        P.op("act", lambda e: e.activation(out=lbe[:], in_=lbl[:], func=AF.Exp), reads=["hlbl"], writes=["hlbe"])
        P.op("dve", lambda e: e.tensor_tensor(out=lbz[:], in0=lbe[:, 0, :], in1=lbe[:, 1, :], op=ALU.add),
             reads=["hlbe"], writes=["hlbz"])
        P.op("dve", lambda e: e.reciprocal(out=lbz[:], in_=lbz[:]), reads=["hlbz"], writes=["hlbz"])
        for l in range(2):
            P.op("dve", lambda e, l=l: e.tensor_tensor(out=lbs[:, l, :], in0=lbe[:, l, :], in1=lbz[:], op=ALU.mult),
                 reads=["hlbe", "hlbz"], writes=["hlbs%d" % l])
        if layer == 0:
            P.op("dve", lambda e: e.tensor_tensor(out=lb[:], in0=lbs[:, 0, :], in1=lbs[:, 0, :], op=ALU.subtract),
                 reads=["hlbs0"], writes=["hlb"])
        else:
            P.op("dve", lambda e: e.tensor_tensor(out=lb[:], in0=lbs[:, 0, :], in1=lbs[:, 1, :], op=ALU.add),
                 reads=["hlbs0", "hlbs1"], writes=["hlb"])
            P.op("dve", lambda e: e.tensor_tensor(out=lb[:], in0=lb[:], in1=lbs[:, 0, :], op=ALU.subtract),
                 reads=["hlb", "hlbs0"], writes=["hlb"])
        P.op("dve", lambda e: e.tensor_scalar(out=oml[:], in0=lb[:], scalar1=-1.0, scalar2=1.0, op0=ALU.mult,
                                               op1=ALU.add), reads=["hlb"], writes=["homl"])

        an = [0]
        for j in range(NB):
            xs = j % 2
            xk = "mx%d" % xs
            if j + 1 < NB:
                _xload(P, X, (j + 1) % 2, dr, j + 1, x_f32)
            for _ in range(nbg):
                if bgq:
                    o_, i_ = bgq.pop(0)
                    P.dma("pool", o_, i_)
            for c in range(8):
                b = bank()
                for kc in range(16):
                    P.op("pe", lambda e, kc=kc, c=c, b=b, xs=xs: e.matmul(
                        ps[b][0:64, :], lhsT=X[xs][:, kc, c * 64:(c + 1) * 64], rhs=W[2][:, kc, :],
                        start=(kc == 0), stop=(kc == 15)), reads=["hw2", xk], writes=["hps%d" % b])
                P.op("act", lambda e, c=c, b=b: e.activation(out=vh[:, c, :], in_=ps[b][0:64, :], func=AF.Copy),
                     reads=["hps%d" % b], writes=["hvh%d" % c])
            pend_x = []
            for h in range(NH):
                def proj(wi, h=h):
                    b = bank()
                    for kc in range(16):
                        P.op("pe", lambda e, kc=kc, b=b, xs=xs: e.matmul(
                            ps[b][:, :], lhsT=W[wi][:, kc, h * 128:(h + 1) * 128], rhs=X[xs][:, kc, :],
                            start=(kc == 0), stop=(kc == 15)), reads=["hw%d" % wi, xk], writes=["hps%d" % b])
                    return b
                bf = proj(1)
                kf = "hps%d" % bf
                P.op("act", lambda e, bf=bf: e.activation(out=tsig[:], in_=ps[bf][:, :], func=AF.Sigmoid),
                     reads=[kf], writes=["htsig"])
                P.op("act", lambda e, bf=bf: e.activation(out=tsgm[:], in_=ps[bf][:, :], func=AF.Sigmoid, scale=-1.0),
                     reads=[kf], writes=["htsgm"])
                P.op("dve", lambda e, h=h: e.tensor_scalar(out=tf[:], in0=tsig[:], scalar1=oml[:, h:h + 1],
                                                            scalar2=lb[:, h:h + 1], op0=ALU.mult, op1=ALU.add),
                     reads=["htsig", "homl", "hlb"], writes=["htf"])
                P.op("dve", lambda e: e.tensor_scalar(out=tf[:], in0=tf[:], scalar1=MIN_FORGET, scalar2=None,
                                                       op0=ALU.max), reads=["htf"], writes=["htf"])
                P.op("act", lambda e: e.activation(out=tf[:], in_=tf[:], func=AF.Ln), reads=["htf"], writes=["htf"])
                P.op("dve", lambda e: e.tensor_tensor_scan(out=tG[:], data0=rst[:], data1=tf[:], initial=0.0,
                                                            op0=ALU.mult, op1=ALU.add),
                     reads=["hrst", "htf"], writes=["htG"])
                Gv = tG[:].rearrange("p (c s) -> p c s", s=CH)
                P.op("dve", lambda e: e.tensor_tensor(out=tD[:].rearrange("p (c s) -> p c s", s=CH), in0=Gv,
                                                       in1=Gv[:, :, 31:32].to_broadcast([128, 8, CH]),
                                                       op=ALU.subtract), reads=["htG"], writes=["htD"])
                P.op("dve", lambda e: e.tensor_tensor(out=tD3[:].rearrange("p (c s) -> p c s", s=CH),
                                                       in0=Gv[:, :, 63:64].to_broadcast([128, 8, CH]), in1=Gv,
                                                       op=ALU.subtract), reads=["htG"], writes=["htD3"])
                P.op("act", lambda e: e.activation(out=tE1[:], in_=tD[:], func=AF.Exp), reads=["htD"], writes=["htE1"])
                P.op("act", lambda e: e.activation(out=tE2[:], in_=tD[:], func=AF.Exp, scale=-1.0),
                     reads=["htD"], writes=["htE2"])
                P.op("act", lambda e: e.activation(out=tEG[:], in_=tG[:], func=AF.Exp), reads=["htG"], writes=["htEG"])
                P.op("act", lambda e: e.activation(out=tE3[:], in_=tD3[:], func=AF.Exp),
                     reads=["htD3"], writes=["htE3"])
                P.op("pool", lambda e, h=h: e.tensor_copy(
                    out=EGl[:, h, :], in_=tEG[:].rearrange("p (c s) -> p c s", s=CH)[:, :, 63]),
                    reads=["htEG"], writes=["hEGl%d" % h])
                bq = proj(0)
                kq = "hps%d" % bq
                P.op("dve", lambda e, h=h, bq=bq: e.scalar_tensor_tensor(
                    out=qtil[:, h, :], in0=ps[bq][:, :], scalar=float(DH) ** -0.5, in1=tE1[:],
                    op0=ALU.mult, op1=ALU.mult), reads=[kq, "htE1"], writes=["hqtil%d" % h])
                P.op("dve", lambda e, h=h, bq=bq: e.scalar_tensor_tensor(
                    out=qg[:, h, :], in0=ps[bq][:, :], scalar=float(DH) ** -0.5, in1=tEG[:],
                    op0=ALU.mult, op1=ALU.mult), reads=[kq, "htEG"], writes=["hqg%d" % h])
                P.op("dve", lambda e, h=h: e.scalar_tensor_tensor(
                    out=ktil[:, h, :], in0=tsgm[:], scalar=oml[:, h:h + 1], in1=tE2[:],
                    op0=ALU.mult, op1=ALU.mult), reads=["htsgm", "homl", "htE2"], writes=["hktil%d" % h])
                P.op("dve", lambda e, h=h: e.scalar_tensor_tensor(
                    out=kdT[h % 2][:], in0=tsgm[:], scalar=oml[:, h:h + 1], in1=tE3[:],
                    op0=ALU.mult, op1=ALU.mult), reads=["htsgm", "homl", "htE3"], writes=["hkdT%d" % (h % 2)])
                bg = proj(3)
                P.op("act", lambda e, h=h, bg=bg: e.activation(out=sg[:, h, :], in_=ps[bg][:, :], func=AF.Silu),
                     reads=["hps%d" % bg], writes=["hsg%d" % h])

                def xpose(h=h):
                    kd = kdT[h % 2]
                    kk = "hkdT%d" % (h % 2)
                    for c in range(8):
                        P.op("pe", lambda e, c=c, kd=kd: e.transpose(out=psb[0:64, c * 128:(c + 1) * 128],
                                                                     in_=kd[:, c * 64:(c + 1) * 64],
                                                                     identity=identb[:]),
                             reads=[kk, "hidentb"], writes=["hpsb"])
                    P.op("act", lambda e, h=h: e.activation(out=kdec[:, h, :, :].rearrange("p c d -> p (c d)"),
                                                             in_=psb[0:64, :], func=AF.Copy),
                         reads=["hpsb"], writes=["hkdec%d" % h])
                if pend_x:
                    pend_x.pop(0)()
                pend_x.append(xpose)
            while pend_x:
                pend_x.pop(0)()
            for c in range(8):
                aks = []
                for h in range(NH):
                    b = bank()
                    P.op("pe", lambda e, c=c, h=h, b=b: e.matmul(
                        ps[b][0:64, 0:64], lhsT=ktil[:, h, c * 64:(c + 1) * 64], rhs=qtil[:, h, c * 64:(c + 1) * 64],
                        start=True, stop=True), reads=["hktil%d" % h, "hqtil%d" % h], writes=["hps%d" % b])
                    ai = an[0] % 4
                    an[0] += 1
                    ak = "hATs%d" % ai
                    P.op("dve", lambda e, b=b, ai=ai: e.copy_predicated(out=ATs[ai][:], mask=tri[:],
                                                                        data=ps[b][0:64, 0:64]),
                         reads=["hps%d" % b, "htri", ak], writes=[ak])
                    aks.append((ai, ak))
                for h in range(NH):
                    ai, ak = aks[h]
                    b2 = bank()
                    P.op("pe", lambda e, c=c, h=h, b2=b2, ai=ai: e.matmul(
                        ps[b2][:, 0:64], lhsT=vh[:, c, h * 128:(h + 1) * 128], rhs=ATs[ai][:],
                        start=True, stop=False), reads=["hvh%d" % c, ak], writes=["hps%d" % b2])
                    P.op("pe", lambda e, c=c, h=h, b2=b2: e.matmul(
                        ps[b2][:, 0:64], lhsT=stb[:, h, :], rhs=qg[:, h, c * 64:(c + 1) * 64],
                        start=False, stop=True), reads=["hstb%d" % h, "hqg%d" % h], writes=["hps%d" % b2])
                    P.op("act", lambda e, c=c, h=h, b2=b2: e.activation(out=oT[:, h, c * 64:(c + 1) * 64],
                                                                        in_=ps[b2][:, 0:64], func=AF.Copy),
                         reads=["hps%d" % b2], writes=["hoT%d_%d" % (h, c)])
                    b3 = bank()
                    P.op("pe", lambda e, c=c, h=h, b3=b3: e.matmul(
                        ps[b3][:, 0:128], lhsT=kdec[:, h, c, :], rhs=vh[:, c, h * 128:(h + 1) * 128],
                        start=True, stop=True), reads=["hkdec%d" % h, "hvh%d" % c], writes=["hps%d" % b3])
                    P.op("dve", lambda e, c=c, h=h, b3=b3: e.scalar_tensor_tensor(
                        out=stf[:, h, :], in0=stf[:, h, :], scalar=EGl[:, h, c:c + 1], in1=ps[b3][:, 0:128],
                        op0=ALU.mult, op1=ALU.add),
                        reads=["hstf%d" % h, "hEGl%d" % h, "hps%d" % b3], writes=["hstf%d" % h])
                    P.op("pool", lambda e, h=h: e.tensor_copy(out=stb[:, h, :], in_=stf[:, h, :]),
                         reads=["hstf%d" % h], writes=["hstb%d" % h])
            for h in range(NH):
                P.op("act", lambda e, h=h: e.activation(out=osq[:], in_=oT[:, h, :], func=AF.Square),
                     reads=["hoT%d_%d" % (h, c) for c in range(8)], writes=["hosq"])
                b = bank()
                P.op("pe", lambda e, b=b: e.matmul(ps[b][:, :], lhsT=ones[:], rhs=osq[:], start=True, stop=True),
                     reads=["hones", "hosq"], writes=["hps%d" % b])
                P.op("dve", lambda e, b=b: e.tensor_scalar(out=rs[:], in0=ps[b][:, :], scalar1=1.0 / DH,
                                                            scalar2=RMS_EPS, op0=ALU.mult, op1=ALU.add),
                     reads=["hps%d" % b], writes=["hrs"])
                P.op("act", lambda e: e.activation(out=rs[:], in_=rs[:], func=AF.Sqrt), reads=["hrs"], writes=["hrs"])
                P.op("dve", lambda e: e.reciprocal(out=rs[:], in_=rs[:]), reads=["hrs"], writes=["hrs"])
                P.op("dve", lambda e, h=h: e.scalar_tensor_tensor(
                    out=hn[:], in0=oT[:, h, :], scalar=nw[:, h:h + 1], in1=rs[:], op0=ALU.mult, op1=ALU.mult),
                    reads=["hoT%d_%d" % (h, c) for c in range(8)] + ["hnw", "hrs"], writes=["hhn"])
                oi = (j * NH + h) % 2
                P.op("dve", lambda e, h=h, oi=oi: e.tensor_tensor(out=ost[oi][:], in0=hn[:], in1=sg[:, h, :],
                                                                  op=ALU.mult),
                     reads=["hhn", "hsg%d" % h], writes=["host%d" % oi])
                ok = "mixh%d_%d" % (h, j)
                P.dma("sp", dr["mix"][512 + h * 128:512 + (h + 1) * 128, j * 512:(j + 1) * 512], ost[oi][:],
                      reads=["host%d" % oi], writes=[ok])
                final_keys.append(ok)
        P.flush(final=final)


D = 2048
DFF = 5632
NMEM = 256
ALPHA = 4.0 ** 0.25
LN_EPS = 1e-5
XH = 4
XDH = 512


class CG:
    def __init__(self, name, n, A, B, G):
        self.name, self.n, self.A, self.B, self.G = name, n, A, B, G

    def a(self, kc):
        return self.A[:, kc, :self.n]

    def b(self, kc):
        return self.B[:, kc, :self.n]

    def g(self, kc):
        return self.G[:, kc, :self.n]

    def ka(self, kc):
        return "%s.A%d" % (self.name, kc)

    def kb(self, kc):
        return "%s.B%d" % (self.name, kc)

    def kg(self, kc):
        return "%s.G%d" % (self.name, kc)


class TCtx:
    def __init__(self, nc, P, st, NT):
        self.nc, self.P, self.NT = nc, P, NT
        sfx = uid()
        sb = lambda name, shape, dt: st.enter_context(nc.sbuf_tensor(name + sfx, shape, dt))
        self.W = [sb("tw%d" % i, [128, 16, 512], BF16) for i in range(3)]
        self.wn = 0
        self.A = sb("tA", [128, 16, 512], F32)
        self.B = sb("tB", [128, 16, 512], BF16)
        self.G = sb("tG", [128, 44, 512], BF16)
        self.Ah = sb("tAh", [128, 16, 2], F32)
        self.Bh = sb("tBh", [128, 16, 2], BF16)
        self.Gh = sb("tGh", [128, 32, 2], BF16)
        self.memT = sb("tmemT", [128, 16, NMEM], BF16)
        self.KT = sb("tKT", [128, 16, NMEM], BF16)
        self.V = sb("tV", [128, 2, D], BF16)
        self.PT = [sb("tPT%d" % i, [128, 2, 512], BF16) for i in range(2)]
        self.rl = [sb("trl%d" % i, [128, 512], F32) for i in range(2)]
        self.mean = sb("tmean", [128, 512], F32)
        self.msq = sb("tmsq", [128, 512], F32)
        self.rstd = sb("trstd", [128, 512], F32)
        self.lt = [sb("tlt%d" % i, [128, 512], F32) for i in range(4)]
        self.hb = [sb("thb%d" % i, [128, 514], F32) for i in range(4)]
        self.ct = [sb("tct%d" % i, [128, 512], F32) for i in range(4)]
        self.hal = sb("thal", [128, 88, 2], F32)
        self.ones = sb("tones", [128, 128], BF16)
        self.flag = sb("tflag", [128, 1], F32)
        self.epsb = sb("tepsb", [128, 1], F32)
        self.fmaskf = sb("tfmaskf", [128, 512], F32)
        self.fmask = self.fmaskf[:].bitcast(I32)
        self.nflag = sb("tnflag", [128, 1], F32)
        self.lnp = sb("tlnp", [128, 6, 16], F32)
        self.cw = sb("tcw", [128, 88, 3], F32)
        self.cb = sb("tcb", [128, 88], F32)
        self.ps = [st.enter_context(nc.psum_tensor("tps%d" % i + sfx, [128, 512], F32)) for i in range(8)]
        self.pn = 0
        self.ltn = 0
        self.hbn = 0

    def bank(self):
        i = self.pn % 4
        self.pn += 1
        return i

    def wload(self, view, nk):
        s = self.wn % 3
        self.wn += 1
        q = "sp" if view.dtype == BF16 else "pool"
        self.P.dma(q, self.W[s][:, :nk, :], view, writes=["tw%d" % s])
        return s


def wview(dr, name, r0, nk, c0):
    if name + "_b" in dr:
        t = (c0 // 512) * 4 + r0 // (11 * 128) if name == "down" else c0 // 512
        return dr[name + "_b"][t].rearrange("p (kc c) -> p kc c", c=512)
    return dr[name][r0:r0 + nk * 128, c0:c0 + 512].rearrange("(kc p) c -> p kc c", p=128)


def conv_tasks(dr):
    tasks = []
    for name, ntile in (("xk", 4), ("xv", 4), ("w_out", 4), ("xq", 4), ("xo", 4), ("up", 22)):
        for t in range(ntile):
            tasks.append((dr[name + "_b"][t].rearrange("p (kc c) -> p kc c", c=512),
                          dr[name][:, t * 512:(t + 1) * 512].rearrange("(kc p) c -> p kc c", p=128)))
    for ct in range(4):
        for kg in range(4):
            tasks.append((dr["down_b"][ct * 4 + kg].rearrange("p (kc c) -> p kc c", c=512),
                          dr["down"][kg * 1408:(kg + 1) * 1408, ct * 512:(ct + 1) * 512].rearrange(
                              "(kc p) c -> p kc c", p=128)))
    return tasks


def dense(T, dr, name, nk, ncols, cgs, src, ksrc, evac, col0=0):
    P = T.P
    for ct in range(ncols // 512):
        c0 = col0 + ct * 512
        s = T.wload(wview(dr, name, 0, nk, c0), nk)
        for cg in cgs:
            for jc in range(4):
                b = T.bank()
                for kc in range(nk):
                    P.op("pe", lambda e, s=s, kc=kc, jc=jc, b=b, cg=cg: e.matmul(
                        T.ps[b][:, :cg.n], lhsT=T.W[s][:, kc, jc * 128:(jc + 1) * 128], rhs=src(cg, kc),
                        start=(kc == 0), stop=(kc == nk - 1)),
                        reads=["tw%d" % s, ksrc(cg, kc)], writes=["tps%d" % b])
                evac(cg, ct * 4 + jc, b)


def ln_pre(T, cg, kc, early):
    P = T.P
    dst, dk = (cg.g(kc), cg.kg(kc)) if early else (cg.b(kc), cg.kb(kc))
    P.op("pool" if kc % 2 else "dve", lambda e: e.tensor_copy(out=dst, in_=cg.a(kc)),
         reads=[cg.ka(kc)], writes=[dk])
    P.op("act", lambda e: e.activation(out=cg.g(16 + kc), in_=cg.a(kc), func=AF.Square),
         reads=[cg.ka(kc)], writes=[cg.kg(16 + kc)])


def layernorm(T, cg, li, early=False):
    P = T.P
    n = cg.n
    gcol, bcol = 2 * li, 2 * li + 1
    if not early:
        for kc in range(16):
            ln_pre(T, cg, kc, False)
    b1, b2 = 4, 5
    for kc in range(16):
        P.op("pe", lambda e, kc=kc: e.matmul(T.ps[b1][:, :n], lhsT=T.ones[:],
                                              rhs=(cg.g(kc) if early else cg.b(kc)),
                                              start=(kc == 0), stop=(kc == 15)),
             reads=["tones", cg.kg(kc) if early else cg.kb(kc)], writes=["tps%d" % b1])
    for kc in range(16):
        P.op("pe", lambda e, kc=kc: e.matmul(T.ps[b2][:, :n], lhsT=T.ones[:], rhs=cg.g(16 + kc),
                                              start=(kc == 0), stop=(kc == 15)),
             reads=["tones", cg.kg(16 + kc)], writes=["tps%d" % b2])
    P.op("act", lambda e: e.activation(out=T.mean[:, :n], in_=T.ps[b1][:, :n], func=AF.Copy, scale=1.0 / D),
         reads=["tps%d" % b1], writes=["tmean"])
    P.op("act", lambda e: e.activation(out=T.msq[:, :n], in_=T.ps[b1][:, :n], func=AF.Square, scale=1.0 / D),
         reads=["tps%d" % b1], writes=["tmsq"])
    P.op("dve", lambda e: e.scalar_tensor_tensor(out=T.rstd[:, :n], in0=T.ps[b2][:, :n], scalar=1.0 / D,
                                                  in1=T.msq[:, :n], op0=ALU.mult, op1=ALU.subtract),
         reads=["tps%d" % b2, "tmsq"], writes=["trstd"])
    P.op("act", lambda e: e.activation(out=T.rstd[:, :n], in_=T.rstd[:, :n], func=AF.Sqrt, bias=T.epsb[:, 0:1]),
         reads=["trstd", "tepsb"], writes=["trstd"])
    P.op("dve", lambda e: e.reciprocal(out=T.rstd[:, :n], in_=T.rstd[:, :n]),
         reads=["trstd"], writes=["trstd"])
    for kc in range(16):
        li_ = T.ltn % 4
        T.ltn += 1
        lt = T.lt[li_]
        lk = "tlt%d" % li_
        P.op("dve", lambda e, kc=kc, lt=lt: e.tensor_tensor(out=lt[:, :n], in0=cg.a(kc), in1=T.mean[:, :n],
                                                            op=ALU.subtract),
             reads=[cg.ka(kc), "tmean"], writes=[lk])
        P.op("pool" if kc % 2 else "dve",
             lambda e, lt=lt: e.tensor_tensor(out=lt[:, :n], in0=lt[:, :n], in1=T.rstd[:, :n], op=ALU.mult),
             reads=[lk, "trstd"], writes=[lk])
        P.op("act", lambda e, kc=kc, lt=lt: e.activation(out=cg.a(kc), in_=lt[:, :n], func=AF.Identity,
                                                          scale=T.lnp[:, gcol, kc:kc + 1],
                                                          bias=T.lnp[:, bcol, kc:kc + 1]),
             reads=[lk, "tlnp"], writes=[cg.ka(kc)])
        P.op("act", lambda e, kc=kc, lt=lt: e.activation(out=cg.b(kc), in_=lt[:, :n], func=AF.Identity,
                                                          scale=T.lnp[:, gcol, kc:kc + 1],
                                                          bias=T.lnp[:, bcol, kc:kc + 1]),
             reads=[lk, "tlnp"], writes=[cg.kb(kc)])


def emit_T(nc, P, NT, dr, final_keys, final=False):
    with ExitStack() as st:
        _emit_T(nc, P, st, NT, dr, final_keys)
        P.flush(final=final)


def _emit_T(nc, P, st, NT, dr, final_keys):
    T = TCtx(nc, P, st, NT)
    NB = NT // 512
    P.op("dve", lambda e: e.memset(T.ones[:], 1.0), writes=["tones"])
    P.op("dve", lambda e: e.memset(T.epsb[:], LN_EPS), writes=["tepsb"])
    P.dma("sp", T.flag[:], dr["flag"], writes=["tflag"])
    P.op("dve", lambda e: e.tensor_scalar(out=T.nflag[:], in0=T.flag[:], scalar1=-1.0, scalar2=1.0,
                                           op0=ALU.mult, op1=ALU.add), reads=["tflag"], writes=["tflag"])
    P.op("dve", lambda e: e.memset(T.fmaskf[:], 1.0), writes=["tflag"])
    P.op("dve", lambda e: e.tensor_scalar(out=T.fmaskf[:], in0=T.fmaskf[:], scalar1=T.flag[:, 0:1], scalar2=None,
                                           op0=ALU.mult), reads=["tflag"], writes=["tflag"])
    P.dma("sp", T.lnp[:], dr["lnp"], writes=["tlnp"])
    P.dma("sp", T.cw[:], dr["cw"], writes=["tcw"])
    P.dma("sp", T.cb[:], dr["cb"], writes=["tcb"])
    P.dma("pool", T.memT[:], dr["memT"].rearrange("(kc p) m -> p kc m", p=128), writes=["tmemT"])

    memcg = CG("mem", NMEM, None, None, None)

    def ev_k(cg, j, b):
        P.op("act", lambda e: e.activation(out=T.KT[:, j, :], in_=T.ps[b][:, :NMEM], func=AF.Copy),
             reads=["tps%d" % b], writes=["tKT%d" % j])

    dense(T, dr, "xk", 16, D, [memcg], lambda cg, kc: T.memT[:, kc, :], lambda cg, kc: "tmemT", ev_k)
    for ct in range(4):
        s = T.wload(wview(dr, "xv", 0, 16, ct * 512), 16)
        for mc in range(2):
            b = T.bank()
            for kc in range(16):
                P.op("pe", lambda e, s=s, kc=kc, mc=mc, b=b: e.matmul(
                    T.ps[b][:, :], lhsT=T.memT[:, kc, mc * 128:(mc + 1) * 128], rhs=T.W[s][:, kc, :],
                    start=(kc == 0), stop=(kc == 15)),
                    reads=["tw%d" % s, "tmemT"], writes=["tps%d" % b])
            P.op("act", lambda e, mc=mc, ct=ct, b=b: e.activation(out=T.V[:, mc, ct * 512:(ct + 1) * 512],
                                                                in_=T.ps[b][:, :], func=AF.Copy),
                 reads=["tps%d" % b], writes=["tV%d_%d" % (mc, ct)])

    main = CG("m", 512, T.A, T.B, T.G)
    halo = CG("h", 2, T.Ah, T.Bh, T.Gh)

    for blk in range(NB):
        t0 = blk * 512
        cgs = [main, halo] if blk == 0 else [main]
        P.dma("sp", T.A[:], dr["xres"][:, t0:t0 + 512].rearrange("(kc p) t -> p kc t", p=128),
              writes=[main.ka(kc) for kc in range(16)])
        if "mixG" in dr:
            P.dma("sp", T.B[:], dr["mixG"][:, t0:t0 + 512].rearrange("(kc p) t -> p kc t", p=128),
                  reads=["mixG"], writes=[main.kb(kc) for kc in range(16)])
            P.dma("sp", T.G[:, 0:16, :], dr["mixG"][:, NT + t0:NT + t0 + 512].rearrange("(kc p) t -> p kc t", p=128),
                  reads=["mixG"], writes=[main.kg(kc) for kc in range(16)])
            for kc in range(16):
                P.op("dve", lambda e, kc=kc: e.copy_predicated(out=main.b(kc), mask=T.fmask[:], data=main.g(kc)),
                     reads=[main.kg(kc), "tflag", main.kb(kc)], writes=[main.kb(kc)])
        else:
            P.dma("sp", T.B[:], dr["mix"][:, t0:t0 + 512].rearrange("(kc p) t -> p kc t", p=128),
                  writes=[main.kb(kc) for kc in range(16)])
        if blk == 0:
            P.dma("sp", T.Ah[:], dr["xres_h"].rearrange("(kc p) t -> p kc t", p=128),
                  writes=[halo.ka(kc) for kc in range(16)])
            P.dma("sp", T.Bh[:], dr["mix_h"].rearrange("(kc p) t -> p kc t", p=128),
                  reads=["mixG"], writes=[halo.kb(kc) for kc in range(16)])

        def ev_res(cg, j, b):
            P.op("dve", lambda e: e.scalar_tensor_tensor(out=cg.a(j), in0=cg.a(j), scalar=ALPHA,
                                                          in1=T.ps[b][:, :cg.n], op0=ALU.mult, op1=ALU.add),
                 reads=[cg.ka(j), "tps%d" % b], writes=[cg.ka(j)])

        srcB = lambda cg, kc: cg.b(kc)
        ksrcB = lambda cg, kc: cg.kb(kc)
        def ev_res_ln(cg, j, b):
            ev_res(cg, j, b)
            ln_pre(T, cg, j, True)

        dense(T, dr, "w_out", 16, D, cgs, srcB, ksrcB, ev_res_ln)
        for cg in cgs:
            layernorm(T, cg, 0, early=True)

        def ev_q(cg, j, b):
            P.op("act", lambda e: e.activation(out=cg.g(j), in_=T.ps[b][:, :cg.n], func=AF.Copy,
                                                scale=float(XDH) ** -0.5),
                 reads=["tps%d" % b], writes=[cg.kg(j)])

        dense(T, dr, "xq", 16, D, cgs, srcB, ksrcB, ev_q)
        def xattn(cg):
            n = cg.n

            def scores(h):
                hp = h % 2
                for mc in range(2):
                    b = T.bank()
                    for dc in range(4):
                        j = h * 4 + dc
                        P.op("pe", lambda e, j=j, mc=mc, b=b, dc=dc: e.matmul(
                            T.ps[b][:, :n], lhsT=T.KT[:, j, mc * 128:(mc + 1) * 128], rhs=cg.g(j),
                            start=(dc == 0), stop=(dc == 3)),
                            reads=["tKT%d" % j, cg.kg(j)], writes=["tps%d" % b])
                    P.op("act", lambda e, mc=mc, b=b, hp=hp: e.activation(out=T.PT[hp][:, mc, :n],
                                                                          in_=T.ps[b][:, :n], func=AF.Exp),
                         reads=["tps%d" % b], writes=["tPT%d_%d" % (hp, mc)])

            def pv(h):
                hp = h % 2
                bl = 6 + hp
                for mc in range(2):
                    P.op("pe", lambda e, mc=mc, hp=hp, bl=bl: e.matmul(T.ps[bl][:, :n], lhsT=T.ones[:],
                                                                      rhs=T.PT[hp][:, mc, :n],
                                                                      start=(mc == 0), stop=(mc == 1)),
                         reads=["tones", "tPT%d_%d" % (hp, mc)], writes=["tps%d" % bl])
                P.op("dve", lambda e, hp=hp, bl=bl: e.reciprocal(out=T.rl[hp][:, :n], in_=T.ps[bl][:, :n]),
                     reads=["tps%d" % bl], writes=["trl%d" % hp])
                for ec in range(4):
                    j = h * 4 + ec
                    b = T.bank()
                    for mc in range(2):
                        P.op("pe", lambda e, mc=mc, j=j, b=b, hp=hp: e.matmul(
                            T.ps[b][:, :n], lhsT=T.V[:, mc, j * 128:(j + 1) * 128], rhs=T.PT[hp][:, mc, :n],
                            start=(mc == 0), stop=(mc == 1)),
                            reads=["tV%d_%d" % (mc, j // 4), "tPT%d_%d" % (hp, mc)], writes=["tps%d" % b])
                    P.op("dve", lambda e, j=j, b=b, hp=hp: e.tensor_tensor(out=cg.b(j), in0=T.ps[b][:, :n],
                                                                           in1=T.rl[hp][:, :n], op=ALU.mult),
                         reads=["tps%d" % b, "trl%d" % hp], writes=[cg.kb(j)])

            scores(0)
            for h in range(XH):
                if h + 1 < XH:
                    scores(h + 1)
                pv(h)

        for cg_ in cgs:
            xattn(cg_)
        dense(T, dr, "xo", 16, D, cgs, srcB, ksrcB, ev_res_ln)
        for cg in cgs:
            layernorm(T, cg, 1, early=True)

        def conv(cg, fc, b):
            n = cg.n
            if cg is halo:
                P.op("dve", lambda e: e.tensor_scalar(out=T.hal[:, fc, :], in0=T.ps[b][:, :2],
                                                       scalar1=T.flag[:, 0:1], scalar2=None, op0=ALU.mult),
                     reads=["tps%d" % b, "tflag"], writes=["thal%d" % fc])
                return None
            hi = T.hbn % 4
            ci = T.hbn % 4
            T.hbn += 1
            hb, hk = T.hb[hi], "thb%d" % hi
            ctb, ck = T.ct[ci], "tct%d" % ci
            P.op("pool", lambda e: e.tensor_copy(out=hb[:, 0:2], in_=T.hal[:, fc, :]),
                 reads=["thal%d" % fc], writes=[hk])
            P.op("act", lambda e: e.activation(out=hb[:, 2:2 + n], in_=T.ps[b][:, :n], func=AF.Copy),
                 reads=["tps%d" % b, hk], writes=[hk])
            P.op("dve", lambda e: e.tensor_scalar(out=ctb[:, :n], in0=hb[:, 2:2 + n], scalar1=T.cw[:, fc, 2:3],
                                                   scalar2=T.cb[:, fc:fc + 1], op0=ALU.mult, op1=ALU.add),
                 reads=[hk, "tcw", "tcb"], writes=[ck])
            P.op("dve", lambda e: e.scalar_tensor_tensor(out=ctb[:, :n], in0=hb[:, 1:1 + n],
                                                          scalar=T.cw[:, fc, 1:2], in1=ctb[:, :n],
                                                          op0=ALU.mult, op1=ALU.add),
                 reads=[hk, "tcw", ck], writes=[ck])
            P.op("dve", lambda e: e.scalar_tensor_tensor(out=ctb[:, :n], in0=hb[:, 0:n],
                                                          scalar=T.cw[:, fc, 0:1], in1=ctb[:, :n],
                                                          op0=ALU.mult, op1=ALU.add),
                 reads=[hk, "tcw", ck], writes=[ck])
            P.op("dve", lambda e: e.tensor_copy(out=T.hal[:, fc, :], in_=hb[:, n:n + 2]),
                 reads=[hk], writes=["thal%d" % fc])
            return ctb, ck

        silu_q = []

        def silu_flush(keep):
            while len(silu_q) > keep:
                cg, j, ctb, ck = silu_q.pop(0)
                P.op("act", lambda e, cg=cg, j=j, ctb=ctb: e.activation(out=cg.g(j), in_=ctb[:, :cg.n], func=AF.Silu),
                     reads=[ck], writes=[cg.kg(j)])

        def ev_a(cg, j, b):
            r = conv(cg, j, b)
            if r is None:
                return
            ctb, ck = r
            silu_q.append((cg, j, ctb, ck))
            silu_flush(2)

        def ev_b(cg, j, b):
            r = conv(cg, 44 + j, b)
            if r is None:
                return
            ctb, ck = r
            P.op("dve", lambda e: e.tensor_tensor(out=cg.g(j), in0=cg.g(j), in1=ctb[:, :cg.n], op=ALU.mult),
                 reads=[ck, cg.kg(j)], writes=[cg.kg(j)])

        ucgs = [halo, main] if blk == 0 else [main]
        dense(T, dr, "up", 16, DFF, ucgs, srcB, ksrcB, ev_a, col0=0)
        silu_flush(0)
        dense(T, dr, "up", 16, DFF, ucgs, srcB, ksrcB, ev_b, col0=DFF)

        cg = main
        for ct in range(4):
            for kg in range(4):
                s = T.wload(wview(dr, "down", kg * 1408, 11, ct * 512), 11)
                for jc in range(4):
                    b = 4 + jc
                    for kc in range(11):
                        kk = kg * 11 + kc
                        P.op("pe", lambda e, s=s, kc=kc, jc=jc, b=b, kk=kk: e.matmul(
                            T.ps[b][:, :], lhsT=T.W[s][:, kc, jc * 128:(jc + 1) * 128], rhs=cg.g(kk),
                            start=(kk == 0), stop=(kk == 43)),
                            reads=["tw%d" % s, cg.kg(kk)], writes=["tps%d" % b])
            for jc in range(4):
                ev_res(cg, ct * 4 + jc, 4 + jc)
        layernorm(T, cg, 2)
        P.dma("sp", dr["xout"][:, t0:t0 + 512].rearrange("(kc p) t -> p kc t", p=128), T.A[:],
              reads=[main.ka(kc) for kc in range(16)], writes=["xout%d" % blk])
        final_keys.append("xout%d" % blk)
        if "xoutb" in dr:
            P.dma("sp", dr["xoutb"][blk].rearrange("(kc p) t -> p kc t", p=128), T.B[:],
                  reads=[main.kb(kc) for kc in range(16)], writes=["xoutb%d" % blk])
            final_keys.append("xoutb%d" % blk)
            P.coll("AllGather", [dr["xoutb"][blk]], [dr["xG_out"][blk]], dr["groups"],
                   reads=["xoutb%d" % blk], writes=["xG%d" % blk])
        if "xouth" in dr and blk == NB - 1:
            P.dma("sp", dr["xouth"].rearrange("(kc p) t -> p kc t", p=128), T.A[:, :, 510:512],
                  reads=[main.ka(kc) for kc in range(16)], writes=["xouth"])
    return T


import ml_dtypes
from concourse.bass_utils import run_bass_kernel_spmd

SEQ = 4096
NB_ = 4
HALF = SEQ // 2
GROUPS = [[0, 1], [2, 3], [4, 5], [6, 7]]


def m_consts():
    sel = np.zeros((4, 4, 128), np.float32)
    for h in range(4):
        sel[h, h, :] = 1.0
    id4 = np.eye(4, dtype=np.float32)
    s = np.arange(128)[:, None]
    t = np.arange(128)[None, :]
    nmask = np.where(s <= t, 0.0, -1e30).astype(np.float32)
    rst = np.ones((128, 512), np.float32)
    rst[:, ::64] = 0.0
    s = np.arange(64)[:, None]
    t = np.arange(64)[None, :]
    tri = (s <= t).astype(np.int32)
    identb = np.eye(128, dtype=np.float32).astype(ml_dtypes.bfloat16)
    return dict(sel=sel, id4=id4, nmask=nmask, rst=rst, tri=tri, identb=identb)


def m_weights(inputs, l, g):
    w_in_l = np.asarray(inputs["w_in"][l], np.float32)
    hs = slice(4 * g * 128, (4 * g + 4) * 128)
    fq = w_in_l[:, 0:1024][:, hs]
    fk = w_in_l[:, 1024:2048][:, hs]
    fv = w_in_l[:, 2048:3072][:, hs]
    ff = w_in_l[:, 3072:3080][:, 4 * g:4 * g + 4]
    o = 3080
    hq = w_in_l[:, o:o + 1024][:, hs]
    hf = w_in_l[:, o + 1024:o + 2048][:, hs]
    hi = w_in_l[:, o + 2048:o + 3072][:, hs]
    hg = w_in_l[:, o + 3072:o + 4096][:, hs]
    c = np.ascontiguousarray
    return {
        "wqkv%d" % l: c(np.concatenate([fq, fk, fv], axis=1)),
        "wff%d" % l: c(ff),
        "fbias%d" % l: c(np.asarray(inputs["fox_f_bias"][l], np.float32)[4 * g:4 * g + 4].reshape(4, 1)),
        "wh%d" % l: c(np.concatenate([hq, hf, hi, hg], axis=1)),
        "normw%d" % l: c(np.asarray(inputs["hgrn_norm_w"][l], np.float32)[hs].reshape(4, 128).T),
        "lbl": c(np.asarray(inputs["hgrn_lb_logits"], np.float32)[:, hs].reshape(2, 4, 128).transpose(2, 0, 1)),
    }


def t_weights(inputs, l):
    f = lambda a: np.ascontiguousarray(np.asarray(a, dtype=np.float32))
    pk = lambda v: f(np.asarray(v).reshape(-1, 128).T)
    lnp = np.stack([pk(inputs[k][l]) for k in ("ln1_g", "ln1_b", "ln2_g", "ln2_b", "ln3_g", "ln3_b")], axis=1)
    cw = np.asarray(inputs["conv_w"][l]).reshape(3, 88, 128).transpose(2, 1, 0)
    wo = np.asarray(inputs["w_out"][l], np.float32).reshape(16, 128, 2048)
    perm = []
    for c_ in range(4):
        for r in range(2):
            for u in range(2):
                lc = c_ * 2 + u
                perm.append((lc // 4) * 8 + 4 * r + (lc % 4))
    wo = wo[perm].reshape(2048, 2048)
    d = dict(lnp=f(lnp), cw=f(cw), cb=pk(inputs["conv_b"][l]),
             w_out=f(wo), xq=f(inputs["xq_w"][l]), xk=f(inputs["xk_w"][l]),
             xv=f(inputs["xv_w"][l]), xo=f(inputs["xo_w"][l]), up=f(inputs["ffn_up"][l]),
             down=f(inputs["ffn_down"][l]))
    return {"%s%d" % (k, l): v for k, v in d.items()}


def build_fused(S=SEQ, GROUPS=GROUPS):
    nc = bass.Bass("TRN2", target_bir_lowering=False)
    st0 = ExitStack()
    with st0:
        def din(name, shape, dt=F32):
            return nc.dram_tensor(name, shape, dt, kind="ExternalInput").ap()

        def dint(name, shape, dt):
            return nc.dram_tensor(name, shape, dt).ap()
        NT = S // 2
        cst = dict(sel=din("sel", [4, 4, 128]), id4=din("id4", [4, 4]), nmask=din("nmask", [128, 128]),
                   rst=din("rst", [128, 512]), tri=din("tri", [64, 64], I32),
                   identb=din("identb", [128, 128], BF16), lbl=din("lbl", [128, 2, 4]))
        xT0 = din("xT0", [2048, S])
        xres0 = din("xres0", [2048, NT])
        xres_h0 = din("xres_h0", [2048, 2])
        flag = din("flag", [128, 1])
        memT = din("memT", [2048, 256])
        mw, tw = [], []
        for l in range(2):
            mw.append(dict(wqkv=din("wqkv%d" % l, [2048, 1536]), wff=din("wff%d" % l, [2048, 4]),
                           fbias=din("fbias%d" % l, [4, 1]), wh=din("wh%d" % l, [2048, 2048]),
                           normw=din("normw%d" % l, [128, 4])))
            tw.append(dict(lnp=din("lnp%d" % l, [128, 6, 16]), cw=din("cw%d" % l, [128, 88, 3]),
                           cb=din("cb%d" % l, [128, 88]), w_out=din("w_out%d" % l, [2048, 2048]),
                           xq=din("xq%d" % l, [2048, 2048]), xk=din("xk%d" % l, [2048, 2048]),
                           xv=din("xv%d" % l, [2048, 2048]), xo=din("xo%d" % l, [2048, 2048]),
                           up=din("up%d" % l, [2048, 11264]), down=din("down%d" % l, [5632, 2048])))
        for l in range(2):
            for nm, nt_, w_ in (("xk", 4, 8192), ("xv", 4, 8192), ("w_out", 4, 8192), ("xq", 4, 8192),
                                ("xo", 4, 8192), ("up", 22, 8192), ("down", 16, 5632)):
                tw[l][nm + "_b"] = dint("%s_b%d" % (nm, l), [nt_, 128, w_], BF16)
        xout = nc.dram_tensor("xout", [2048, NT], F32, kind="ExternalOutput").ap()
        mixA = [dint("mixA%d" % l, [1024, S], BF16) for l in range(2)]
        mixG = [dint("mixG%d" % l, [2048, S], BF16) for l in range(2)]
        xI = dint("xI", [2048, NT], F32)
        xB = dint("xB", [NT // 512, 2048, 512], BF16)
        xG = dint("xG", [NT // 512, 4096, 512], BF16)
        xH = dint("xH", [2048, 2], F32)
        xHG = dint("xHG", [4096, 2], F32)
        P = Prog(nc, st0)
        fk = []
        for l in range(2):
            dm = dict(cst)
            dm.update(mw[l])
            dm["mix"] = mixA[l]
            if l == 0:
                dm["xT"] = xT0
            else:
                P.coll("AllGather", [xH], [xHG], GROUPS, writes=["xHG"])
                dm["xG"] = xG
            bg = conv_tasks(tw[l])
            nblk = S // 512
            nbg = -(-len(bg) // (2 * nblk))
            emit_fox(nc, P, S, dm, fk, x_f32=(l == 0), bgq=bg, nbg=nbg)
            for c_ in range(2):
                P.coll("AllGather", [mixA[l][c_ * 256:(c_ + 1) * 256, :]], [mixG[l][c_ * 512:(c_ + 1) * 512, :]],
                       GROUPS, writes=["mixG"])
            emit_hgrn(nc, P, S, l, dm, fk, x_f32=(l == 0), bgq=bg, nbg=nbg)
            while bg:
                o_, i_ = bg.pop(0)
                P.dma("pool", o_, i_)
            for c_ in range(2, 4):
                P.coll("AllGather", [mixA[l][c_ * 256:(c_ + 1) * 256, :]], [mixG[l][c_ * 512:(c_ + 1) * 512, :]],
                       GROUPS, writes=["mixG"])
            dt = dict(tw[l])
            dt.update(flag=flag, memT=memT, mixG=mixG[l], mix_h=mixG[l][:, NT - 2:NT])
            if l == 0:
                dt.update(xres=xres0, xres_h=xres_h0, xout=xI, xoutb=xB, xouth=xH, xG_out=xG, groups=GROUPS)
            else:
                dt.update(xres=xI, xres_h=xHG[0:2048, :], xout=xout)
            emit_T(nc, P, NT, dt, fk, final=(l == 1))
    return nc


def kernel(**inputs):
    c = np.ascontiguousarray
    x = np.asarray(inputs["x"], np.float32)
    mem = np.asarray(inputs["mem"], np.float32)
    consts = m_consts()
    tw = {}
    for l in range(2):
        tw.update(t_weights(inputs, l))
    mwg = []
    for g in range(2):
        d = {}
        for l in range(2):
            d.update(m_weights(inputs, l, g))
        mwg.append(d)
    in_maps = []
    for b in range(NB_):
        xT = c(x[b].T)
        mT = c(mem[b].T)
        for g in range(2):
            d = dict(tw)
            d.update(mwg[g])
            d.update(consts)
            lo = g * HALF
            d["xT0"] = xT
            d["xres0"] = c(xT[:, lo:lo + HALF])
            d["xres_h0"] = c(xT[:, lo - 2:lo]) if g == 1 else np.zeros((2048, 2), np.float32)
            d["flag"] = np.full((128, 1), float(g), np.float32)
            d["memT"] = mT
            in_maps.append(d)
    nc = build_fused()
    res = run_bass_kernel_spmd(nc, in_maps, core_ids=list(range(8))).results
    out = np.empty((NB_, SEQ, 2048), np.float32)
    for b in range(NB_):
        for g in range(2):
            out[b, g * HALF:(g + 1) * HALF, :] = np.asarray(res[2 * b + g]["xout"]).T
    return out
```

```python
import numpy as np
import concourse.bass as bass
import concourse.mybir as mybir
from contextlib import ExitStack

F32 = mybir.dt.float32
BF16 = mybir.dt.bfloat16
I32 = mybir.dt.int32
AF = mybir.ActivationFunctionType
ALU = mybir.AluOpType

ENGS = ("pe", "act", "dve", "pool", "sp")
DMA_RING = 8


class _Op:
    __slots__ = ("fn", "deps", "kind", "signal", "count", "dslot", "dcount")

    def __init__(self, fn, deps, kind):
        self.fn = fn
        self.deps = deps
        self.kind = kind
        self.signal = False
        self.count = 0
        self.dslot = None
        self.dcount = 0


class Prog:
    def __init__(self, nc, st):
        self.nc = nc
        self.csem = {e: st.enter_context(nc.semaphore("c_" + e)) for e in ENGS}
        self.dsem = {e: [st.enter_context(nc.semaphore("d_%s_%d" % (e, i))) for i in range(DMA_RING)]
                     for e in ("sp", "pool")}
        self.ksem = st.enter_context(nc.semaphore("k_coll"))
        self.ncoll = 0
        self.cbase = {e: 0 for e in ENGS}
        self.ndma = {e: 0 for e in ENGS}
        self.barrier = []
        self.total_ops = {e: 0 for e in ENGS}
        self.total_waits = {e: 0 for e in ENGS}
        self._reset()

    def _reset(self):
        self.ops = {e: [] for e in ENGS}
        self.last_writer = {}
        self.readers = {}

    def _add(self, eng, fn, reads, writes, kind):
        deps = set()
        for k in reads:
            lw = self.last_writer.get(k)
            if lw is not None:
                deps.add(lw)
        for k in writes:
            lw = self.last_writer.get(k)
            if lw is not None:
                deps.add(lw)
            for r in self.readers.get(k, ()):
                deps.add(r)
        idx = len(self.ops[eng])
        if eng == "pe":
            deps = {d for d in deps if d[0] != "pe"}
        op = _Op(fn, deps, kind)
        if kind == "k":
            self.ncoll += 1
            op.dcount = self.ncoll
        if kind == "d":
            n = self.ndma[eng]
            self.ndma[eng] = n + 1
            op.dslot = n % DMA_RING
            op.dcount = (n // DMA_RING + 1) * 16
        self.ops[eng].append(op)
        me = (eng, idx)
        for k in writes:
            self.last_writer[k] = me
            self.readers[k] = []
        for k in reads:
            self.readers.setdefault(k, []).append(me)
        return me

    def op(self, eng, fn, reads=(), writes=()):
        return self._add(eng, fn, reads, writes, "c")

    def dma(self, eng, out, in_, reads=(), writes=(), **kw):
        return self._add(eng, lambda e: e.dma_start(out=out, in_=in_, **kw), reads, writes, "d")

    def flush(self, final=False):
        nc = self.nc
        for e in ENGS:
            for op in self.ops[e]:
                best = {}
                keep = set()
                for (de, di) in op.deps:
                    if self.ops[de][di].kind == "c":
                        if best.get(de, -1) < di:
                            best[de] = di
                    else:
                        keep.add((de, di))
                for de, di in best.items():
                    keep.add((de, di))
                    self.ops[de][di].signal = True
                op.deps = keep
            for op in reversed(self.ops[e]):
                if op.kind == "c":
                    op.signal = True
                    break
        for e in ENGS:
            c = self.cbase[e]
            for op in self.ops[e]:
                if op.kind == "c" and op.signal:
                    c += 1
                    op.count = c
            self.cbase[e] = c
        nxt = []
        for e in ENGS:
            if self.cbase[e] > 0:
                nxt.append((self.csem[e], self.cbase[e]))
        if self.ncoll > 0:
            nxt.append((self.ksem, self.ncoll))
        for e in self.dsem:
            n = self.ndma[e]
            for slot in range(DMA_RING):
                if n > slot:
                    cnt = ((n - 1 - slot) // DMA_RING + 1) * 16
                    nxt.append((self.dsem[e][slot], cnt))
        prev_barrier = self.barrier

        def resolve(dep):
            de, di = dep
            dop = self.ops[de][di]
            if dop.kind == "c":
                return self.csem[de], dop.count
            if dop.kind == "k":
                return self.ksem, dop.dcount
            return self.dsem[de][dop.dslot], dop.dcount

        def body(e, eng):
            waited = {}
            nw = 0

            def wait(sem, val):
                nonlocal nw
                key = id(sem)
                if waited.get(key, 0) < val:
                    eng.wait_ge(sem, val)
                    waited[key] = val
                    nw += 1

            for (s, v) in prev_barrier:
                wait(s, v)
            for op in self.ops[e]:
                for dep in sorted(op.deps):
                    s, v = resolve(dep)
                    wait(s, v)
                if op.kind == "k":
                    ins = op.fn(eng)
                    ins.then_inc(self.ksem, 1)
                elif op.kind == "d":
                    if op.dcount > 16:
                        wait(self.dsem[e][op.dslot], op.dcount - 16)
                    ins = op.fn(eng)
                    ins.then_inc(self.dsem[e][op.dslot], 16)
                else:
                    ins = op.fn(eng)
                    if op.signal:
                        ins.then_inc(self.csem[e], 1)
            if final and e == "sp":
                for (s, v) in nxt:
                    wait(s, v)
            self.total_ops[e] += len(self.ops[e])
            self.total_waits[e] += nw

        with nc.Block() as block:
            @block.tensor
            def _(eng):
                body("pe", eng)

            @block.scalar
            def _(eng):
                body("act", eng)

            @block.vector
            def _(eng):
                body("dve", eng)

            @block.gpsimd
            def _(eng):
                body("pool", eng)

            @block.sync
            def _(eng):
                body("sp", eng)
        self.barrier = nxt
        self._reset()


def _coll(self, kind, ins, outs, groups, reads=(), writes=()):
    return self._add("pool", lambda e: e.collective_compute(kind, ALU.bypass, replica_groups=groups,
                                                            ins=[a.opt() for a in ins], outs=[a.opt() for a in outs]),
                     reads, writes, "k")


Prog.coll = _coll


_UIDC = [0]


def uid():
    _UIDC[0] += 1
    return "_u%d" % _UIDC[0]


D = 2048
NH = 4
DH = 128
CH = 64
RMS_EPS = 1e-6
MIN_FORGET = 1e-6


def _xload(P, X, xs, dr, j, x_f32):
    if "xG" in dr:
        nb_half = dr["xG"].shape[0]
        r, blk = j // nb_half, j % nb_half
        view = dr["xG"][blk][r * 2048:(r + 1) * 2048, :].rearrange("(kc p) t -> p kc t", p=128)
        P.dma("sp", X[xs][:], view, writes=["mx%d" % xs])
        return
    view = dr["xT"][:, j * 512:(j + 1) * 512].rearrange("(kc p) t -> p kc t", p=128)
    P.dma("pool" if x_f32 else "sp", X[xs][:], view, writes=["mx%d" % xs])


def emit_fox(nc, P, S, dr, final_keys, x_f32=True, final=False, bgq=None, nbg=0):
    NB = S // 512
    with ExitStack() as st:
        sfx = uid()
        sb = lambda name, shape, dt: st.enter_context(nc.sbuf_tensor(name + sfx, shape, dt))
        W = [sb("mw%d" % i, [128, 16, 512], BF16) for i in range(3)]
        wff = sb("mwff", [128, 16, 4], BF16)
        X = [sb("mx%d" % i, [128, 16, 512], BF16) for i in range(2)]
        kT = sb("mkT", [128, NH, S], BF16)
        vt = sb("mvt", [128, S // 128, 512], BF16)
        qT = sb("mqT", [128, NH, 512], BF16)
        cbc = sb("mcbc", [128, NH, 512], F32)
        cbm = sb("mcbm", [128, NH, 4, 128], F32)
        nct = sb("mnct", [128, S // 128, NH], F32)
        NTMP, NPT = 4, 4
        tmp = [sb("mtmp%d" % i, [128, 512], F32) for i in range(NTMP)]
        PT = [sb("mPT%d" % i, [128, 512], BF16) for i in range(NPT)]
        ost = [sb("most%d" % i, [128, 512], BF16) for i in range(2)]
        rl = sb("mrl", [128, 512], F32)
        e1 = sb("me1", [4, 512], F32)
        lf = sb("mlf", [4, 512], F32)
        cT = sb("mcT", [4, 512], F32)
        cprev = sb("mcprev", [4, 1], F32)
        ones4 = sb("mones4", [4, 512], F32)
        nfb = sb("mnfb", [4, 1], F32)
        sel = sb("msel", [4, NH, 128], F32)
        id4 = sb("mid4", [4, 4], F32)
        nmask = sb("mnmask", [128, 128], F32)
        ones = sb("mones", [128, 128], BF16)
        ps = [st.enter_context(nc.psum_tensor("mps%d" % i + sfx, [128, 512], F32)) for i in range(8)]
        pn = [0]

        def bank():
            i = pn[0] % 4
            pn[0] += 1
            return i

        P.op("dve", lambda e: e.memset(ones[:], 1.0), writes=["mones"])
        P.op("dve", lambda e: e.memset(ones4[:], 1.0), writes=["mones4"])
        P.op("dve", lambda e: e.memset(cprev[:], 0.0), writes=["mcprev"])
        P.dma("sp", sel[:], dr["sel"], writes=["msel"])
        P.dma("sp", id4[:], dr["id4"], writes=["mid4"])
        P.dma("sp", nmask[:], dr["nmask"], writes=["mnmask"])
        P.dma("sp", nfb[:], dr["fbias"], writes=["mnfb"])
        P.op("dve", lambda e: e.tensor_scalar(out=nfb[:], in0=nfb[:], scalar1=-1.0, scalar2=None, op0=ALU.mult),
             reads=["mnfb"], writes=["mnfb"])
        P.dma("pool", wff[:], dr["wff"].rearrange("(kc p) c -> p kc c", p=128), writes=["mwff"])
        _xload(P, X, 0, dr, 0, x_f32)
        for i in range(3):
            P.dma("pool", W[i][:], dr["wqkv"][:, i * 512:(i + 1) * 512].rearrange("(kc p) c -> p kc c", p=128),
                  writes=["mw%d" % i])

        tn = [0]
        pn3 = [0]
        for j in range(NB):
            xs = j % 2
            xk = "mx%d" % xs
            if j + 1 < NB:
                _xload(P, X, (j + 1) % 2, dr, j + 1, x_f32)
            for _ in range(nbg):
                if bgq:
                    o_, i_ = bgq.pop(0)
                    P.dma("pool", o_, i_)
            b = bank()
            for kc in range(16):
                P.op("pe", lambda e, kc=kc, b=b, xs=xs: e.matmul(
                    ps[b][0:4, :], lhsT=wff[:, kc, :], rhs=X[xs][:, kc, :],
                    start=(kc == 0), stop=(kc == 15)), reads=["mwff", xk], writes=["mps%d" % b])
            P.op("act", lambda e, b=b: e.activation(out=e1[:], in_=ps[b][0:4, :], func=AF.Exp, scale=-1.0,
                                                     bias=nfb[:, 0:1]),
                 reads=["mps%d" % b, "mnfb"], writes=["me1"])
            P.op("act", lambda e: e.activation(out=lf[:], in_=e1[:], func=AF.Ln, bias=1.0),
                 reads=["me1"], writes=["mlf"])
            P.op("dve", lambda e: e.tensor_scalar(out=lf[:], in0=lf[:], scalar1=-1.0, scalar2=None, op0=ALU.mult),
                 reads=["mlf"], writes=["mlf"])
            P.op("dve", lambda e: e.tensor_tensor_scan(out=cT[:], data0=ones4[:], data1=lf[:], initial=cprev[:, 0:1],
                                                        op0=ALU.mult, op1=ALU.add),
                 reads=["mones4", "mlf", "mcprev"], writes=["mcT"])
            P.op("dve", lambda e: e.tensor_copy(out=cprev[:], in_=cT[:, 511:512]),
                 reads=["mcT"], writes=["mcprev"])
            for h in range(NH):
                b = bank()
                for kc in range(16):
                    P.op("pe", lambda e, kc=kc, h=h, b=b, xs=xs: e.matmul(
                        ps[b][:, :], lhsT=W[0][:, kc, h * 128:(h + 1) * 128], rhs=X[xs][:, kc, :],
                        start=(kc == 0), stop=(kc == 15)), reads=["mw0", xk], writes=["mps%d" % b])
                P.op("act", lambda e, h=h, b=b: e.activation(out=qT[:, h, :], in_=ps[b][:, :], func=AF.Copy,
                                                              scale=float(DH) ** -0.5),
                     reads=["mps%d" % b], writes=["mq%d" % h])
            for h in range(NH):
                b = bank()
                for kc in range(16):
                    P.op("pe", lambda e, kc=kc, h=h, b=b, xs=xs: e.matmul(
                        ps[b][:, :], lhsT=W[1][:, kc, h * 128:(h + 1) * 128], rhs=X[xs][:, kc, :],
                        start=(kc == 0), stop=(kc == 15)), reads=["mw1", xk], writes=["mps%d" % b])
                P.op("act", lambda e, h=h, b=b, j=j: e.activation(out=kT[:, h, j * 512:(j + 1) * 512],
                                                                  in_=ps[b][:, :], func=AF.Copy),
                     reads=["mps%d" % b], writes=["mk%d_%d" % (h, j)])
            for s4 in range(4):
                b = bank()
                for kc in range(16):
                    P.op("pe", lambda e, kc=kc, s4=s4, b=b, xs=xs: e.matmul(
                        ps[b][:, :], lhsT=X[xs][:, kc, s4 * 128:(s4 + 1) * 128], rhs=W[2][:, kc, :],
                        start=(kc == 0), stop=(kc == 15)), reads=["mw2", xk], writes=["mps%d" % b])
                P.op("dve", lambda e, s4=s4, b=b, j=j: e.tensor_copy(out=vt[:, 4 * j + s4, :], in_=ps[b][:, :]),
                     reads=["mps%d" % b], writes=["mv%d" % (4 * j + s4)])
            for s4 in range(4):
                b = bank()
                P.op("pe", lambda e, s4=s4, b=b: e.matmul(ps[b][:, 0:4], lhsT=cT[:, s4 * 128:(s4 + 1) * 128],
                                                          rhs=id4[:], start=True, stop=True),
                     reads=["mcT", "mid4"], writes=["mps%d" % b])
                P.op("dve", lambda e, s4=s4, b=b, j=j: e.tensor_scalar(out=nct[:, 4 * j + s4, :], in0=ps[b][:, 0:4],
                                                                        scalar1=-1.0, scalar2=None, op0=ALU.mult),
                     reads=["mps%d" % b], writes=["mnc%d" % (4 * j + s4)])
            for h in range(NH):
                b = bank()
                P.op("pe", lambda e, h=h, b=b: e.matmul(ps[b][:, :], lhsT=sel[:, h, :], rhs=cT[:],
                                                        start=True, stop=True),
                     reads=["mcT", "msel"], writes=["mps%d" % b])
                P.op("act", lambda e, h=h, b=b: e.activation(out=cbc[:, h, :], in_=ps[b][:, :], func=AF.Copy),
                     reads=["mps%d" % b], writes=["mcbc%d" % h])
                for r in range(4):
                    P.op("pool", lambda e, h=h, r=r: e.tensor_tensor(out=cbm[:, h, r, :],
                                                                     in0=cbc[:, h, r * 128:(r + 1) * 128],
                                                                     in1=nmask[:], op=ALU.add),
                         reads=["mcbc%d" % h, "mnmask"], writes=["mcbm%d_%d" % (h, r)])
            LOOK = 2
            for h in range(NH):
                bo, bl = 4 + h % 2, 6 + h % 2
                last = 4 * j + 3
                pend = {}

                def stage_a(i, h=h, j=j):
                    r = i - 4 * j
                    c0 = 128 * r if r >= 0 else 0
                    b = bank()
                    P.op("pe", lambda e, h=h, i=i, c0=c0, b=b: e.matmul(
                        ps[b][:, c0:512], lhsT=kT[:, h, i * 128:(i + 1) * 128], rhs=qT[:, h, c0:512],
                        start=True, stop=True),
                        reads=["mk%d_%d" % (h, i // 4), "mq%d" % h], writes=["mps%d" % b])
                    ti = tn[0] % NTMP
                    tn[0] += 1
                    tk = "mtmp%d" % ti
                    if r >= 0:
                        P.op("dve", lambda e, h=h, r=r, c0=c0, b=b, ti=ti: e.tensor_tensor(
                            out=tmp[ti][:, c0:c0 + 128], in0=ps[b][:, c0:c0 + 128], in1=cbm[:, h, r, :], op=ALU.add),
                            reads=["mps%d" % b, "mcbm%d_%d" % (h, r)], writes=[tk])
                        if c0 + 128 < 512:
                            P.op("dve", lambda e, h=h, c0=c0, b=b, ti=ti: e.tensor_tensor(
                                out=tmp[ti][:, c0 + 128:512], in0=ps[b][:, c0 + 128:512],
                                in1=cbc[:, h, c0 + 128:512], op=ALU.add),
                                reads=["mps%d" % b, "mcbc%d" % h, tk], writes=[tk])
                    else:
                        P.op("dve", lambda e, h=h, b=b, ti=ti: e.tensor_tensor(
                            out=tmp[ti][:, :], in0=ps[b][:, :], in1=cbc[:, h, :], op=ALU.add),
                            reads=["mps%d" % b, "mcbc%d" % h], writes=[tk])
                    pi = pn3[0] % NPT
                    pn3[0] += 1
                    pk = "mPT%d" % pi
                    P.op("act", lambda e, h=h, i=i, c0=c0, ti=ti, pi=pi: e.activation(
                        out=PT[pi][:, c0:512], in_=tmp[ti][:, c0:512], func=AF.Exp, bias=nct[:, i, h:h + 1]),
                        reads=[tk, "mnc%d" % i], writes=[pk])
                    pend[i] = (c0, pi, pk)

                def stage_b(i, h=h, bo=bo, bl=bl, last=last):
                    c0, pi, pk = pend.pop(i)
                    P.op("pe", lambda e, h=h, i=i, c0=c0, pi=pi, bo=bo, last=last: e.matmul(
                        ps[bo][:, c0:512], lhsT=vt[:, i, h * 128:(h + 1) * 128], rhs=PT[pi][:, c0:512],
                        start=(i == 0), stop=(i == last)),
                        reads=["mv%d" % i, pk], writes=["mps%d" % bo])
                    P.op("pe", lambda e, i=i, c0=c0, pi=pi, bl=bl, last=last: e.matmul(
                        ps[bl][:, c0:512], lhsT=ones[:], rhs=PT[pi][:, c0:512],
                        start=(i == 0), stop=(i == last)),
                        reads=["mones", pk], writes=["mps%d" % bl])

                for i in range(min(LOOK, last + 1)):
                    stage_a(i)
                for i in range(last + 1):
                    if i + LOOK <= last:
                        stage_a(i + LOOK)
                    stage_b(i)
                P.op("dve", lambda e, bl=bl: e.reciprocal(out=rl[:], in_=ps[bl][:, :]),
                     reads=["mps%d" % bl], writes=["mrl"])
                oi = (j * NH + h) % 2
                P.op("dve", lambda e, bo=bo, oi=oi: e.tensor_tensor(out=ost[oi][:], in0=ps[bo][:, :], in1=rl[:],
                                                                    op=ALU.mult),
                     reads=["mps%d" % bo, "mrl"], writes=["most%d" % oi])
                ok = "mixf%d_%d" % (h, j)
                P.dma("sp", dr["mix"][h * 128:(h + 1) * 128, j * 512:(j + 1) * 512], ost[oi][:],
                      reads=["most%d" % oi], writes=[ok])
                final_keys.append(ok)
        P.flush(final=final)


def emit_hgrn(nc, P, S, layer, dr, final_keys, x_f32=True, final=False, bgq=None, nbg=0):
    NB = S // 512
    with ExitStack() as st:
        sfx = uid()
        sb = lambda name, shape, dt: st.enter_context(nc.sbuf_tensor(name + sfx, shape, dt))
        W = [sb("hw%d" % i, [128, 16, 512], BF16) for i in range(4)]
        X = [sb("hx%d" % i, [128, 16, 512], BF16) for i in range(2)]
        tsig = sb("htsig", [128, 512], F32)
        tsgm = sb("htsgm", [128, 512], F32)
        tf = sb("htf", [128, 512], F32)
        tG = sb("htG", [128, 512], F32)
        tD = sb("htD", [128, 512], F32)
        tD3 = sb("htD3", [128, 512], F32)
        tE1 = sb("htE1", [128, 512], F32)
        tE2 = sb("htE2", [128, 512], F32)
        tEG = sb("htEG", [128, 512], F32)
        tE3 = sb("htE3", [128, 512], F32)
        qtil = sb("hqtil", [128, NH, 512], BF16)
        ktil = sb("hktil", [128, NH, 512], BF16)
        qg = sb("hqg", [128, NH, 512], BF16)
        kdT = [sb("hkdT%d" % i, [128, 512], BF16) for i in range(2)]
        kdec = sb("hkdec", [64, NH, 8, 128], BF16)
        vh = sb("hvh", [64, 8, 512], BF16)
        EGl = sb("hEGl", [128, NH, 8], F32)
        sg = sb("hsg", [128, NH, 512], F32)
        oT = sb("hoT", [128, NH, 512], F32)
        osq = sb("hosq", [128, 512], BF16)
        rs = sb("hrs", [128, 512], F32)
        hn = sb("hhn", [128, 512], F32)
        ost = [sb("host%d" % i, [128, 512], BF16) for i in range(2)]
        stf = sb("hstf", [128, NH, 128], F32)
        stb = sb("hstb", [128, NH, 128], BF16)
        ATs = [sb("hATs%d" % i, [64, 64], BF16) for i in range(4)]
        rst = sb("hrst", [128, 512], F32)
        tri = sb("htri", [64, 64], I32)
        identb = sb("hidentb", [128, 128], BF16)
        ones = sb("hones", [128, 128], BF16)
        lbl = sb("hlbl", [128, 2, NH], F32)
        lbe = sb("hlbe", [128, 2, NH], F32)
        lbz = sb("hlbz", [128, NH], F32)
        lbs = sb("hlbs", [128, 2, NH], F32)
        lb = sb("hlb", [128, NH], F32)
        oml = sb("homl", [128, NH], F32)
        nw = sb("hnw", [128, NH], F32)
        ps = [st.enter_context(nc.psum_tensor("hps%d" % i + sfx, [128, 512], F32)) for i in range(7)]
        psb = st.enter_context(nc.psum_tensor("hpsb" + sfx, [128, 1024], BF16))
        pn = [0]

        def bank():
            i = pn[0] % 7
            pn[0] += 1
            return i

        P.op("dve", lambda e: e.memset(ones[:], 1.0), writes=["hones"])
        P.op("dve", lambda e: e.memset(stf[:], 0.0), writes=["hstf%d" % h for h in range(NH)])
        P.op("dve", lambda e: e.memset(stb[:], 0.0), writes=["hstb%d" % h for h in range(NH)])
        for i in range(4):
            P.op("pool", lambda e, i=i: e.memset(ATs[i][:], 0.0), writes=["hATs%d" % i])
        P.dma("sp", rst[:], dr["rst"], writes=["hrst"])
        P.dma("sp", tri[:], dr["tri"], writes=["htri"])
        P.dma("sp", identb[:], dr["identb"], writes=["hidentb"])
        P.dma("sp", lbl[:], dr["lbl"], writes=["hlbl"])
        P.dma("sp", nw[:], dr["normw"], writes=["hnw"])
        _xload(P, X, 0, dr, 0, x_f32)
        for i in (2, 1, 0, 3):
            P.dma("pool", W[i][:], dr["wh"][:, i * 512:(i + 1) * 512].rearrange("(kc p) c -> p kc c", p=128),
                  writes=["hw%d" % i])
        P.op("act", lambda e: e.activation(out=lbe[:], in_=lbl[:], func=AF.Exp), reads=["hlbl"], writes=["hlbe"])
        P.op("dve", lambda e: e.tensor_tensor(out=lbz[:], in0=lbe[:, 0, :], in1=lbe[:, 1, :], op=ALU.add),
             reads=["hlbe"], writes=["hlbz"])
        P.op("dve", lambda e: e.reciprocal(out=lbz[:], in_=lbz[:]), reads=["hlbz"], writes=["hlbz"])
        for l in range(2):
            P.op("dve", lambda e, l=l: e.tensor_tensor(out=lbs[:, l, :], in0=lbe[:, l, :], in1=lbz[:], op=ALU.mult),
                 reads=["hlbe", "hlbz"], writes=["hlbs%d" % l])
        if layer == 0:
            P.op("dve", lambda e: e.tensor_tensor(out=lb[:], in0=lbs[:, 0, :], in1=lbs[:, 0, :], op=ALU.subtract),
                 reads=["hlbs0"], writes=["hlb"])
        else:
            P.op("dve", lambda e: e.tensor_tensor(out=lb[:], in0=lbs[:, 0, :], in1=lbs[:, 1, :], op=ALU.add),
                 reads=["hlbs0", "hlbs1"], writes=["hlb"])
            P.op("dve", lambda e: e.tensor_tensor(out=lb[:], in0=lb[:], in1=lbs[:, 0, :], op=ALU.subtract),
                 reads=["hlb", "hlbs0"], writes=["hlb"])
        P.op("dve", lambda e: e.tensor_scalar(out=oml[:], in0=lb[:], scalar1=-1.0, scalar2=1.0, op0=ALU.mult,
                                               op1=ALU.add), reads=["hlb"], writes=["homl"])

        an = [0]
        for j in range(NB):
            xs = j % 2
            xk = "mx%d" % xs
            if j + 1 < NB:
                _xload(P, X, (j + 1) % 2, dr, j + 1, x_f32)
            for _ in range(nbg):
                if bgq:
                    o_, i_ = bgq.pop(0)
                    P.dma("pool", o_, i_)
            for c in range(8):
                b = bank()
                for kc in range(16):
                    P.op("pe", lambda e, kc=kc, c=c, b=b, xs=xs: e.matmul(
                        ps[b][0:64, :], lhsT=X[xs][:, kc, c * 64:(c + 1) * 64], rhs=W[2][:, kc, :],
                        start=(kc == 0), stop=(kc == 15)), reads=["hw2", xk], writes=["hps%d" % b])
                P.op("act", lambda e, c=c, b=b: e.activation(out=vh[:, c, :], in_=ps[b][0:64, :], func=AF.Copy),
                     reads=["hps%d" % b], writes=["hvh%d" % c])
            pend_x = []
            for h in range(NH):
                def proj(wi, h=h):
                    b = bank()
                    for kc in range(16):
                        P.op("pe", lambda e, kc=kc, b=b, xs=xs: e.matmul(
                            ps[b][:, :], lhsT=W[wi][:, kc, h * 128:(h + 1) * 128], rhs=X[xs][:, kc, :],
                            start=(kc == 0), stop=(kc == 15)), reads=["hw%d" % wi, xk], writes=["hps%d" % b])
                    return b
                bf = proj(1)
                kf = "hps%d" % bf
                P.op("act", lambda e, bf=bf: e.activation(out=tsig[:], in_=ps[bf][:, :], func=AF.Sigmoid),
                     reads=[kf], writes=["htsig"])
                P.op("act", lambda e, bf=bf: e.activation(out=tsgm[:], in_=ps[bf][:, :], func=AF.Sigmoid, scale=-1.0),
                     reads=[kf], writes=["htsgm"])
                P.op("dve", lambda e, h=h: e.tensor_scalar(out=tf[:], in0=tsig[:], scalar1=oml[:, h:h + 1],
                                                            scalar2=lb[:, h:h + 1], op0=ALU.mult, op1=ALU.add),
                     reads=["htsig", "homl", "hlb"], writes=["htf"])
                P.op("dve", lambda e: e.tensor_scalar(out=tf[:], in0=tf[:], scalar1=MIN_FORGET, scalar2=None,
                                                       op0=ALU.max), reads=["htf"], writes=["htf"])
                P.op("act", lambda e: e.activation(out=tf[:], in_=tf[:], func=AF.Ln), reads=["htf"], writes=["htf"])
                P.op("dve", lambda e: e.tensor_tensor_scan(out=tG[:], data0=rst[:], data1=tf[:], initial=0.0,
                                                            op0=ALU.mult, op1=ALU.add),
                     reads=["hrst", "htf"], writes=["htG"])
                Gv = tG[:].rearrange("p (c s) -> p c s", s=CH)
                P.op("dve", lambda e: e.tensor_tensor(out=tD[:].rearrange("p (c s) -> p c s", s=CH), in0=Gv,
                                                       in1=Gv[:, :, 31:32].to_broadcast([128, 8, CH]),
                                                       op=ALU.subtract), reads=["htG"], writes=["htD"])
                P.op("dve", lambda e: e.tensor_tensor(out=tD3[:].rearrange("p (c s) -> p c s", s=CH),
                                                       in0=Gv[:, :, 63:64].to_broadcast([128, 8, CH]), in1=Gv,
                                                       op=ALU.subtract), reads=["htG"], writes=["htD3"])
                P.op("act", lambda e: e.activation(out=tE1[:], in_=tD[:], func=AF.Exp), reads=["htD"], writes=["htE1"])
                P.op("act", lambda e: e.activation(out=tE2[:], in_=tD[:], func=AF.Exp, scale=-1.0),
                     reads=["htD"], writes=["htE2"])
                P.op("act", lambda e: e.activation(out=tEG[:], in_=tG[:], func=AF.Exp), reads=["htG"], writes=["htEG"])
                P.op("act", lambda e: e.activation(out=tE3[:], in_=tD3[:], func=AF.Exp),
                     reads=["htD3"], writes=["htE3"])
                P.op("pool", lambda e, h=h: e.tensor_copy(
                    out=EGl[:, h, :], in_=tEG[:].rearrange("p (c s) -> p c s", s=CH)[:, :, 63]),
                    reads=["htEG"], writes=["hEGl%d" % h])
                bq = proj(0)
                kq = "hps%d" % bq
                P.op("dve", lambda e, h=h, bq=bq: e.scalar_tensor_tensor(
                    out=qtil[:, h, :], in0=ps[bq][:, :], scalar=float(DH) ** -0.5, in1=tE1[:],
                    op0=ALU.mult, op1=ALU.mult), reads=[kq, "htE1"], writes=["hqtil%d" % h])
                P.op("dve", lambda e, h=h, bq=bq: e.scalar_tensor_tensor(
                    out=qg[:, h, :], in0=ps[bq][:, :], scalar=float(DH) ** -0.5, in1=tEG[:],
                    op0=ALU.mult, op1=ALU.mult), reads=[kq, "htEG"], writes=["hqg%d" % h])
                P.op("dve", lambda e, h=h: e.scalar_tensor_tensor(
                    out=ktil[:, h, :], in0=tsgm[:], scalar=oml[:, h:h + 1], in1=tE2[:],
                    op0=ALU.mult, op1=ALU.mult), reads=["htsgm", "homl", "htE2"], writes=["hktil%d" % h])
                P.op("dve", lambda e, h=h: e.scalar_tensor_tensor(
                    out=kdT[h % 2][:], in0=tsgm[:], scalar=oml[:, h:h + 1], in1=tE3[:],
                    op0=ALU.mult, op1=ALU.mult), reads=["htsgm", "homl", "htE3"], writes=["hkdT%d" % (h % 2)])
                bg = proj(3)
                P.op("act", lambda e, h=h, bg=bg: e.activation(out=sg[:, h, :], in_=ps[bg][:, :], func=AF.Silu),
                     reads=["hps%d" % bg], writes=["hsg%d" % h])

                def xpose(h=h):
                    kd = kdT[h % 2]
                    kk = "hkdT%d" % (h % 2)
                    for c in range(8):
                        P.op("pe", lambda e, c=c, kd=kd: e.transpose(out=psb[0:64, c * 128:(c + 1) * 128],
                                                                     in_=kd[:, c * 64:(c + 1) * 64],
                                                                     identity=identb[:]),
                             reads=[kk, "hidentb"], writes=["hpsb"])
                    P.op("act", lambda e, h=h: e.activation(out=kdec[:, h, :, :].rearrange("p c d -> p (c d)"),
                                                             in_=psb[0:64, :], func=AF.Copy),
                         reads=["hpsb"], writes=["hkdec%d" % h])
                if pend_x:
                    pend_x.pop(0)()
                pend_x.append(xpose)
            while pend_x:
                pend_x.pop(0)()
            for c in range(8):
                aks = []
                for h in range(NH):
                    b = bank()
                    P.op("pe", lambda e, c=c, h=h, b=b: e.matmul(
                        ps[b][0:64, 0:64], lhsT=ktil[:, h, c * 64:(c + 1) * 64], rhs=qtil[:, h, c * 64:(c + 1) * 64],
                        start=True, stop=True), reads=["hktil%d" % h, "hqtil%d" % h], writes=["hps%d" % b])
                    ai = an[0] % 4
                    an[0] += 1
                    ak = "hATs%d" % ai
                    P.op("dve", lambda e, b=b, ai=ai: e.copy_predicated(out=ATs[ai][:], mask=tri[:],
                                                                        data=ps[b][0:64, 0:64]),
                         reads=["hps%d" % b, "htri", ak], writes=[ak])
                    aks.append((ai, ak))
                for h in range(NH):
                    ai, ak = aks[h]
                    b2 = bank()
                    P.op("pe", lambda e, c=c, h=h, b2=b2, ai=ai: e.matmul(
                        ps[b2][:, 0:64], lhsT=vh[:, c, h * 128:(h + 1) * 128], rhs=ATs[ai][:],
                        start=True, stop=False), reads=["hvh%d" % c, ak], writes=["hps%d" % b2])
                    P.op("pe", lambda e, c=c, h=h, b2=b2: e.matmul(
                        ps[b2][:, 0:64], lhsT=stb[:, h, :], rhs=qg[:, h, c * 64:(c + 1) * 64],
                        start=False, stop=True), reads=["hstb%d" % h, "hqg%d" % h], writes=["hps%d" % b2])
                    P.op("act", lambda e, c=c, h=h, b2=b2: e.activation(out=oT[:, h, c * 64:(c + 1) * 64],
                                                                        in_=ps[b2][:, 0:64], func=AF.Copy),
                         reads=["hps%d" % b2], writes=["hoT%d_%d" % (h, c)])
                    b3 = bank()
                    P.op("pe", lambda e, c=c, h=h, b3=b3: e.matmul(
                        ps[b3][:, 0:128], lhsT=kdec[:, h, c, :], rhs=vh[:, c, h * 128:(h + 1) * 128],
                        start=True, stop=True), reads=["hkdec%d" % h, "hvh%d" % c], writes=["hps%d" % b3])
                    P.op("dve", lambda e, c=c, h=h, b3=b3: e.scalar_tensor_tensor(
                        out=stf[:, h, :], in0=stf[:, h, :], scalar=EGl[:, h, c:c + 1], in1=ps[b3][:, 0:128],
                        op0=ALU.mult, op1=ALU.add),
                        reads=["hstf%d" % h, "hEGl%d" % h, "hps%d" % b3], writes=["hstf%d" % h])
                    P.op("pool", lambda e, h=h: e.tensor_copy(out=stb[:, h, :], in_=stf[:, h, :]),
                         reads=["hstf%d" % h], writes=["hstb%d" % h])
            for h in range(NH):
                P.op("act", lambda e, h=h: e.activation(out=osq[:], in_=oT[:, h, :], func=AF.Square),
                     reads=["hoT%d_%d" % (h, c) for c in range(8)], writes=["hosq"])
                b = bank()
                P.op("pe", lambda e, b=b: e.matmul(ps[b][:, :], lhsT=ones[:], rhs=osq[:], start=True, stop=True),
                     reads=["hones", "hosq"], writes=["hps%d" % b])
                P.op("dve", lambda e, b=b: e.tensor_scalar(out=rs[:], in0=ps[b][:, :], scalar1=1.0 / DH,
                                                            scalar2=RMS_EPS, op0=ALU.mult, op1=ALU.add),
                     reads=["hps%d" % b], writes=["hrs"])
                P.op("act", lambda e: e.activation(out=rs[:], in_=rs[:], func=AF.Sqrt), reads=["hrs"], writes=["hrs"])
                P.op("dve", lambda e: e.reciprocal(out=rs[:], in_=rs[:]), reads=["hrs"], writes=["hrs"])
                P.op("dve", lambda e, h=h: e.scalar_tensor_tensor(
                    out=hn[:], in0=oT[:, h, :], scalar=nw[:, h:h + 1], in1=rs[:], op0=ALU.mult, op1=ALU.mult),
                    reads=["hoT%d_%d" % (h, c) for c in range(8)] + ["hnw", "hrs"], writes=["hhn"])
                oi = (j * NH + h) % 2
                P.op("dve", lambda e, h=h, oi=oi: e.tensor_tensor(out=ost[oi][:], in0=hn[:], in1=sg[:, h, :],
                                                                  op=ALU.mult),
                     reads=["hhn", "hsg%d" % h], writes=["host%d" % oi])
                ok = "mixh%d_%d" % (h, j)
                P.dma("sp", dr["mix"][512 + h * 128:512 + (h + 1) * 128, j * 512:(j + 1) * 512], ost[oi][:],
                      reads=["host%d" % oi], writes=[ok])
                final_keys.append(ok)
        P.flush(final=final)


D = 2048
DFF = 5632
NMEM = 256
ALPHA = 4.0 ** 0.25
LN_EPS = 1e-5
XH = 4
XDH = 512


class CG:
    def __init__(self, name, n, A, B, G):
        self.name, self.n, self.A, self.B, self.G = name, n, A, B, G

    def a(self, kc):
        return self.A[:, kc, :self.n]

    def b(self, kc):
        return self.B[:, kc, :self.n]

    def g(self, kc):
        return self.G[:, kc, :self.n]

    def ka(self, kc):
        return "%s.A%d" % (self.name, kc)

    def kb(self, kc):
        return "%s.B%d" % (self.name, kc)

    def kg(self, kc):
        return "%s.G%d" % (self.name, kc)


class TCtx:
    def __init__(self, nc, P, st, NT):
        self.nc, self.P, self.NT = nc, P, NT
        sfx = uid()
        sb = lambda name, shape, dt: st.enter_context(nc.sbuf_tensor(name + sfx, shape, dt))
        self.W = [sb("tw%d" % i, [128, 16, 512], BF16) for i in range(3)]
        self.wn = 0
        self.A = sb("tA", [128, 16, 512], F32)
        self.B = sb("tB", [128, 16, 512], BF16)
        self.G = sb("tG", [128, 44, 512], BF16)
        self.Ah = sb("tAh", [128, 16, 2], F32)
        self.Bh = sb("tBh", [128, 16, 2], BF16)
        self.Gh = sb("tGh", [128, 32, 2], BF16)
        self.memT = sb("tmemT", [128, 16, NMEM], BF16)
        self.KT = sb("tKT", [128, 16, NMEM], BF16)
        self.V = sb("tV", [128, 2, D], BF16)
        self.PT = [sb("tPT%d" % i, [128, 2, 512], BF16) for i in range(2)]
        self.rl = [sb("trl%d" % i, [128, 512], F32) for i in range(2)]
        self.mean = sb("tmean", [128, 512], F32)
        self.msq = sb("tmsq", [128, 512], F32)
        self.rstd = sb("trstd", [128, 512], F32)
        self.lt = [sb("tlt%d" % i, [128, 512], F32) for i in range(4)]
        self.hb = [sb("thb%d" % i, [128, 514], F32) for i in range(4)]
        self.ct = [sb("tct%d" % i, [128, 512], F32) for i in range(4)]
        self.hal = sb("thal", [128, 88, 2], F32)
        self.ones = sb("tones", [128, 128], BF16)
        self.flag = sb("tflag", [128, 1], F32)
        self.epsb = sb("tepsb", [128, 1], F32)
        self.fmaskf = sb("tfmaskf", [128, 512], F32)
        self.fmask = self.fmaskf[:].bitcast(I32)
        self.nflag = sb("tnflag", [128, 1], F32)
        self.lnp = sb("tlnp", [128, 6, 16], F32)
        self.cw = sb("tcw", [128, 88, 3], F32)
        self.cb = sb("tcb", [128, 88], F32)
        self.ps = [st.enter_context(nc.psum_tensor("tps%d" % i + sfx, [128, 512], F32)) for i in range(8)]
        self.pn = 0
        self.ltn = 0
        self.hbn = 0

    def bank(self):
        i = self.pn % 4
        self.pn += 1
        return i

    def wload(self, view, nk):
        s = self.wn % 3
        self.wn += 1
        q = "sp" if view.dtype == BF16 else "pool"
        self.P.dma(q, self.W[s][:, :nk, :], view, writes=["tw%d" % s])
        return s


def wview(dr, name, r0, nk, c0):
    if name + "_b" in dr:
        t = (c0 // 512) * 4 + r0 // (11 * 128) if name == "down" else c0 // 512
        return dr[name + "_b"][t].rearrange("p (kc c) -> p kc c", c=512)
    return dr[name][r0:r0 + nk * 128, c0:c0 + 512].rearrange("(kc p) c -> p kc c", p=128)


def conv_tasks(dr):
    tasks = []
    for name, ntile in (("xk", 4), ("xv", 4), ("w_out", 4), ("xq", 4), ("xo", 4), ("up", 22)):
        for t in range(ntile):
            tasks.append((dr[name + "_b"][t].rearrange("p (kc c) -> p kc c", c=512),
                          dr[name][:, t * 512:(t + 1) * 512].rearrange("(kc p) c -> p kc c", p=128)))
    for ct in range(4):
        for kg in range(4):
            tasks.append((dr["down_b"][ct * 4 + kg].rearrange("p (kc c) -> p kc c", c=512),
                          dr["down"][kg * 1408:(kg + 1) * 1408, ct * 512:(ct + 1) * 512].rearrange(
                              "(kc p) c -> p kc c", p=128)))
    return tasks


def dense(T, dr, name, nk, ncols, cgs, src, ksrc, evac, col0=0):
    P = T.P
    for ct in range(ncols // 512):
        c0 = col0 + ct * 512
        s = T.wload(wview(dr, name, 0, nk, c0), nk)
        for cg in cgs:
            for jc in range(4):
                b = T.bank()
                for kc in range(nk):
                    P.op("pe", lambda e, s=s, kc=kc, jc=jc, b=b, cg=cg: e.matmul(
                        T.ps[b][:, :cg.n], lhsT=T.W[s][:, kc, jc * 128:(jc + 1) * 128], rhs=src(cg, kc),
                        start=(kc == 0), stop=(kc == nk - 1)),
                        reads=["tw%d" % s, ksrc(cg, kc)], writes=["tps%d" % b])
                evac(cg, ct * 4 + jc, b)


def ln_pre(T, cg, kc, early):
    P = T.P
    dst, dk = (cg.g(kc), cg.kg(kc)) if early else (cg.b(kc), cg.kb(kc))
    P.op("pool" if kc % 2 else "dve", lambda e: e.tensor_copy(out=dst, in_=cg.a(kc)),
         reads=[cg.ka(kc)], writes=[dk])
    P.op("act", lambda e: e.activation(out=cg.g(16 + kc), in_=cg.a(kc), func=AF.Square),
         reads=[cg.ka(kc)], writes=[cg.kg(16 + kc)])


def layernorm(T, cg, li, early=False):
    P = T.P
    n = cg.n
    gcol, bcol = 2 * li, 2 * li + 1
    if not early:
        for kc in range(16):
            ln_pre(T, cg, kc, False)
    b1, b2 = 4, 5
    for kc in range(16):
        P.op("pe", lambda e, kc=kc: e.matmul(T.ps[b1][:, :n], lhsT=T.ones[:],
                                              rhs=(cg.g(kc) if early else cg.b(kc)),
                                              start=(kc == 0), stop=(kc == 15)),
             reads=["tones", cg.kg(kc) if early else cg.kb(kc)], writes=["tps%d" % b1])
    for kc in range(16):
        P.op("pe", lambda e, kc=kc: e.matmul(T.ps[b2][:, :n], lhsT=T.ones[:], rhs=cg.g(16 + kc),
                                              start=(kc == 0), stop=(kc == 15)),
             reads=["tones", cg.kg(16 + kc)], writes=["tps%d" % b2])
    P.op("act", lambda e: e.activation(out=T.mean[:, :n], in_=T.ps[b1][:, :n], func=AF.Copy, scale=1.0 / D),
         reads=["tps%d" % b1], writes=["tmean"])
    P.op("act", lambda e: e.activation(out=T.msq[:, :n], in_=T.ps[b1][:, :n], func=AF.Square, scale=1.0 / D),
         reads=["tps%d" % b1], writes=["tmsq"])
    P.op("dve", lambda e: e.scalar_tensor_tensor(out=T.rstd[:, :n], in0=T.ps[b2][:, :n], scalar=1.0 / D,
                                                  in1=T.msq[:, :n], op0=ALU.mult, op1=ALU.subtract),
         reads=["tps%d" % b2, "tmsq"], writes=["trstd"])
    P.op("act", lambda e: e.activation(out=T.rstd[:, :n], in_=T.rstd[:, :n], func=AF.Sqrt, bias=T.epsb[:, 0:1]),
         reads=["trstd", "tepsb"], writes=["trstd"])
    P.op("dve", lambda e: e.reciprocal(out=T.rstd[:, :n], in_=T.rstd[:, :n]),
         reads=["trstd"], writes=["trstd"])
    for kc in range(16):
        li_ = T.ltn % 4
        T.ltn += 1
        lt = T.lt[li_]
        lk = "tlt%d" % li_
        P.op("dve", lambda e, kc=kc, lt=lt: e.tensor_tensor(out=lt[:, :n], in0=cg.a(kc), in1=T.mean[:, :n],
                                                            op=ALU.subtract),
             reads=[cg.ka(kc), "tmean"], writes=[lk])
        P.op("pool" if kc % 2 else "dve",
             lambda e, lt=lt: e.tensor_tensor(out=lt[:, :n], in0=lt[:, :n], in1=T.rstd[:, :n], op=ALU.mult),
             reads=[lk, "trstd"], writes=[lk])
        P.op("act", lambda e, kc=kc, lt=lt: e.activation(out=cg.a(kc), in_=lt[:, :n], func=AF.Identity,
                                                          scale=T.lnp[:, gcol, kc:kc + 1],
                                                          bias=T.lnp[:, bcol, kc:kc + 1]),
             reads=[lk, "tlnp"], writes=[cg.ka(kc)])
        P.op("act", lambda e, kc=kc, lt=lt: e.activation(out=cg.b(kc), in_=lt[:, :n], func=AF.Identity,
                                                          scale=T.lnp[:, gcol, kc:kc + 1],
                                                          bias=T.lnp[:, bcol, kc:kc + 1]),
             reads=[lk, "tlnp"], writes=[cg.kb(kc)])


def emit_T(nc, P, NT, dr, final_keys, final=False):
    with ExitStack() as st:
        _emit_T(nc, P, st, NT, dr, final_keys)
        P.flush(final=final)


def _emit_T(nc, P, st, NT, dr, final_keys):
    T = TCtx(nc, P, st, NT)
    NB = NT // 512
    P.op("dve", lambda e: e.memset(T.ones[:], 1.0), writes=["tones"])
    P.op("dve", lambda e: e.memset(T.epsb[:], LN_EPS), writes=["tepsb"])
    P.dma("sp", T.flag[:], dr["flag"], writes=["tflag"])
    P.op("dve", lambda e: e.tensor_scalar(out=T.nflag[:], in0=T.flag[:], scalar1=-1.0, scalar2=1.0,
                                           op0=ALU.mult, op1=ALU.add), reads=["tflag"], writes=["tflag"])
    P.op("dve", lambda e: e.memset(T.fmaskf[:], 1.0), writes=["tflag"])
    P.op("dve", lambda e: e.tensor_scalar(out=T.fmaskf[:], in0=T.fmaskf[:], scalar1=T.flag[:, 0:1], scalar2=None,
                                           op0=ALU.mult), reads=["tflag"], writes=["tflag"])
    P.dma("sp", T.lnp[:], dr["lnp"], writes=["tlnp"])
    P.dma("sp", T.cw[:], dr["cw"], writes=["tcw"])
    P.dma("sp", T.cb[:], dr["cb"], writes=["tcb"])
    P.dma("pool", T.memT[:], dr["memT"].rearrange("(kc p) m -> p kc m", p=128), writes=["tmemT"])

    memcg = CG("mem", NMEM, None, None, None)

    def ev_k(cg, j, b):
        P.op("act", lambda e: e.activation(out=T.KT[:, j, :], in_=T.ps[b][:, :NMEM], func=AF.Copy),
             reads=["tps%d" % b], writes=["tKT%d" % j])

    dense(T, dr, "xk", 16, D, [memcg], lambda cg, kc: T.memT[:, kc, :], lambda cg, kc: "tmemT", ev_k)
    for ct in range(4):
        s = T.wload(wview(dr, "xv", 0, 16, ct * 512), 16)
        for mc in range(2):
            b = T.bank()
            for kc in range(16):
                P.op("pe", lambda e, s=s, kc=kc, mc=mc, b=b: e.matmul(
                    T.ps[b][:, :], lhsT=T.memT[:, kc, mc * 128:(mc + 1) * 128], rhs=T.W[s][:, kc, :],
                    start=(kc == 0), stop=(kc == 15)),
                    reads=["tw%d" % s, "tmemT"], writes=["tps%d" % b])
            P.op("act", lambda e, mc=mc, ct=ct, b=b: e.activation(out=T.V[:, mc, ct * 512:(ct + 1) * 512],
                                                                in_=T.ps[b][:, :], func=AF.Copy),
                 reads=["tps%d" % b], writes=["tV%d_%d" % (mc, ct)])

    main = CG("m", 512, T.A, T.B, T.G)
    halo = CG("h", 2, T.Ah, T.Bh, T.Gh)

    for blk in range(NB):
        t0 = blk * 512
        cgs = [main, halo] if blk == 0 else [main]
        P.dma("sp", T.A[:], dr["xres"][:, t0:t0 + 512].rearrange("(kc p) t -> p kc t", p=128),
              writes=[main.ka(kc) for kc in range(16)])
        if "mixG" in dr:
            P.dma("sp", T.B[:], dr["mixG"][:, t0:t0 + 512].rearrange("(kc p) t -> p kc t", p=128),
                  reads=["mixG"], writes=[main.kb(kc) for kc in range(16)])
            P.dma("sp", T.G[:, 0:16, :], dr["mixG"][:, NT + t0:NT + t0 + 512].rearrange("(kc p) t -> p kc t", p=128),
                  reads=["mixG"], writes=[main.kg(kc) for kc in range(16)])
            for kc in range(16):
                P.op("dve", lambda e, kc=kc: e.copy_predicated(out=main.b(kc), mask=T.fmask[:], data=main.g(kc)),
                     reads=[main.kg(kc), "tflag", main.kb(kc)], writes=[main.kb(kc)])
        else:
            P.dma("sp", T.B[:], dr["mix"][:, t0:t0 + 512].rearrange("(kc p) t -> p kc t", p=128),
                  writes=[main.kb(kc) for kc in range(16)])
        if blk == 0:
            P.dma("sp", T.Ah[:], dr["xres_h"].rearrange("(kc p) t -> p kc t", p=128),
                  writes=[halo.ka(kc) for kc in range(16)])
            P.dma("sp", T.Bh[:], dr["mix_h"].rearrange("(kc p) t -> p kc t", p=128),
                  reads=["mixG"], writes=[halo.kb(kc) for kc in range(16)])

        def ev_res(cg, j, b):
            P.op("dve", lambda e: e.scalar_tensor_tensor(out=cg.a(j), in0=cg.a(j), scalar=ALPHA,
                                                          in1=T.ps[b][:, :cg.n], op0=ALU.mult, op1=ALU.add),
                 reads=[cg.ka(j), "tps%d" % b], writes=[cg.ka(j)])

        srcB = lambda cg, kc: cg.b(kc)
        ksrcB = lambda cg, kc: cg.kb(kc)
        def ev_res_ln(cg, j, b):
            ev_res(cg, j, b)
            ln_pre(T, cg, j, True)

        dense(T, dr, "w_out", 16, D, cgs, srcB, ksrcB, ev_res_ln)
        for cg in cgs:
            layernorm(T, cg, 0, early=True)

        def ev_q(cg, j, b):
            P.op("act", lambda e: e.activation(out=cg.g(j), in_=T.ps[b][:, :cg.n], func=AF.Copy,
                                                scale=float(XDH) ** -0.5),
                 reads=["tps%d" % b], writes=[cg.kg(j)])

        dense(T, dr, "xq", 16, D, cgs, srcB, ksrcB, ev_q)
        def xattn(cg):
            n = cg.n

            def scores(h):
                hp = h % 2
                for mc in range(2):
                    b = T.bank()
                    for dc in range(4):
                        j = h * 4 + dc
                        P.op("pe", lambda e, j=j, mc=mc, b=b, dc=dc: e.matmul(
                            T.ps[b][:, :n], lhsT=T.KT[:, j, mc * 128:(mc + 1) * 128], rhs=cg.g(j),
                            start=(dc == 0), stop=(dc == 3)),
                            reads=["tKT%d" % j, cg.kg(j)], writes=["tps%d" % b])
                    P.op("act", lambda e, mc=mc, b=b, hp=hp: e.activation(out=T.PT[hp][:, mc, :n],
                                                                          in_=T.ps[b][:, :n], func=AF.Exp),
                         reads=["tps%d" % b], writes=["tPT%d_%d" % (hp, mc)])

            def pv(h):
                hp = h % 2
                bl = 6 + hp
                for mc in range(2):
                    P.op("pe", lambda e, mc=mc, hp=hp, bl=bl: e.matmul(T.ps[bl][:, :n], lhsT=T.ones[:],
                                                                      rhs=T.PT[hp][:, mc, :n],
                                                                      start=(mc == 0), stop=(mc == 1)),
                         reads=["tones", "tPT%d_%d" % (hp, mc)], writes=["tps%d" % bl])
                P.op("dve", lambda e, hp=hp, bl=bl: e.reciprocal(out=T.rl[hp][:, :n], in_=T.ps[bl][:, :n]),
                     reads=["tps%d" % bl], writes=["trl%d" % hp])
                for ec in range(4):
                    j = h * 4 + ec
                    b = T.bank()
                    for mc in range(2):
                        P.op("pe", lambda e, mc=mc, j=j, b=b, hp=hp: e.matmul(
                            T.ps[b][:, :n], lhsT=T.V[:, mc, j * 128:(j + 1) * 128], rhs=T.PT[hp][:, mc, :n],
                            start=(mc == 0), stop=(mc == 1)),
                            reads=["tV%d_%d" % (mc, j // 4), "tPT%d_%d" % (hp, mc)], writes=["tps%d" % b])
                    P.op("dve", lambda e, j=j, b=b, hp=hp: e.tensor_tensor(out=cg.b(j), in0=T.ps[b][:, :n],
                                                                           in1=T.rl[hp][:, :n], op=ALU.mult),
                         reads=["tps%d" % b, "trl%d" % hp], writes=[cg.kb(j)])

            scores(0)
            for h in range(XH):
                if h + 1 < XH:
                    scores(h + 1)
                pv(h)

        for cg_ in cgs:
            xattn(cg_)
        dense(T, dr, "xo", 16, D, cgs, srcB, ksrcB, ev_res_ln)
        for cg in cgs:
            layernorm(T, cg, 1, early=True)

        def conv(cg, fc, b):
            n = cg.n
            if cg is halo:
                P.op("dve", lambda e: e.tensor_scalar(out=T.hal[:, fc, :], in0=T.ps[b][:, :2],
                                                       scalar1=T.flag[:, 0:1], scalar2=None, op0=ALU.mult),
                     reads=["tps%d" % b, "tflag"], writes=["thal%d" % fc])
                return None
            hi = T.hbn % 4
            ci = T.hbn % 4
            T.hbn += 1
            hb, hk = T.hb[hi], "thb%d" % hi
            ctb, ck = T.ct[ci], "tct%d" % ci
            P.op("pool", lambda e: e.tensor_copy(out=hb[:, 0:2], in_=T.hal[:, fc, :]),
                 reads=["thal%d" % fc], writes=[hk])
            P.op("act", lambda e: e.activation(out=hb[:, 2:2 + n], in_=T.ps[b][:, :n], func=AF.Copy),
                 reads=["tps%d" % b, hk], writes=[hk])
            P.op("dve", lambda e: e.tensor_scalar(out=ctb[:, :n], in0=hb[:, 2:2 + n], scalar1=T.cw[:, fc, 2:3],
                                                   scalar2=T.cb[:, fc:fc + 1], op0=ALU.mult, op1=ALU.add),
                 reads=[hk, "tcw", "tcb"], writes=[ck])
            P.op("dve", lambda e: e.scalar_tensor_tensor(out=ctb[:, :n], in0=hb[:, 1:1 + n],
                                                          scalar=T.cw[:, fc, 1:2], in1=ctb[:, :n],
                                                          op0=ALU.mult, op1=ALU.add),
                 reads=[hk, "tcw", ck], writes=[ck])
            P.op("dve", lambda e: e.scalar_tensor_tensor(out=ctb[:, :n], in0=hb[:, 0:n],
                                                          scalar=T.cw[:, fc, 0:1], in1=ctb[:, :n],
                                                          op0=ALU.mult, op1=ALU.add),
                 reads=[hk, "tcw", ck], writes=[ck])
            P.op("dve", lambda e: e.tensor_copy(out=T.hal[:, fc, :], in_=hb[:, n:n + 2]),
                 reads=[hk], writes=["thal%d" % fc])
            return ctb, ck

        silu_q = []

        def silu_flush(keep):
            while len(silu_q) > keep:
                cg, j, ctb, ck = silu_q.pop(0)
                P.op("act", lambda e, cg=cg, j=j, ctb=ctb: e.activation(out=cg.g(j), in_=ctb[:, :cg.n], func=AF.Silu),
                     reads=[ck], writes=[cg.kg(j)])

        def ev_a(cg, j, b):
            r = conv(cg, j, b)
            if r is None:
                return
            ctb, ck = r
            silu_q.append((cg, j, ctb, ck))
            silu_flush(2)

        def ev_b(cg, j, b):
            r = conv(cg, 44 + j, b)
            if r is None:
                return
            ctb, ck = r
            P.op("dve", lambda e: e.tensor_tensor(out=cg.g(j), in0=cg.g(j), in1=ctb[:, :cg.n], op=ALU.mult),
                 reads=[ck, cg.kg(j)], writes=[cg.kg(j)])

        ucgs = [halo, main] if blk == 0 else [main]
        dense(T, dr, "up", 16, DFF, ucgs, srcB, ksrcB, ev_a, col0=0)
        silu_flush(0)
        dense(T, dr, "up", 16, DFF, ucgs, srcB, ksrcB, ev_b, col0=DFF)

        cg = main
        for ct in range(4):
            for kg in range(4):
                s = T.wload(wview(dr, "down", kg * 1408, 11, ct * 512), 11)
                for jc in range(4):
                    b = 4 + jc
                    for kc in range(11):
                        kk = kg * 11 + kc
                        P.op("pe", lambda e, s=s, kc=kc, jc=jc, b=b, kk=kk: e.matmul(
                            T.ps[b][:, :], lhsT=T.W[s][:, kc, jc * 128:(jc + 1) * 128], rhs=cg.g(kk),
                            start=(kk == 0), stop=(kk == 43)),
                            reads=["tw%d" % s, cg.kg(kk)], writes=["tps%d" % b])
            for jc in range(4):
                ev_res(cg, ct * 4 + jc, 4 + jc)
        layernorm(T, cg, 2)
        P.dma("sp", dr["xout"][:, t0:t0 + 512].rearrange("(kc p) t -> p kc t", p=128), T.A[:],
              reads=[main.ka(kc) for kc in range(16)], writes=["xout%d" % blk])
        final_keys.append("xout%d" % blk)
        if "xoutb" in dr:
            P.dma("sp", dr["xoutb"][blk].rearrange("(kc p) t -> p kc t", p=128), T.B[:],
                  reads=[main.kb(kc) for kc in range(16)], writes=["xoutb%d" % blk])
            final_keys.append("xoutb%d" % blk)
            P.coll("AllGather", [dr["xoutb"][blk]], [dr["xG_out"][blk]], dr["groups"],
                   reads=["xoutb%d" % blk], writes=["xG%d" % blk])
        if "xouth" in dr and blk == NB - 1:
            P.dma("sp", dr["xouth"].rearrange("(kc p) t -> p kc t", p=128), T.A[:, :, 510:512],
                  reads=[main.ka(kc) for kc in range(16)], writes=["xouth"])
    return T


import ml_dtypes
from concourse.bass_utils import run_bass_kernel_spmd

SEQ = 4096
NB_ = 4
HALF = SEQ // 2
GROUPS = [[0, 1], [2, 3], [4, 5], [6, 7]]


def m_consts():
    sel = np.zeros((4, 4, 128), np.float32)
    for h in range(4):
        sel[h, h, :] = 1.0
    id4 = np.eye(4, dtype=np.float32)
    s = np.arange(128)[:, None]
    t = np.arange(128)[None, :]
    nmask = np.where(s <= t, 0.0, -1e30).astype(np.float32)
    rst = np.ones((128, 512), np.float32)
    rst[:, ::64] = 0.0
    s = np.arange(64)[:, None]
    t = np.arange(64)[None, :]
    tri = (s <= t).astype(np.int32)
    identb = np.eye(128, dtype=np.float32).astype(ml_dtypes.bfloat16)
    return dict(sel=sel, id4=id4, nmask=nmask, rst=rst, tri=tri, identb=identb)


def m_weights(inputs, l, g):
    w_in_l = np.asarray(inputs["w_in"][l], np.float32)
    hs = slice(4 * g * 128, (4 * g + 4) * 128)
    fq = w_in_l[:, 0:1024][:, hs]
    fk = w_in_l[:, 1024:2048][:, hs]
    fv = w_in_l[:, 2048:3072][:, hs]
    ff = w_in_l[:, 3072:3080][:, 4 * g:4 * g + 4]
    o = 3080
    hq = w_in_l[:, o:o + 1024][:, hs]
    hf = w_in_l[:, o + 1024:o + 2048][:, hs]
    hi = w_in_l[:, o + 2048:o + 3072][:, hs]
    hg = w_in_l[:, o + 3072:o + 4096][:, hs]
    c = np.ascontiguousarray
    return {
        "wqkv%d" % l: c(np.concatenate([fq, fk, fv], axis=1)),
        "wff%d" % l: c(ff),
        "fbias%d" % l: c(np.asarray(inputs["fox_f_bias"][l], np.float32)[4 * g:4 * g + 4].reshape(4, 1)),
        "wh%d" % l: c(np.concatenate([hq, hf, hi, hg], axis=1)),
        "normw%d" % l: c(np.asarray(inputs["hgrn_norm_w"][l], np.float32)[hs].reshape(4, 128).T),
        "lbl": c(np.asarray(inputs["hgrn_lb_logits"], np.float32)[:, hs].reshape(2, 4, 128).transpose(2, 0, 1)),
    }


def t_weights(inputs, l):
    f = lambda a: np.ascontiguousarray(np.asarray(a, dtype=np.float32))
    pk = lambda v: f(np.asarray(v).reshape(-1, 128).T)
    lnp = np.stack([pk(inputs[k][l]) for k in ("ln1_g", "ln1_b", "ln2_g", "ln2_b", "ln3_g", "ln3_b")], axis=1)
    cw = np.asarray(inputs["conv_w"][l]).reshape(3, 88, 128).transpose(2, 1, 0)
    wo = np.asarray(inputs["w_out"][l], np.float32).reshape(16, 128, 2048)
    perm = []
    for c_ in range(4):
        for r in range(2):
            for u in range(2):
                lc = c_ * 2 + u
                perm.append((lc // 4) * 8 + 4 * r + (lc % 4))
    wo = wo[perm].reshape(2048, 2048)
    d = dict(lnp=f(lnp), cw=f(cw), cb=pk(inputs["conv_b"][l]),
             w_out=f(wo), xq=f(inputs["xq_w"][l]), xk=f(inputs["xk_w"][l]),
             xv=f(inputs["xv_w"][l]), xo=f(inputs["xo_w"][l]), up=f(inputs["ffn_up"][l]),
             down=f(inputs["ffn_down"][l]))
    return {"%s%d" % (k, l): v for k, v in d.items()}


def build_fused(S=SEQ, GROUPS=GROUPS):
    nc = bass.Bass("TRN2", target_bir_lowering=False)
    st0 = ExitStack()
    with st0:
        def din(name, shape, dt=F32):
            return nc.dram_tensor(name, shape, dt, kind="ExternalInput").ap()

        def dint(name, shape, dt):
            return nc.dram_tensor(name, shape, dt).ap()
        NT = S // 2
        cst = dict(sel=din("sel", [4, 4, 128]), id4=din("id4", [4, 4]), nmask=din("nmask", [128, 128]),
                   rst=din("rst", [128, 512]), tri=din("tri", [64, 64], I32),
                   identb=din("identb", [128, 128], BF16), lbl=din("lbl", [128, 2, 4]))
        xT0 = din("xT0", [2048, S])
        xres0 = din("xres0", [2048, NT])
        xres_h0 = din("xres_h0", [2048, 2])
        flag = din("flag", [128, 1])
        memT = din("memT", [2048, 256])
        mw, tw = [], []
        for l in range(2):
            mw.append(dict(wqkv=din("wqkv%d" % l, [2048, 1536]), wff=din("wff%d" % l, [2048, 4]),
                           fbias=din("fbias%d" % l, [4, 1]), wh=din("wh%d" % l, [2048, 2048]),
                           normw=din("normw%d" % l, [128, 4])))
            tw.append(dict(lnp=din("lnp%d" % l, [128, 6, 16]), cw=din("cw%d" % l, [128, 88, 3]),
                           cb=din("cb%d" % l, [128, 88]), w_out=din("w_out%d" % l, [2048, 2048]),
                           xq=din("xq%d" % l, [2048, 2048]), xk=din("xk%d" % l, [2048, 2048]),
                           xv=din("xv%d" % l, [2048, 2048]), xo=din("xo%d" % l, [2048, 2048]),
                           up=din("up%d" % l, [2048, 11264]), down=din("down%d" % l, [5632, 2048])))
        for l in range(2):
            for nm, nt_, w_ in (("xk", 4, 8192), ("xv", 4, 8192), ("w_out", 4, 8192), ("xq", 4, 8192),
                                ("xo", 4, 8192), ("up", 22, 8192), ("down", 16, 5632)):
                tw[l][nm + "_b"] = dint("%s_b%d" % (nm, l), [nt_, 128, w_], BF16)
        xout = nc.dram_tensor("xout", [2048, NT], F32, kind="ExternalOutput").ap()
        mixA = [dint("mixA%d" % l, [1024, S], BF16) for l in range(2)]
        mixG = [dint("mixG%d" % l, [2048, S], BF16) for l in range(2)]
        xI = dint("xI", [2048, NT], F32)
        xB = dint("xB", [NT // 512, 2048, 512], BF16)
        xG = dint("xG", [NT // 512, 4096, 512], BF16)
        xH = dint("xH", [2048, 2], F32)
        xHG = dint("xHG", [4096, 2], F32)
        P = Prog(nc, st0)
        fk = []
        for l in range(2):
            dm = dict(cst)
            dm.update(mw[l])
            dm["mix"] = mixA[l]
            if l == 0:
                dm["xT"] = xT0
            else:
                P.coll("AllGather", [xH], [xHG], GROUPS, writes=["xHG"])
                dm["xG"] = xG
            bg = conv_tasks(tw[l])
            nblk = S // 512
            nbg = -(-len(bg) // (2 * nblk))
            emit_fox(nc, P, S, dm, fk, x_f32=(l == 0), bgq=bg, nbg=nbg)
            for c_ in range(2):
                P.coll("AllGather", [mixA[l][c_ * 256:(c_ + 1) * 256, :]], [mixG[l][c_ * 512:(c_ + 1) * 512, :]],
                       GROUPS, writes=["mixG"])
            emit_hgrn(nc, P, S, l, dm, fk, x_f32=(l == 0), bgq=bg, nbg=nbg)
            while bg:
                o_, i_ = bg.pop(0)
                P.dma("pool", o_, i_)
            for c_ in range(2, 4):
                P.coll("AllGather", [mixA[l][c_ * 256:(c_ + 1) * 256, :]], [mixG[l][c_ * 512:(c_ + 1) * 512, :]],
                       GROUPS, writes=["mixG"])
            dt = dict(tw[l])
            dt.update(flag=flag, memT=memT, mixG=mixG[l], mix_h=mixG[l][:, NT - 2:NT])
            if l == 0:
                dt.update(xres=xres0, xres_h=xres_h0, xout=xI, xoutb=xB, xouth=xH, xG_out=xG, groups=GROUPS)
            else:
                dt.update(xres=xI, xres_h=xHG[0:2048, :], xout=xout)
            emit_T(nc, P, NT, dt, fk, final=(l == 1))
    return nc


def kernel(**inputs):
    c = np.ascontiguousarray
    x = np.asarray(inputs["x"], np.float32)
    mem = np.asarray(inputs["mem"], np.float32)
    consts = m_consts()
    tw = {}
    for l in range(2):
        tw.update(t_weights(inputs, l))
    mwg = []
    for g in range(2):
        d = {}
        for l in range(2):
            d.update(m_weights(inputs, l, g))
        mwg.append(d)
    in_maps = []
    for b in range(NB_):
        xT = c(x[b].T)
        mT = c(mem[b].T)
        for g in range(2):
            d = dict(tw)
            d.update(mwg[g])
            d.update(consts)
            lo = g * HALF
            d["xT0"] = xT
            d["xres0"] = c(xT[:, lo:lo + HALF])
            d["xres_h0"] = c(xT[:, lo - 2:lo]) if g == 1 else np.zeros((2048, 2), np.float32)
            d["flag"] = np.full((128, 1), float(g), np.float32)
            d["memT"] = mT
            in_maps.append(d)
    nc = build_fused()
    res = run_bass_kernel_spmd(nc, in_maps, core_ids=list(range(8))).results
    out = np.empty((NB_, SEQ, 2048), np.float32)
    for b in range(NB_):
        for g in range(2):
            out[b, g * HALF:(g + 1) * HALF, :] = np.asarray(res[2 * b + g]["xout"]).T
    return out
```
